# Optimizing a Trainium2 kernel written in Bass

```python
import math
import jax, jax.numpy as jnp
from jax import lax
import numpy as np


D_MODEL = 1024
BATCH = 16
SEQ = 2048
DEPTH = 1
DEC_BATCH = 4
DEC_SEQ = 4096
PAST_LEN = 128

ATT_HEAD_DIM = 64
ATT_HEADS_PER_GROUP = 8
DILATED_GROUPS = ((128, 1), (512, 4), (2048, 16))
ATT_HEADS = ATT_HEADS_PER_GROUP * len(DILATED_GROUPS)
ATT_WIDTH = ATT_HEADS * ATT_HEAD_DIM
ATT_OUT_WIDTH = ATT_HEADS_PER_GROUP * ATT_HEAD_DIM
ROPE_DIM = ATT_HEAD_DIM // 4
ROPE_THETA = 500000.0
DN_HEADS = 8
DN_HEAD_DIM = 128
DN_WIDTH = DN_HEADS * DN_HEAD_DIM
SHORT_CONV = 5
DN_CHUNK = 64
D_FF = 2816
FFN_CONV = 3
EPS = 1e-6
COL_ATT = 3 * ATT_WIDTH
COL_DN_QKV = 3 * DN_WIDTH
COL_DN_Z = DN_WIDTH
COL_DN_SMALL = 4 * DN_HEADS
COL_GATES = 2 * D_MODEL
IN_COLS = COL_ATT + COL_DN_QKV + COL_DN_Z + COL_DN_SMALL + COL_GATES

kernel_name = "hybrid_dilated_attn_gdn_encoder"


def rms_norm(x, g):
    xf = x.astype(jnp.float32)
    y = xf * lax.rsqrt(jnp.mean(xf * xf, axis=-1, keepdims=True) + EPS)
    return (y * g.astype(jnp.float32)).astype(x.dtype)


def l2_normalize(t):
    tf = t.astype(jnp.float32)
    return tf * lax.rsqrt(jnp.sum(tf * tf, axis=-1, keepdims=True) + EPS)


def centred_depthwise_conv(t, w):
    K = w.shape[0]
    r = K // 2
    S = t.shape[1]
    tp = jnp.pad(t, ((0, 0), (r, r), (0, 0)))
    out = tp[:, 0:S] * w[0]
    for i in range(1, K):
        out = out + tp[:, i:i + S] * w[i]
    return out


def partial_rope(t, pos):
    half = ROPE_DIM // 2
    inv = ROPE_THETA ** (-jnp.arange(half, dtype=jnp.float32) / half)
    ang = pos[:, None] * inv[None, :]
    cos = jnp.cos(ang)[None, :, None, :]
    sin = jnp.sin(ang)[None, :, None, :]
    tr = t[..., :ROPE_DIM].astype(jnp.float32)
    t1, t2 = tr[..., :half], tr[..., half:]
    rot = jnp.concatenate([t1 * cos - t2 * sin, t2 * cos + t1 * sin], axis=-1).astype(t.dtype)
    return jnp.concatenate([rot, t[..., ROPE_DIM:]], axis=-1)


def dilated_window_attention(q, k, v, window, dil):
    B, S, H, Dh = q.shape
    half = window // (2 * dil)
    blk = half
    L = S // dil
    nb = -(-L // blk)
    Lp = nb * blk

    def residue_major(t):
        t = t.reshape(B, L, dil, H, Dh).transpose(0, 2, 1, 3, 4)
        return jnp.pad(t, ((0, 0), (0, 0), (0, Lp - L), (0, 0), (0, 0)))

    def key_windows(t):
        t = jnp.pad(residue_major(t), ((0, 0), (0, 0), (blk, blk), (0, 0), (0, 0)))
        t = t.reshape(B, dil, nb + 2, blk, H, Dh)
        return jnp.concatenate([t[:, :, :-2], t[:, :, 1:-1], t[:, :, 2:]], axis=3)

    qs = residue_major(q).reshape(B, dil, nb, blk, H, Dh).astype(jnp.float32)
    ks = key_windows(k).astype(jnp.float32)
    vs = key_windows(v).astype(jnp.float32)
    qpos = jnp.arange(nb)[:, None] * blk + jnp.arange(blk)[None, :]
    kpos = (jnp.arange(nb)[:, None] - 1) * blk + jnp.arange(3 * blk)[None, :]
    diff = kpos[:, None, :] - qpos[:, :, None]
    valid = (jnp.abs(diff) <= half) & (kpos[:, None, :] >= 0) & (kpos[:, None, :] < L)
    s = jnp.einsum('bdnqhe,bdnkhe->bdnhqk', qs, ks) * (Dh ** -0.5)
    s = jnp.where(valid[None, None, :, None], s, -1e30)
    m = jnp.max(s, axis=-1, keepdims=True)
    p = jnp.exp(s - m)
    den = jnp.sum(p, axis=-1, keepdims=True)
    o = jnp.einsum('bdnhqk,bdnkhe->bdnqhe', p / den, vs)
    lse = (m + jnp.log(den))[..., 0]
    o = o.reshape(B, dil, Lp, H, Dh)[:, :, :L].transpose(0, 2, 1, 3, 4).reshape(B, S, H, Dh)
    lse = lse.transpose(0, 1, 2, 4, 3).reshape(B, dil, Lp, H)[:, :, :L].transpose(0, 2, 1, 3).reshape(B, S, H)
    return o, lse


def dilated_attention_branch(qkv, pos):
    B, S, _ = qkv.shape
    q, k, v = jnp.split(qkv, 3, axis=-1)
    q = partial_rope(q.reshape(B, S, ATT_HEADS, ATT_HEAD_DIM), pos)
    k = partial_rope(k.reshape(B, S, ATT_HEADS, ATT_HEAD_DIM), pos)
    v = v.reshape(B, S, ATT_HEADS, ATT_HEAD_DIM)
    outs, lses = [], []
    for gi, (window, dil) in enumerate(DILATED_GROUPS):
        sl = slice(gi * ATT_HEADS_PER_GROUP, (gi + 1) * ATT_HEADS_PER_GROUP)
        o, l = dilated_window_attention(q[:, :, sl], k[:, :, sl], v[:, :, sl], window, dil)
        outs.append(o)
        lses.append(l)
    wts = jax.nn.softmax(jnp.stack(lses, axis=0), axis=0)
    o = jnp.sum(wts[..., None] * jnp.stack(outs, axis=0), axis=0)
    return o.reshape(B, S, ATT_OUT_WIDTH).astype(qkv.dtype)


def gated_delta_rule(q, k, v, g, beta):
    B, S, H, Dk = q.shape
    Dv = v.shape[-1]
    C = DN_CHUNK
    N = S // C

    def chunks(t):
        t = t.reshape((B, N, C, H) + t.shape[3:])
        return jnp.moveaxis(t, 3, 1)

    qc, kc, vc = chunks(q), chunks(k), chunks(v)
    gc = jnp.cumsum(chunks(g), axis=-1)
    bc = chunks(beta)
    tril = jnp.tril(jnp.ones((C, C), dtype=bool))
    strict = jnp.tril(jnp.ones((C, C), dtype=bool), -1)
    gdiff = gc[..., :, None] - gc[..., None, :]
    dmask = jnp.where(tril, jnp.exp(jnp.where(tril, gdiff, 0.0)), 0.0)
    kb = kc * bc[..., None]
    vb = vc * bc[..., None]
    m_low = jnp.where(strict, jnp.einsum('bhnid,bhnjd->bhnij', kb, kc) * dmask, 0.0)
    eye = jnp.eye(C, dtype=jnp.float32)
    rhs = jnp.concatenate([vb, kb * jnp.exp(gc)[..., None]], axis=-1)
    sol = lax.linalg.triangular_solve(m_low + eye, rhs, left_side=True, lower=True, unit_diagonal=True)
    u, w = sol[..., :Dv], sol[..., Dv:]
    intra = jnp.where(tril, jnp.einsum('bhnid,bhnjd->bhnij', qc, kc) * dmask, 0.0)

    def step(state, xs):
        qi, ki, ui, wi, gi, ai = xs
        v_new = ui - jnp.einsum('bhck,bhkv->bhcv', wi, state)
        o = (jnp.einsum('bhck,bhkv->bhcv', qi * jnp.exp(gi)[..., None], state)
             + jnp.einsum('bhij,bhjv->bhiv', ai, v_new))
        glast = gi[..., -1]
        state = (state * jnp.exp(glast)[..., None, None]
                 + jnp.einsum('bhck,bhcv->bhkv', ki * jnp.exp(glast[..., None] - gi)[..., None], v_new))
        return state, o

    xs = tuple(jnp.moveaxis(t, 2, 0) for t in (qc, kc, u, w, gc, intra))
    state0 = jnp.zeros((B, H, Dk, Dv), dtype=jnp.float32)
    _, o = lax.scan(step, state0, xs)
    return o.transpose(1, 0, 3, 2, 4).reshape(B, S, H, Dv)


def deltanet_branch(qkv_raw, z, small, conv_qkv_w, a_log_f, a_log_b, dt_bias_f, dt_bias_b, out_norm_g):
    B, S, _ = qkv_raw.shape
    qkv = jax.nn.silu(centred_depthwise_conv(qkv_raw, conv_qkv_w))
    q, k, v = jnp.split(qkv, 3, axis=-1)
    q = l2_normalize(q.reshape(B, S, DN_HEADS, DN_HEAD_DIM)) * (DN_HEAD_DIM ** -0.5)
    k = l2_normalize(k.reshape(B, S, DN_HEADS, DN_HEAD_DIM))
    v = v.reshape(B, S, DN_HEADS, DN_HEAD_DIM).astype(jnp.float32)
    small = small.astype(jnp.float32)
    beta_f = jax.nn.sigmoid(small[..., 0:DN_HEADS])
    beta_b = jax.nn.sigmoid(small[..., DN_HEADS:2 * DN_HEADS])
    g_f = -jnp.exp(a_log_f.astype(jnp.float32)) * jax.nn.softplus(small[..., 2 * DN_HEADS:3 * DN_HEADS] + dt_bias_f.astype(jnp.float32))
    g_b = -jnp.exp(a_log_b.astype(jnp.float32)) * jax.nn.softplus(small[..., 3 * DN_HEADS:4 * DN_HEADS] + dt_bias_b.astype(jnp.float32))
    o_f = gated_delta_rule(q, k, v, g_f, beta_f)
    flip = lambda t: jnp.flip(t, axis=1)
    o_b = flip(gated_delta_rule(flip(q), flip(k), flip(v), flip(g_b), flip(beta_b)))
    o = rms_norm(o_f + o_b, out_norm_g)
    o = o * jax.nn.silu(z.reshape(B, S, DN_HEADS, DN_HEAD_DIM).astype(jnp.float32))
    return o.reshape(B, S, DN_WIDTH).astype(qkv_raw.dtype)


def trunk(x, norm_mix_g, w_in, conv_qkv_w, a_log_f, a_log_b, dt_bias_f, dt_bias_b, out_norm_g,
          w_branch_a, w_branch_b, w_out, norm_ffn_g, w_up, ffn_conv_w, ffn_conv_b, w_down, norm_final_g):
    B, S, _ = x.shape
    pos = jnp.arange(S, dtype=jnp.float32)
    for _layer in range(DEPTH):
        h = rms_norm(x, norm_mix_g)
        proj = h @ w_in
        o0 = COL_ATT
        o1 = o0 + COL_DN_QKV
        o2 = o1 + COL_DN_Z
        o3 = o2 + COL_DN_SMALL
        att_qkv = proj[..., :o0]
        dn_qkv = proj[..., o0:o1]
        dn_z = proj[..., o1:o2]
        dn_small = proj[..., o2:o3]
        gate_a = jax.nn.sigmoid(proj[..., o3:o3 + D_MODEL])
        gate_b = jax.nn.sigmoid(proj[..., o3 + D_MODEL:])
        y_a = dilated_attention_branch(att_qkv, pos) @ w_branch_a
        y_b = deltanet_branch(dn_qkv, dn_z, dn_small, conv_qkv_w, a_log_f, a_log_b, dt_bias_f, dt_bias_b, out_norm_g) @ w_branch_b
        x = x + (gate_a * y_a + gate_b * y_b) @ w_out
        h2 = rms_norm(x, norm_ffn_g)
        up = h2 @ w_up
        gate, val = jnp.split(up, 2, axis=-1)
        gate = centred_depthwise_conv(gate, ffn_conv_w) + ffn_conv_b
        x = x + (jax.nn.gelu(gate, approximate=False) * val) @ w_down
    return rms_norm(x, norm_final_g)


def setup_inputs(seed: int = 0) -> dict:
    key = jax.random.key(seed)
    ks = jax.random.split(key, 24)
    f32 = jnp.float32
    nrm = lambda k, shape, fan_in: jax.random.normal(k, shape, f32) * (fan_in ** -0.5)
    gain = lambda k, n: 1.0 + 0.02 * jax.random.normal(k, (n,), f32)

    def dt_bias(k):
        dt = jnp.exp(jax.random.uniform(k, (DN_HEADS,), f32, math.log(1e-3), math.log(1e-1)))
        return dt + jnp.log(-jnp.expm1(-dt))

    return {
        'x_prompt': jax.random.normal(ks[0], (BATCH, SEQ, D_MODEL), f32),
        'x_sample': jax.random.normal(ks[1], (DEC_BATCH, DEC_SEQ, D_MODEL), f32),
        'norm_mix_g': gain(ks[2], D_MODEL),
        'w_in': nrm(ks[3], (D_MODEL, IN_COLS), D_MODEL),
        'conv_qkv_w': nrm(ks[4], (SHORT_CONV, COL_DN_QKV), SHORT_CONV),
        'a_log_f': jnp.log(jax.random.uniform(ks[5], (DN_HEADS,), f32, 1.0, 16.0)),
        'a_log_b': jnp.log(jax.random.uniform(ks[6], (DN_HEADS,), f32, 1.0, 16.0)),
        'dt_bias_f': dt_bias(ks[7]),
        'dt_bias_b': dt_bias(ks[8]),
        'out_norm_g': gain(ks[9], DN_HEAD_DIM),
        'w_branch_a': nrm(ks[10], (ATT_OUT_WIDTH, D_MODEL), ATT_OUT_WIDTH),
        'w_branch_b': nrm(ks[11], (DN_WIDTH, D_MODEL), DN_WIDTH),
        'w_out': nrm(ks[12], (D_MODEL, D_MODEL), D_MODEL),
        'norm_ffn_g': gain(ks[13], D_MODEL),
        'w_up': nrm(ks[14], (D_MODEL, 2 * D_FF), D_MODEL),
        'ffn_conv_w': nrm(ks[15], (FFN_CONV, D_FF), FFN_CONV),
        'ffn_conv_b': 0.02 * jax.random.normal(ks[16], (D_FF,), f32),
        'w_down': nrm(ks[17], (D_FF, D_MODEL), D_FF),
        'norm_final_g': gain(ks[18], D_MODEL),
    }


def reference(x_prompt, x_sample, norm_mix_g, w_in, conv_qkv_w, a_log_f, a_log_b, dt_bias_f, dt_bias_b,
              out_norm_g, w_branch_a, w_branch_b, w_out, norm_ffn_g, w_up, ffn_conv_w, ffn_conv_b, w_down,
              norm_final_g):
    y_prompt = trunk(x_prompt, norm_mix_g, w_in, conv_qkv_w, a_log_f, a_log_b, dt_bias_f, dt_bias_b, out_norm_g,
                     w_branch_a, w_branch_b, w_out, norm_ffn_g, w_up, ffn_conv_w, ffn_conv_b, w_down, norm_final_g)
    y_sample = trunk(x_sample, norm_mix_g, w_in, conv_qkv_w, a_log_f, a_log_b, dt_bias_f, dt_bias_b, out_norm_g,
                     w_branch_a, w_branch_b, w_out, norm_ffn_g, w_up, ffn_conv_w, ffn_conv_b, w_down, norm_final_g)
    return (y_prompt, y_sample)
```

```python
import numpy as np
from contextlib import ExitStack
import concourse.bass as bass
import concourse.mybir as mybir
from concourse.bass_utils import run_bass_kernel_spmd

F32 = mybir.dt.float32
BF16 = mybir.dt.bfloat16
AF = mybir.ActivationFunctionType
ALU = mybir.AluOpType
AX = mybir.AxisListType

EPOCH = 12000
NRING = 6


class Ev:
    __slots__ = ("lane", "sem", "val")

    def __init__(self, lane, sem, val):
        self.lane, self.sem, self.val = lane, sem, val


class Lane:
    def __init__(self, trk, name, eng, dma, seen):
        self.trk, self.name, self.eng, self.dma, self.seen = trk, name, eng, dma, seen
        self.sem = None
        self.count = 0
        self.nsem = 0
        self.ring = []
        self.ndma = 0
        self.pending = False
        if dma:
            self.ring = [trk.new_sem(f"{name}r{i}") for i in range(NRING)]
        else:
            self._newsem()

    def _newsem(self):
        self.sem = self.trk.new_sem(f"{self.name}e{self.nsem}")
        self.nsem += 1
        self.count = 0

    def wait(self, ev):
        k = id(ev.sem)
        if self.seen.get(k, 0) >= ev.val:
            return
        self.eng.wait_ge(ev.sem, ev.val)
        self.seen[k] = ev.val

    def mark(self, ins, inc):
        if self.dma:
            i = self.ndma
            self.ndma += 1
            k = i % NRING
            sem = self.ring[k]
            ins.then_inc(sem, 16)
            return Ev(self, sem, 16 * (i // NRING + 1))
        if inc:
            self.count += 1
            ins.then_inc(self.sem, 1)
            ev = Ev(self, self.sem, self.count)
            self.pending = False
            if self.count >= EPOCH:
                self._newsem()
            return ev
        self.pending = True
        return Ev(self, self.sem, self.count + 1)

    def pre_dma(self):
        i = self.ndma
        if i >= NRING:
            k = i % NRING
            self.wait(Ev(self, self.ring[k], 16 * (i // NRING)))

    def latest(self):
        if self.dma:
            out = []
            for j in range(max(0, self.ndma - NRING), self.ndma):
                out.append(Ev(self, self.ring[j % NRING], 16 * (j // NRING + 1)))
            return out
        assert not self.pending
        if self.count == 0:
            return []
        return [Ev(self, self.sem, self.count)]


class St:
    __slots__ = ("w", "r")

    def __init__(self):
        self.w = None
        self.r = []


class Buf:
    def __init__(self, t, name):
        self.t = t
        self.name = name
        self.ap = t.ap() if hasattr(t, "ap") and callable(getattr(t, "ap")) else t
        self.st = {}

    def __getitem__(self, idx):
        return self.ap[idx]

    def states(self, key):
        if key is None:
            if None not in self.st:
                self.st[None] = St()
            return list(self.st.values())
        out = []
        if key not in self.st:
            self.st[key] = St()
        out.append(self.st[key])
        if None in self.st:
            out.append(self.st[None])
        return out


class Trk:
    def __init__(self, nc, es):
        self.nc, self.es = nc, es
        self.sem_es = es
        self.nsems = 0
        s_act, s_pool, s_sp = {}, {}, {}
        self.pe = Lane(self, "pe", nc.tensor, False, {})
        self.act = Lane(self, "act", nc.scalar, False, s_act)
        self.dve = Lane(self, "dve", nc.vector, False, {})
        self.pool = Lane(self, "pool", nc.gpsimd, False, s_pool)
        self.sp = Lane(self, "sp", nc.sync, True, s_sp)
        self.actq = Lane(self, "actq", nc.scalar, True, s_act)
        self.poolq = Lane(self, "poolq", nc.gpsimd, True, s_pool)
        self.lanes = [self.pe, self.act, self.dve, self.pool, self.sp, self.actq, self.poolq]
        self.nops = 0

    def new_sem(self, name):
        self.nsems += 1
        return self.sem_es.enter_context(self.nc.semaphore(name))

    def sbuf(self, name, shape, dt):
        self.nbuf = getattr(self, "nbuf", 0) + 1
        name = f"{name}_{self.nbuf}"
        return Buf(self.es.enter_context(self.nc.sbuf_tensor(name, list(shape), dt)), name)

    def psum(self, name, shape, dt):
        return Buf(self.es.enter_context(self.nc.psum_tensor(name, list(shape), dt)), name)

    def op(self, lane, fn, reads=(), writes=(), inc=True):
        evs = []
        for (b, k) in reads:
            for st in b.states(k):
                if st.w is not None:
                    evs.append((st.w, True))
        for (b, k) in writes:
            for st in b.states(k):
                if st.w is not None:
                    evs.append((st.w, False))
                for r in st.r:
                    evs.append((r, False))
        if lane.dma:
            lane.pre_dma()
        for ev, raw in evs:
            if ev.lane is lane and (not lane.dma):
                if lane is self.pe and not raw:
                    continue
                assert ev.val <= lane.count or ev.sem is not lane.sem, "same-lane wait on a pending event"
            lane.wait(ev)
        ins = fn()
        ev = lane.mark(ins, inc)
        for (b, k) in reads:
            for st in (b.states(k)[:1] if k is not None else [b.st[None]]):
                if not lane.dma:
                    st.r = [r for r in st.r if r.lane is not lane]
                st.r.append(ev)
        for (b, k) in writes:
            if k is None:
                for st in b.st.values():
                    st.w = None
                    st.r = []
                b.st[None].w = ev
            else:
                st = b.states(k)[0]
                st.w = ev
                st.r = []
        self.nops += 1
        return ev

    def barrier(self):
        evs = []
        for l in self.lanes:
            evs += l.latest()
        for l in self.lanes:
            for ev in evs:
                if ev.lane is l and not l.dma:
                    continue
                l.wait(ev)

    def finish(self, lane, bufs):
        for (b, k) in bufs:
            for st in b.states(k):
                if st.w is not None:
                    lane.wait(st.w)


def phase1(T, nc, x_ap, tok0, Ts, hT, gB, ident, ps_t, xbufs, xnbufs, ssq, junk):
    nt = Ts // 128
    for i in range(nt):
        xb = xbufs[i % len(xbufs)]
        xn = xnbufs[i % len(xnbufs)]
        pt = ps_t[i % len(ps_t)]
        o = 4 * (i % 4)
        if i % 2 == 0:
            T.op(T.sp, lambda: nc.sync.dma_start(out=xb[:], in_=x_ap[tok0 + 128 * i: tok0 + 128 * (i + 1), :]),
                 writes=[(xb, None)])
        else:
            T.op(T.poolq, lambda: nc.gpsimd.dma_start(out=xb[:], in_=x_ap[tok0 + 128 * i: tok0 + 128 * (i + 1), :]),
                 writes=[(xb, None)])
        T.op(T.act, lambda: nc.scalar.activation(out=junk[:], in_=xb[:], func=AF.Square, accum_out=ssq[:, o + 0:o + 1]),
             reads=[(xb, None)], writes=[(junk, None), (ssq, o + 0)])
        T.op(T.dve, lambda: nc.vector.tensor_scalar(out=ssq[:, o + 1:o + 2], in0=ssq[:, o + 0:o + 1], scalar1=1.0 / 1024, scalar2=1e-6,
                                                    op0=ALU.mult, op1=ALU.add),
             reads=[(ssq, o + 0)], writes=[(ssq, o + 1)])
        T.op(T.act, lambda: nc.scalar.activation(out=ssq[:, o + 2:o + 3], in_=ssq[:, o + 1:o + 2], func=AF.Sqrt),
             reads=[(ssq, o + 1)], writes=[(ssq, o + 2)])
        T.op(T.dve, lambda: nc.vector.reciprocal(out=ssq[:, o + 3:o + 4], in_=ssq[:, o + 2:o + 3]),
             reads=[(ssq, o + 2)], writes=[(ssq, o + 3)])
        T.op(T.dve, lambda: nc.vector.tensor_scalar(out=xn[:], in0=xb[:], scalar1=ssq[:, o + 3:o + 4], scalar2=None, op0=ALU.mult),
             reads=[(xb, None), (ssq, o + 3)], writes=[(xn, None)])
        for k in range(8):
            T.op(T.pe, lambda: nc.tensor.transpose(out=pt[:, k * 128:(k + 1) * 128], in_=xn[:, k * 128:(k + 1) * 128], identity=ident[:]),
                 reads=[(xn, None), (ident, None)], writes=[(pt, None)], inc=(k == 7))
        T.op(T.dve, lambda: nc.vector.tensor_tensor(out=hT[:, :, 128 * i:128 * (i + 1)],
                                                    in0=pt[:].rearrange("p (k t) -> p k t", k=8), in1=gB[:], op=ALU.mult),
             reads=[(pt, None), (gB, None)], writes=[(hT, ("t", i))])


DILS = (1, 4, 16)


def att_consts_host():
    p = np.arange(128)[:, None, None] % 64
    t = np.arange(4)[None, :, None]
    n = np.arange(128)[None, None, :]
    mask = (np.abs(64 * (t - 1) + p - n) <= 64).astype(np.float32).reshape(128, 512)
    half = 8
    inv = (np.float32(500000.0) ** (-np.arange(half, dtype=np.float32) / np.float32(half))).astype(np.float32)
    pos = np.arange(4096, dtype=np.float32)
    ang = (pos[:, None] * inv[None, :]).astype(np.float32)
    cos = np.cos(ang).astype(np.float32).T
    sin = np.sin(ang).astype(np.float32).T
    C = np.ones((64, 4096), np.float32)
    S = np.zeros((64, 4096), np.float32)
    C[0:8] = cos; C[8:16] = cos
    S[0:8] = -sin; S[8:16] = sin
    C = np.concatenate([C, C], 0); S = np.concatenate([S, S], 0)
    onesbd = np.zeros((128, 128), np.float32)
    onesbd[0:64, 0:64] = 1; onesbd[64:128, 64:128] = 1
    return dict(amask=mask, ropec=np.ascontiguousarray(C), ropes=np.ascontiguousarray(S), onesbd=onesbd)


class AttBufs:
    def __init__(self, T, Tmax):
        self.qTs = [T.sbuf(f"qTc{i}", [128, Tmax], BF16) for i in range(2)]
        self.kTs = [T.sbuf(f"kTc{i}", [128, Tmax], BF16) for i in range(2)]
        self.vbd = T.sbuf("vbd", [128, Tmax // 64, 128], BF16)
        self.vT = T.sbuf("vTfm", [128, Tmax], BF16)
        self.acc = T.sbuf("aacc", [128, 2, Tmax], F32)
        self.wst = [T.sbuf(f"awst{i}", [128, 8, 128], F32) for i in range(2)]
        self.w = {nm: T.sbuf(f"aw_{nm}", [128, 8, 128], BF16) for nm in ("q", "k", "v", "qs", "ks")}
        self.ct = [T.sbuf(f"act{i}", [128, 512], F32) for i in range(2)]
        self.sg = [T.sbuf(f"asg{i}", [128, 512], F32) for i in range(2)]
        self.t1 = [T.sbuf(f"at1{i}", [128, 512], F32) for i in range(2)]
        self.t2 = [T.sbuf(f"at2{i}", [128, 512], F32) for i in range(2)]
        self.pe = [T.sbuf(f"ape{i}", [128, 512], BF16) for i in range(4)]
        self.pm = [T.sbuf(f"apm{i}", [128, 512], BF16) for i in range(4)]
        self.m_norm = T.sbuf("am_n", [128, 512], BF16)
        self.m_lo = T.sbuf("am_lo", [128, 512], BF16)
        self.m_hi = T.sbuf("am_hi", [128, 512], BF16)
        self.mf = T.sbuf("am_f", [128, 512], F32)
        self.onesbd = T.sbuf("aonesbd", [128, 128], BF16)
        self.onesf = T.sbuf("aonesf", [128, 128], F32)
        self.rec = [T.sbuf(f"arec{i}", [128, 512], F32) for i in range(2)]
        self.ao = [T.sbuf(f"aao{i}", [128, 512], BF16) for i in range(2)]


def att_setup(T, nc, A, amask_d, onesbd_d, linkcol):
    T.op(T.sp, lambda: nc.sync.dma_start(out=A.mf[:], in_=amask_d[:, :]), writes=[(A.mf, None)])
    T.op(T.sp, lambda: nc.sync.dma_start(out=A.onesf[:], in_=onesbd_d[:, :]), writes=[(A.onesf, None)])
    T.op(T.dve, lambda: nc.vector.tensor_copy(out=A.onesbd[:], in_=A.onesf[:]), reads=[(A.onesf, None)], writes=[(A.onesbd, None)])
    T.op(T.dve, lambda: nc.vector.tensor_copy(out=A.m_norm[:], in_=A.mf[:]), reads=[(A.mf, None)], writes=[(A.m_norm, None)])
    T.op(T.dve, lambda: nc.vector.tensor_copy(out=A.m_lo[:], in_=A.mf[:]), reads=[(A.mf, None)], writes=[(A.m_lo, None)])
    T.op(T.dve, lambda: nc.vector.tensor_copy(out=A.m_hi[:], in_=A.mf[:]), reads=[(A.mf, None)], writes=[(A.m_hi, None)])
    T.op(T.dve, lambda: nc.vector.tensor_scalar(out=A.m_lo[:, 0:128], in0=A.mf[:, 0:128], scalar1=linkcol[:, 0:1], scalar2=None, op0=ALU.mult),
         reads=[(A.mf, None), (linkcol, None)], writes=[(A.m_lo, None)])
    T.op(T.dve, lambda: nc.vector.tensor_scalar(out=A.m_hi[:, 384:512], in0=A.mf[:, 384:512], scalar1=linkcol[:, 0:1], scalar2=None, op0=ALU.mult),
         reads=[(A.mf, None), (linkcol, None)], writes=[(A.m_hi, None)])
    T.op(T.pool, lambda: nc.gpsimd.memset(A.vbd[:], 0.0), writes=[(A.vbd, None)])


def load_w_bf16(T, nc, w_d, col0, ncols, stage, dst, cast_lane, cast_eng, kchunks=8):
    T.op(T.sp, lambda: nc.sync.dma_start(out=stage[:, 0:kchunks, 0:ncols],
                                         in_=w_d[0:kchunks * 128, col0:col0 + ncols].rearrange("(k p) c -> p k c", p=128)),
         writes=[(stage, None)])
    T.op(cast_lane, lambda: cast_eng.tensor_copy(out=dst[:, 0:kchunks, 0:ncols], in_=stage[:, 0:kchunks, 0:ncols]),
         reads=[(stage, None)], writes=[(dst, None)])


def attention_phase(T, nc, A, hT, Ts, w_in, ropec_d, ropes_d, ps, att_out_cb, ident):
    nblk = Ts // 512
    cnt = {"w": 0, "b": 0, "q": 0}
    units = [(hp, g) for hp in range(4) for g in range(3)]

    def proj_gen(n):
        hp, g = units[n]
        dil = DILS[g]
        Lc = Ts // dil
        col_q = (g * 8 + 2 * hp) * 64
        qT, kT = A.qTs[n % 2], A.kTs[n % 2]
        for nm, off in (("q", 0), ("k", 1536), ("v", 3072)):
            st = A.wst[cnt["w"] % 2]; cnt["w"] += 1
            load_w_bf16(T, nc, w_in, off + col_q, 128, st, A.w[nm], T.pool, nc.gpsimd)
            yield None
        for nm in ("q", "k"):
            src, dst = A.w[nm], A.w[nm + "s"]
            sv = src[:].rearrange("p k (h d) -> p k h d", h=2)
            dv = dst[:].rearrange("p k (h d) -> p k h d", h=2)
            T.op(T.pool, lambda: nc.gpsimd.tensor_copy(out=dst[:], in_=src[:]), reads=[(src, None)], writes=[(dst, None)])
            T.op(T.pool, lambda: nc.gpsimd.tensor_copy(out=dv[:, :, :, 0:8], in_=sv[:, :, :, 8:16]), reads=[(src, None)], writes=[(dst, None)])
            T.op(T.pool, lambda: nc.gpsimd.tensor_copy(out=dv[:, :, :, 8:16], in_=sv[:, :, :, 0:8]), reads=[(src, None)], writes=[(dst, None)])
            yield None
        for b in range(nblk):
            i2 = cnt["b"] % 2; cnt["b"] += 1
            ct, sg = A.ct[i2], A.sg[i2]
            T.op(T.sp, lambda: nc.sync.dma_start(out=ct[:], in_=ropec_d[:, 512 * b:512 * (b + 1)]), writes=[(ct, None)])
            T.op(T.sp, lambda: nc.sync.dma_start(out=sg[:], in_=ropes_d[:, 512 * b:512 * (b + 1)]), writes=[(sg, None)])
            hk = [(hT, ("t", 4 * b + j)) for j in range(4)]
            for xi, nm in enumerate(("q", "k")):
                p1, p2 = ps[6], ps[7]
                for k in range(8):
                    T.op(T.pe, lambda: nc.tensor.matmul(p1[:], lhsT=A.w[nm][:, k, :], rhs=hT[:, k, 512 * b:512 * (b + 1)], start=(k == 0), stop=(k == 7)),
                         reads=[(A.w[nm], None)] + hk, writes=[(p1, None)], inc=(k == 7))
                    if k % 2 == 1:
                        yield None
                for k in range(8):
                    T.op(T.pe, lambda: nc.tensor.matmul(p2[:], lhsT=A.w[nm + "s"][:, k, :], rhs=hT[:, k, 512 * b:512 * (b + 1)], start=(k == 0), stop=(k == 7)),
                         reads=[(A.w[nm + "s"], None)] + hk, writes=[(p2, None)], inc=(k == 7))
                    if k % 2 == 1:
                        yield None
                t1, t2 = A.t1[xi], A.t2[xi]
                yield T.op(T.dve, lambda: nc.vector.tensor_tensor(out=t1[:], in0=p1[:], in1=ct[:], op=ALU.mult),
                           reads=[(p1, None), (ct, None)], writes=[(t1, None)])
                yield T.op(T.dve, lambda: nc.vector.tensor_tensor(out=t2[:], in0=p2[:], in1=sg[:], op=ALU.mult),
                           reads=[(p2, None), (sg, None)], writes=[(t2, None)])
                dstb = qT if nm == "q" else kT
                cl = 512 // dil
                oap = bass.AP(tensor=dstb.t, offset=(512 * b) // dil, ap=[[dstb.ap.ap[0][0], 128], [1, cl], [Lc, dil]])
                yield T.op(T.pool, lambda: nc.gpsimd.tensor_tensor(out=oap, in0=t1[:].rearrange("p (c r) -> p c r", r=dil),
                                                                   in1=t2[:].rearrange("p (c r) -> p c r", r=dil), op=ALU.add),
                           reads=[(t1, None), (t2, None)], writes=[(dstb, ("b", b))])

    def v1_gen(n):
        for b in range(nblk):
            pvt = ps[6 + (b % 2)]
            hk = [(hT, ("t", 4 * b + j)) for j in range(4)]
            for k in range(8):
                T.op(T.pe, lambda: nc.tensor.matmul(pvt[:], lhsT=A.w["v"][:, k, :], rhs=hT[:, k, 512 * b:512 * (b + 1)], start=(k == 0), stop=(k == 7)),
                     reads=[(A.w["v"], None)] + hk, writes=[(pvt, None)], inc=(k == 7))
            yield T.op(T.act, lambda: nc.scalar.copy(out=A.vT[:, 512 * b:512 * (b + 1)], in_=pvt[:]), reads=[(pvt, None)], writes=[(A.vT, ("b", b))])

    def v_gen(n):
        hp, g = units[n]
        dil = DILS[g]
        Lc = Ts // dil
        ntile = Ts // 64
        pstep = A.vT.ap.ap[0][0]
        for gq in range(ntile // 4):
            pv = ps[4 + (gq % 2)]
            pvb = pv.ap.bitcast(BF16)
            for u in range(4):
                vt = gq * 4 + u
                r, j = divmod(vt, Lc // 64)
                tok = r + dil * 64 * j
                apA = bass.AP(tensor=A.vT.t, offset=tok, ap=[[pstep, 64], [dil, 64]])
                apB = bass.AP(tensor=A.vT.t, offset=64 * pstep + tok, ap=[[pstep, 64], [dil, 64]])
                T.op(T.pe, lambda: nc.tensor.transpose(out=pvb[0:64, u * 128:u * 128 + 64], in_=apA, identity=ident[0:64, 0:64], tile_position=(0, 0)),
                     reads=[(A.vT, None), (ident, None)], writes=[(pv, None)], inc=False)
                T.op(T.pe, lambda: nc.tensor.transpose(out=pvb[64:128, u * 128 + 64:u * 128 + 128], in_=apB, identity=ident[64:128, 64:128], tile_position=(64, 64)),
                     reads=[(A.vT, None), (ident, None)], writes=[(pv, None)], inc=(u == 3))
            pvv = pvb[:, 0:512].rearrange("p (u c) -> p u c", u=4)
            yield T.op(T.act, lambda: nc.scalar.copy(out=A.vbd[0:64, gq * 4:gq * 4 + 4, 0:64], in_=pvv[0:64, :, 0:64]),
                       reads=[(pv, None)], writes=[(A.vbd, ("v", gq, 0))])
            yield T.op(T.dve, lambda: nc.vector.tensor_copy(out=A.vbd[64:128, gq * 4:gq * 4 + 4, 64:128], in_=pvv[64:128, :, 64:128]),
                       reads=[(pv, None)], writes=[(A.vbd, ("v", gq, 1))])

    def qblock(n, r, qb, qi):
        hp, g = units[n]
        dil = DILS[g]
        Lc = Ts // dil
        Lg = 2048 // dil
        qT, kT = A.qTs[n % 2], A.kTs[n % 2]
        sq = (128 * qb) // Lg
        tiles = []
        mk = A.m_norm
        for t in range(4):
            j = 2 * qb - 1 + t
            if j < 0 or j >= Lc // 64:
                continue
            stg = (64 * j) // Lg
            if stg != sq:
                mk = A.m_lo if t == 0 else A.m_hi
            tiles.append((t, j))
        t_a, t_b = tiles[0][0], tiles[-1][0] + 1
        pss = ps[qi % 3]
        pso = ps[3 + (qi % 3)]
        qsl = slice(r * Lc + 128 * qb, r * Lc + 128 * qb + 128)
        for (t, j) in tiles:
            ksl = slice(r * Lc + 64 * j, r * Lc + 64 * j + 64)
            last = (t == tiles[-1][0])
            T.op(T.pe, lambda: nc.tensor.matmul(pss[0:64, t * 128:(t + 1) * 128], lhsT=kT[0:64, ksl], rhs=qT[0:64, qsl],
                                                start=True, stop=True, tile_position=(0, 0)),
                 reads=[(kT, None), (qT, None)], writes=[(pss, None)], inc=False)
            T.op(T.pe, lambda: nc.tensor.matmul(pss[64:128, t * 128:(t + 1) * 128], lhsT=kT[64:128, ksl], rhs=qT[64:128, qsl],
                                                start=True, stop=True, tile_position=(64, 64)),
                 reads=[(kT, None), (qT, None)], writes=[(pss, None)], inc=last)
        yield None
        pe_, pm_ = A.pe[qi % 4], A.pm[qi % 4]
        fs = slice(t_a * 128, t_b * 128)
        yield T.op(T.act, lambda: nc.scalar.activation(out=pe_[:, fs], in_=pss[:, fs], func=AF.Exp, scale=0.125),
                   reads=[(pss, None)], writes=[(pe_, None)])
        yield T.op(T.dve, lambda: nc.vector.tensor_tensor(out=pm_[:, fs], in0=pe_[:, fs], in1=mk[:, fs], op=ALU.mult),
                   reads=[(pe_, None), (mk, None)], writes=[(pm_, None)])
        for ti, (t, j) in enumerate(tiles):
            vt = r * (Lc // 64) + j
            T.op(T.pe, lambda: nc.tensor.matmul(pso[:, 0:128], lhsT=A.vbd[:, vt, :], rhs=pm_[:, t * 128:(t + 1) * 128],
                                                start=(ti == 0), stop=(ti == len(tiles) - 1)),
                 reads=[(A.vbd, ("v", vt // 4, 0)), (A.vbd, ("v", vt // 4, 1)), (pm_, None)], writes=[(pso, None)], inc=False)
        for ti, (t, j) in enumerate(tiles):
            T.op(T.pe, lambda: nc.tensor.matmul(pso[:, 128:256], lhsT=A.onesbd[:], rhs=pm_[:, t * 128:(t + 1) * 128],
                                                start=(ti == 0), stop=(ti == len(tiles) - 1)),
                 reads=[(A.onesbd, None), (pm_, None)], writes=[(pso, None)], inc=(ti == len(tiles) - 1))
        yield None
        tok0 = r + dil * 128 * qb
        aap = bass.AP(tensor=A.acc.t, offset=tok0, ap=[[A.acc.ap.ap[0][0], 128], [A.acc.ap.ap[1][0], 2], [dil, 128]])
        psv = pso[:, 0:256].rearrange("p (a n) -> p a n", a=2)
        akeys = [(A.acc, ("t", tt)) for tt in range(tok0 // 512, (tok0 + dil * 127) // 512 + 1)]
        if g == 0:
            yield T.op(T.dve, lambda: nc.vector.tensor_copy(out=aap, in_=psv), reads=[(pso, None)], writes=akeys)
        else:
            yield T.op(T.dve, lambda: nc.vector.tensor_tensor(out=aap, in0=psv, in1=aap, op=ALU.add),
                       reads=[(pso, None)] + akeys, writes=akeys)

    def finalize(hp):
        for b in range(nblk):
            rec, ao = A.rec[b % 2], A.ao[b % 2]
            T.op(T.dve, lambda: nc.vector.reciprocal(out=rec[:], in_=A.acc[:, 1, 512 * b:512 * (b + 1)]),
                 reads=[(A.acc, ("t", b))], writes=[(rec, None)])
            T.op(T.pool, lambda: nc.gpsimd.tensor_tensor(out=ao[:], in0=A.acc[:, 0, 512 * b:512 * (b + 1)], in1=rec[:], op=ALU.mult),
                 reads=[(A.acc, ("t", b)), (rec, None)], writes=[(ao, None)])
            att_out_cb(hp, b, ao)

    def drain(g_):
        for _ in g_:
            pass

    def side_gen(n):
        yield from proj_gen(n)
        yield from v1_gen(n)

    drain(side_gen(0))
    drain(v_gen(0))
    for n in range(len(units)):
        hp, g = units[n]
        dil = DILS[g]
        nq = (Ts // dil) // 128
        qlist = [(r, qb) for r in range(dil) for qb in range(nq)]
        side = side_gen(n + 1) if n + 1 < len(units) else None
        qi = 0
        gl = []
        while qi < len(qlist) or gl:
            while len(gl) < 3 and qi < len(qlist):
                gl.append(qblock(n, qlist[qi][0], qlist[qi][1], cnt["q"])); qi += 1; cnt["q"] += 1
            for g_ in list(gl):
                try:
                    next(g_)
                except StopIteration:
                    gl.remove(g_)
            if side is not None:
                try:
                    next(side)
                except StopIteration:
                    side = None
        if side is not None:
            drain(side)
        if g == 2:
            finalize(hp)
        if n + 1 < len(units):
            drain(v_gen(n + 1))


NEG = -30000.0
START_GAP = 1


def dn_consts_host():
    j = np.arange(128)[:, None]; i = np.arange(128)[None, :]
    f = np.float32
    return dict(
        triU=(j <= i).astype(f), triL=(j >= i).astype(f), ones128=np.ones((128, 128), f),
        negmU=np.where(i >= j, 0.0, NEG).astype(f), negmL=np.where(i <= j, 0.0, NEG).astype(f),
        strU=(i > j).astype(f), strL=(i < j).astype(f),
        negbd2=np.tile(-((j // 16) == (i // 16)).astype(f), (1, 2)), ident2=np.tile(np.eye(128, dtype=f), (1, 2)),
        ms16=np.tile((((j // 32) == (i // 32)) & ((j // 16) != (i // 16))).astype(f), (1, 2)),
        ms32=np.tile((((j // 64) == (i // 64)) & ((j // 32) != (i // 32))).astype(f), (1, 2)),
        ms64=np.tile(((j // 64) != (i // 64)).astype(f), (1, 2)))


class DnBufs:
    def __init__(self, T, Tmax):
        NC = Tmax // 128
        self.NC = NC
        sb = T.sbuf
        self.wst = [sb(f"dwst{i}", [128, 8, 128], F32) for i in range(1)]
        self.w = {nm: sb(f"dw_{nm}", [128, 8, 128], BF16) for nm in ("q", "k", "v", "z")}
        self.wsm = sb("dwsm", [128, 8, 32], BF16)
        for nm in ("beta", "gc", "egc", "negegc", "kdec", "egtot"):
            setattr(self, nm, sb("d" + nm, [128, NC, 16], F32))
        self.dtb = sb("ddtb", [128, 16], F32)
        self.negA = sb("dnegA", [128, 16], F32)
        self.cw = sb("dcw", [128, 3, 5], F32)
        self.raws = [sb(f"draw{i}", [128, 2052], F32) for i in range(2)]
        self.caccs = [sb(f"dcacc{i}", [128, 2048], F32) for i in range(2)]

        def view(base, off, shape, name):
            b = Buf(base.t, name)
            pstep = base.ap.ap[0][0]
            b.ap = bass.AP(tensor=base.t, offset=off, ap=[[pstep, 128], [shape[2], shape[1]], [1, shape[2]]])
            b.st = base.st
            return b
        self.small = view(self.raws[1], 0, [128, NC, 32], "dsmall")
        self.xs = view(self.caccs[1], 0, [128, NC, 16], "dxs")
        self.gval = view(self.caccs[1], 1024, [128, NC, 16], "dgval")
        self.c2stg3 = view(self.raws[0], 0, [128, 1, 256], "dc2stg")
        self.sq = [sb(f"dsq{i}", [128, 512], F32) for i in range(2)]
        self.rs = [sb(f"drs{i}", [128, 512], F32) for i in range(2)]
        self.epsc = sb("depsc", [128, 1], F32)
        self.qT = sb("dqT", [128, Tmax], BF16)
        self.kT = sb("dkT", [128, Tmax], BF16)
        self.vtm = sb("dvtm", [128, NC, 128], BF16)
        self.oacc = sb("doacc", [128, NC, 128], F32)
        self.cst = {nm: sb("dc_" + nm, [128, 128], F32) for nm in ("triU", "triL", "ones128", "negmU", "negmL", "strU", "strL", "identf")}
        self.identb = sb("dc_identb", [128, 128], BF16)
        R = 4
        self.R = R
        self.U = [[sb(f"dU{d}{r}", [128, 128], BF16) for r in range(R)] for d in range(2)]
        self.iT = [[sb(f"diT{d}{r}", [128, 128], BF16) for r in range(R)] for d in range(2)]
        self.kd = [[sb(f"dkd{d}{r}", [128, 128], BF16) for r in range(R)] for d in range(2)]
        NU = 4
        self.NU = NU
        self.diag = [sb(f"ddiag{u}", [128, 128], F32) for u in range(NU)]
        self.dd = [sb(f"ddd{u}", [128, 128], F32) for u in range(NU)]
        self.dT = self.dd
        self.tmp = [sb(f"dtmp{u}", [128, 128], F32) for u in range(NU)]
        self.AA = [[sb(f"dAA{u}{i}", [128, 256], BF16) for i in range(2)] for u in range(NU)]
        self.UU = [[sb(f"dUU{u}{i}", [128, 256], BF16) for i in range(2)] for u in range(NU)]
        self.NN = [sb(f"dNN{u}", [128, 256], BF16) for u in range(NU)]
        self.T1 = [sb(f"dT1{u}", [128, 128], BF16) for u in range(NU)]
        self.c2 = {nm: sb("dc2_" + nm, [128, 256], BF16) for nm in ("negbd2", "ident2", "ms16", "ms32", "ms64")}
        self.S = [sb(f"dS{d}", [128, 128], F32) for d in range(2)]
        self.Sb = [sb(f"dSb{d}", [128, 128], BF16) for d in range(2)]
        self.Rp = [sb(f"dRp{d}", [128, 128], BF16) for d in range(2)]
        self.vn = [sb(f"dvn{d}", [128, 128], BF16) for d in range(2)]
        self.ot = [sb(f"dot{d}", [128, 128], F32) for d in range(2)]
        self.gob = sb("dgob", [128, 128], F32)
        self.zs = [sb(f"dzs{i}", [128, 128], F32) for i in range(2)]
        self.nss = sb("dnss", [128, NC], F32)
        self.nrs = sb("dnrs", [128, NC], F32)
        self.on = [sb(f"don{i}", [128, 128], BF16) for i in range(2)]
        self.onT = [sb(f"donT{i}", [128, 512], BF16) for i in range(2)]


def dn_setup(T, nc, D, cd, ident_d, dtbf, dtbb, alf, alb, gout):
    for nm in ("triU", "triL", "ones128", "negmU", "negmL", "strU", "strL"):
        T.op(T.sp, lambda: nc.sync.dma_start(out=D.cst[nm][:], in_=cd[nm][:, :]), writes=[(D.cst[nm], None)])
    T.op(T.sp, lambda: nc.sync.dma_start(out=D.cst["identf"][:], in_=ident_d[:, :]), writes=[(D.cst["identf"], None)])
    for nm in ("negbd2", "ident2", "ms16", "ms32", "ms64"):
        T.op(T.sp, lambda: nc.sync.dma_start(out=D.c2stg3[:, 0, :], in_=cd[nm][:, :]), writes=[(D.c2stg3, None)])
        T.op(T.dve, lambda: nc.vector.tensor_copy(out=D.c2[nm][:], in_=D.c2stg3[:, 0, :]), reads=[(D.c2stg3, None)], writes=[(D.c2[nm], None)])
    T.op(T.dve, lambda: nc.vector.tensor_copy(out=D.identb[:], in_=D.cst["identf"][:]), reads=[(D.cst["identf"], None)], writes=[(D.identb, None)])
    T.op(T.sp, lambda: nc.sync.dma_start(out=D.dtb[:, 0:8], in_=dtbf.partition_broadcast(128)), writes=[(D.dtb, None)])
    T.op(T.sp, lambda: nc.sync.dma_start(out=D.dtb[:, 8:16], in_=dtbb.partition_broadcast(128)), writes=[(D.dtb, None)])
    T.op(T.sp, lambda: nc.sync.dma_start(out=D.negA[:, 0:8], in_=alf.partition_broadcast(128)), writes=[(D.negA, None)])
    T.op(T.sp, lambda: nc.sync.dma_start(out=D.negA[:, 8:16], in_=alb.partition_broadcast(128)), writes=[(D.negA, None)])
    T.op(T.sp, lambda: nc.sync.dma_start(out=D.gob[:], in_=gout.partition_broadcast(128)), writes=[(D.gob, None)])
    T.op(T.act, lambda: nc.scalar.activation(out=D.negA[:], in_=D.negA[:], func=AF.Exp), reads=[(D.negA, None)], writes=[(D.negA, None)])
    T.op(T.dve, lambda: nc.vector.tensor_scalar(out=D.negA[:], in0=D.negA[:], scalar1=-1.0, scalar2=None, op0=ALU.mult), reads=[(D.negA, None)], writes=[(D.negA, None)])
    T.op(T.dve, lambda: nc.vector.memset(D.epsc[:], 1e-6), writes=[(D.epsc, None)])


def dn_scalars(T, nc, D, hT, Ts, w_in, ps):
    NC = Ts // 128
    load_w_bf16(T, nc, w_in, 8704, 32, D.wst[0], D.wsm, T.pool, nc.gpsimd)
    for c0 in range(0, NC, 16):
        pb = ps[0]
        for c in range(c0, min(NC, c0 + 16)):
            for k in range(8):
                T.op(T.pe, lambda: nc.tensor.matmul(pb[:, (c - c0) * 32:(c - c0 + 1) * 32], lhsT=hT[:, k, c * 128:(c + 1) * 128], rhs=D.wsm[:, k, :],
                                                    start=(k == 0), stop=(k == 7)),
                     reads=[(hT, ("t", c)), (D.wsm, None)], writes=[(pb, None)], inc=(k == 7))
        n = min(NC, c0 + 16) - c0
        T.op(T.act, lambda: nc.scalar.copy(out=D.small[:, c0:c0 + n, :], in_=pb[:, 0:n * 32].rearrange("p (c s) -> p c s", s=32)),
             reads=[(pb, None)], writes=[(D.small, None)])
    sl = slice(0, NC)
    T.op(T.act, lambda: nc.scalar.activation(out=D.beta[:, sl, :], in_=D.small[:, sl, 0:16], func=AF.Sigmoid), reads=[(D.small, None)], writes=[(D.beta, None)])
    T.op(T.dve, lambda: nc.vector.tensor_tensor(out=D.xs[:, sl, :], in0=D.small[:, sl, 16:32], in1=D.dtb[:].unsqueeze(1).to_broadcast([128, NC, 16]), op=ALU.add),
         reads=[(D.small, None), (D.dtb, None)], writes=[(D.xs, None)])
    T.op(T.act, lambda: nc.scalar.activation(out=D.xs[:, sl, :], in_=D.xs[:, sl, :], func=AF.Exp), reads=[(D.xs, None)], writes=[(D.xs, None)])
    T.op(T.act, lambda: nc.scalar.activation(out=D.xs[:, sl, :], in_=D.xs[:, sl, :], func=AF.Ln, bias=1.0), reads=[(D.xs, None)], writes=[(D.xs, None)])
    T.op(T.dve, lambda: nc.vector.tensor_tensor(out=D.gval[:, sl, :], in0=D.xs[:, sl, :], in1=D.negA[:].unsqueeze(1).to_broadcast([128, NC, 16]), op=ALU.mult),
         reads=[(D.xs, None), (D.negA, None)], writes=[(D.gval, None)])
    pg, pt = ps[1], ps[2]
    for c in range(NC):
        T.op(T.pe, lambda: nc.tensor.matmul(pg[:, c * 16:c * 16 + 8], lhsT=D.cst["triU"][:], rhs=D.gval[:, c, 0:8], start=True, stop=True),
             reads=[(D.cst["triU"], None), (D.gval, None)], writes=[(pg, None)], inc=False)
        T.op(T.pe, lambda: nc.tensor.matmul(pg[:, c * 16 + 8:c * 16 + 16], lhsT=D.cst["triL"][:], rhs=D.gval[:, c, 8:16], start=True, stop=True),
             reads=[(D.cst["triL"], None), (D.gval, None)], writes=[(pg, None)], inc=False)
        T.op(T.pe, lambda: nc.tensor.matmul(pt[:, c * 16:c * 16 + 16], lhsT=D.cst["ones128"][:], rhs=D.gval[:, c, 0:16], start=True, stop=True),
             reads=[(D.cst["ones128"], None), (D.gval, None)], writes=[(pt, None)], inc=(c == NC - 1))
    pgv = pg[:, 0:NC * 16].rearrange("p (c s) -> p c s", s=16)
    ptv = pt[:, 0:NC * 16].rearrange("p (c s) -> p c s", s=16)
    T.op(T.act, lambda: nc.scalar.copy(out=D.gc[:, sl, :], in_=pgv), reads=[(pg, None)], writes=[(D.gc, None)])
    T.op(T.act, lambda: nc.scalar.activation(out=D.egc[:, sl, :], in_=pgv, func=AF.Exp), reads=[(pg, None)], writes=[(D.egc, None)])
    T.op(T.dve, lambda: nc.vector.tensor_scalar(out=D.negegc[:, sl, :], in0=D.egc[:, sl, :], scalar1=-1.0, scalar2=None, op0=ALU.mult),
         reads=[(D.egc, None)], writes=[(D.negegc, None)])
    T.op(T.dve, lambda: nc.vector.tensor_tensor(out=D.kdec[:, sl, :], in0=ptv, in1=D.gc[:, sl, :], op=ALU.subtract),
         reads=[(pt, None), (D.gc, None)], writes=[(D.kdec, None)])
    T.op(T.act, lambda: nc.scalar.activation(out=D.kdec[:, sl, :], in_=D.kdec[:, sl, :], func=AF.Exp), reads=[(D.kdec, None)], writes=[(D.kdec, None)])
    T.op(T.act, lambda: nc.scalar.activation(out=D.egtot[:, sl, :], in_=ptv, func=AF.Exp), reads=[(pt, None)], writes=[(D.egtot, None)])


def dn_head(T, nc, D, hT, Ts, hd, w_in, conv_w, linkcol, ps, out_cb, part):
    NC = Ts // 128
    nseg = Ts // 2048
    nblk = Ts // 512
    def load_weights(names):
        for xi, nm in enumerate(("q", "k", "v", "z")):
            if nm not in names:
                continue
            col = 4608 + xi * 1024 + hd * 128
            load_w_bf16(T, nc, w_in, col, 128, D.wst[0], D.w[nm], T.pool, nc.gpsimd)

    def load_cw():
        with nc.allow_non_contiguous_dma(reason="small"):
            for xi in range(3):
                c0 = xi * 1024 + hd * 128
                T.op(T.sp, lambda: nc.sync.dma_start(out=D.cw[:, xi, :], in_=conv_w[:, c0:c0 + 128].rearrange("k c -> c k")), writes=[(D.cw, None)])
    its = [(xi, nm, sg_) for xi, nm in enumerate(("q", "k", "v")) for sg_ in range(nseg)]

    def stageA(ii):
        xi, nm, s = its[ii]
        raw, cacc = D.raws[ii % 2], D.caccs[ii % 2]
        t0 = s * 2048
        lo = t0 - 2 if s > 0 else t0
        hi = t0 + 2050 if s < nseg - 1 else t0 + 2048
        yield T.op(T.pool, lambda: nc.gpsimd.memset(raw[:, 0:2], 0.0), writes=[(raw, "lo")])
        yield T.op(T.pool, lambda: nc.gpsimd.memset(raw[:, 2050:2052], 0.0), writes=[(raw, "hi")])
        pieces = []
        a_ = lo
        while a_ < hi:
            b_ = min(hi, (a_ // 512 + 1) * 512)
            pieces.append((a_, b_)); a_ = b_
        for pi, (a_, b_) in enumerate(pieces):
            pp = ps[pi % 2]
            n = b_ - a_
            tl = [(hT, ("t", tt)) for tt in range(a_ // 128, (b_ - 1) // 128 + 1)]
            for k in range(8):
                yield T.op(T.pe, lambda: nc.tensor.matmul(pp[:, 0:n], lhsT=D.w[nm][:, k, :], rhs=hT[:, k, a_:b_], start=(k == 0), stop=(k == 7)),
                     reads=[(D.w[nm], None)] + tl, writes=[(pp, None)], inc=(k == 7))
            yield None
            is_halo = (b_ <= t0) or (a_ >= t0 + 2048)
            key = ("lo" if b_ <= t0 else "hi") if is_halo else ("m", pi)
            if is_halo:
                yield T.op(T.dve, lambda: nc.vector.tensor_scalar(out=raw[:, a_ - t0 + 2:b_ - t0 + 2], in0=pp[:, 0:n], scalar1=linkcol[:, 0:1], scalar2=None, op0=ALU.mult),
                     reads=[(pp, None), (linkcol, None)], writes=[(raw, key)])
            else:
                yield T.op(T.act, lambda: nc.scalar.copy(out=raw[:, a_ - t0 + 2:b_ - t0 + 2], in_=pp[:, 0:n]), reads=[(pp, None)], writes=[(raw, key)])
        yield T.op(T.dve, lambda: nc.vector.tensor_scalar(out=cacc[:], in0=raw[:, 0:2048], scalar1=D.cw[:, xi, 0:1], scalar2=None, op0=ALU.mult),
             reads=[(raw, None), (D.cw, None)], writes=[(cacc, None)])
        for i in range(1, 5):
            yield T.op(T.dve, lambda: nc.vector.scalar_tensor_tensor(out=cacc[:], in0=raw[:, i:i + 2048], scalar=D.cw[:, xi, i:i + 1], in1=cacc[:],
                                                               op0=ALU.mult, op1=ALU.add),
                 reads=[(raw, None), (D.cw, None), (cacc, None)], writes=[(cacc, None)])
        yield T.op(T.act, lambda: nc.scalar.activation(out=cacc[:], in_=cacc[:], func=AF.Silu), reads=[(cacc, None)], writes=[(cacc, None)])

    def stageN(ii):
        xi, nm, s = its[ii]
        cacc = D.caccs[ii % 2]
        t0 = s * 2048
        if nm == "v":
            for cc in range(16):
                c = s * 16 + cc
                pv = ps[2 + cc % 2]
                yield T.op(T.pe, lambda: nc.tensor.transpose(out=pv[:, 0:128], in_=cacc[:, cc * 128:(cc + 1) * 128], identity=D.cst["identf"][:]),
                     reads=[(cacc, None), (D.cst["identf"], None)], writes=[(pv, None)])
                yield T.op(T.act, lambda: nc.scalar.copy(out=D.vtm[:, c, :], in_=pv[:, 0:128]), reads=[(pv, None)], writes=[(D.vtm, ("c", c))])
        else:
            dst = D.qT if nm == "q" else D.kT
            scl = (128.0 ** -0.5) if nm == "q" else 1.0
            for bb in range(4):
                sq, rs = D.sq[bb % 2], D.rs[bb % 2]
                pn = ps[2 + bb % 2]
                fs = slice(bb * 512, (bb + 1) * 512)
                yield T.op(T.act, lambda: nc.scalar.activation(out=sq[:], in_=cacc[:, fs], func=AF.Square), reads=[(cacc, None)], writes=[(sq, None)])
                yield T.op(T.pe, lambda: nc.tensor.matmul(pn[:], lhsT=D.cst["ones128"][:], rhs=sq[:], start=True, stop=True),
                     reads=[(D.cst["ones128"], None), (sq, None)], writes=[(pn, None)])
                yield T.op(T.act, lambda: nc.scalar.activation(out=rs[:], in_=pn[:], func=AF.Ln, bias=D.epsc[:, 0:1]), reads=[(pn, None), (D.epsc, None)], writes=[(rs, None)])
                yield T.op(T.act, lambda: nc.scalar.activation(out=rs[:], in_=rs[:], func=AF.Exp, scale=-0.5), reads=[(rs, None)], writes=[(rs, None)])
                yield T.op(T.dve, lambda: nc.vector.scalar_tensor_tensor(out=dst[:, t0 + bb * 512:t0 + (bb + 1) * 512], in0=cacc[:, fs], scalar=scl, in1=rs[:],
                                                                   op0=ALU.mult, op1=ALU.mult),
                     reads=[(cacc, None), (rs, None)], writes=[(dst, ("b", s * 4 + bb))])

    def rr2g(gs):
        gs = [g_ for g_ in gs if g_ is not None]
        while gs:
            for g_ in list(gs):
                try:
                    next(g_)
                    yield None
                except StopIteration:
                    gs.remove(g_)

    def pro_gen():
        load_weights(("q", "k", "v"))
        load_cw()
        yield None
        yield from rr2g([stageA(0)])
        for ii in range(len(its)):
            yield from rr2g([stageA(ii + 1) if ii + 1 < len(its) else None, stageN(ii)])

    if part == "pro":
        return pro_gen()
    if part == "out":
        return out_gen_factory(T, nc, D, hT, Ts, hd, ps, out_cb, load_weights)
    T.op(T.pool, lambda: nc.gpsimd.memset(D.oacc[:, 0:NC, :], 0.0), writes=[(D.oacc, None)])
    for d in range(2):
        T.op(T.pool, lambda: nc.gpsimd.memset(D.S[d][:], 0.0), writes=[(D.S[d], None)])
        T.op(T.pool, lambda: nc.gpsimd.memset(D.Sb[d][:], 0.0), writes=[(D.Sb[d], None)])

    def pre_unit(u, c, d):
        pA = ps[u]
        col = d * 8 + hd
        ch = slice(c * 128, (c + 1) * 128)
        pK = pA.ap.bitcast(BF16)[:, 0:128]
        yield T.op(T.pe, lambda: nc.tensor.transpose(out=pK, in_=D.kT[:, ch], identity=D.identb[:]),
             reads=[(D.kT, ("b", c // 4)), (D.identb, None)], writes=[(pA, None)])
        kd = D.kd[d][c % D.R]
        yield T.op(T.act, lambda: nc.scalar.activation(out=kd[:], in_=pK, func=AF.Copy, scale=D.kdec[:, c, col:col + 1]),
             reads=[(pA, None), (D.kdec, None)], writes=[(kd, None)])
        kq = [(D.kT, ("b", c // 4)), (D.qT, ("b", c // 4))]
        T.op(T.pe, lambda: nc.tensor.matmul(pA[:, 0:128], lhsT=D.kT[:, ch], rhs=D.kT[:, ch], start=True, stop=True), reads=kq, writes=[(pA, None)], inc=False)
        yield T.op(T.pe, lambda: nc.tensor.matmul(pA[:, 128:256], lhsT=D.kT[:, ch], rhs=D.qT[:, ch], start=True, stop=True), reads=kq, writes=[(pA, None)])
        yield T.op(T.act, lambda: nc.scalar.activation(out=D.diag[u][:], in_=D.cst["identf"][:], func=AF.Copy, scale=D.gc[:, c, col:col + 1]),
             reads=[(D.cst["identf"], None), (D.gc, None)], writes=[(D.diag[u], None)])
        yield T.op(T.pe, lambda: nc.tensor.matmul(pA[:, 256:384], lhsT=D.cst["ones128"][:], rhs=D.diag[u][:], start=True, stop=True),
             reads=[(D.cst["ones128"], None), (D.diag[u], None)], writes=[(pA, None)])
        nm_ = D.cst["negmU" if d == 0 else "negmL"]
        st_ = D.cst["strU" if d == 0 else "strL"]
        yield T.op(T.dve, lambda: nc.vector.scalar_tensor_tensor(out=D.dd[u][:], in0=pA[:, 256:384], scalar=D.gc[:, c, col:col + 1], in1=nm_[:], op0=ALU.subtract, op1=ALU.add),
             reads=[(pA, None), (D.gc, None), (nm_, None)], writes=[(D.dd[u], None)])
        yield T.op(T.act, lambda: nc.scalar.activation(out=D.dT[u][:], in_=D.dd[u][:], func=AF.Exp), reads=[(D.dd[u], None)], writes=[(D.dT[u], None)])
        yield T.op(T.dve, lambda: nc.vector.tensor_tensor(out=D.tmp[u][:], in0=pA[:, 0:128], in1=D.dT[u][:], op=ALU.mult),
             reads=[(pA, None), (D.dT[u], None)], writes=[(D.tmp[u], None)])
        NN, T1 = D.NN[u], D.T1[u]
        posbeta = D.beta[:, c, col:col + 1]
        yield T.op(T.dve, lambda: nc.vector.scalar_tensor_tensor(out=NN[:, 0:128], in0=D.tmp[u][:], scalar=posbeta, in1=st_[:], op0=ALU.mult, op1=ALU.mult),
             reads=[(D.tmp[u], None), (D.beta, None), (st_, None)], writes=[(NN, "A")])
        iT = D.iT[d][c % D.R]
        yield T.op(T.dve, lambda: nc.vector.tensor_tensor(out=iT[:], in0=pA[:, 128:256], in1=D.dT[u][:], op=ALU.mult),
             reads=[(pA, None), (D.dT[u], None)], writes=[(iT, None)])
        pNT = pA.ap.bitcast(BF16)[:, 0:128]
        yield T.op(T.pe, lambda: nc.tensor.transpose(out=pNT, in_=NN[:, 0:128], identity=D.identb[:]),
             reads=[(NN, "A"), (D.identb, None)], writes=[(pA, None)])
        yield T.op(T.act, lambda: nc.scalar.copy(out=NN[:, 128:256], in_=pNT), reads=[(pA, None)], writes=[(NN, "T")])
        AAp, UUp = D.AA[u][0], D.UU[u][0]
        yield T.op(T.pool, lambda: nc.gpsimd.tensor_tensor(out=AAp[:], in0=NN[:], in1=D.c2["negbd2"][:], op=ALU.mult),
             reads=[(NN, "A"), (NN, "T"), (D.c2["negbd2"], None)], writes=[(AAp, None)])
        yield T.op(T.pool, lambda: nc.gpsimd.tensor_tensor(out=UUp[:], in0=AAp[:], in1=D.c2["ident2"][:], op=ALU.add),
             reads=[(AAp, None), (D.c2["ident2"], None)], writes=[(UUp, None)])
        def cp(stage, out, in_, reads, writes):
            if stage != 7:
                return T.op(T.act, lambda: nc.scalar.copy(out=out, in_=in_), reads=reads, writes=writes)
            return T.op(T.dve, lambda: nc.vector.tensor_copy(out=out, in_=in_), reads=reads, writes=writes)
        idb = D.identb
        for lvl in range(1, 4):
            AAc, UUc = D.AA[u][lvl % 2], D.UU[u][lvl % 2]
            T.op(T.pe, lambda: nc.tensor.matmul(pA[:, 0:128], lhsT=AAp[:, 128:256], rhs=AAp[:, 0:128], start=True, stop=True), reads=[(AAp, None)], writes=[(pA, None)], inc=False)
            yield T.op(T.pe, lambda: nc.tensor.matmul(pA[:, 128:256], lhsT=AAp[:, 0:128], rhs=AAp[:, 128:256], start=True, stop=True), reads=[(AAp, None)], writes=[(pA, None)])
            yield cp(2 * lvl, AAc[:], pA[:, 0:256], [(pA, None)], [(AAc, None)])
            rd = [(AAc, None), (UUp, None), (idb, None)]
            T.op(T.pe, lambda: nc.tensor.matmul(pA[:, 256:384], lhsT=AAc[:, 128:256], rhs=UUp[:, 0:128], start=True, stop=False), reads=rd, writes=[(pA, None)], inc=False)
            yield T.op(T.pe, lambda: nc.tensor.matmul(pA[:, 256:384], lhsT=idb[:], rhs=UUp[:, 0:128], start=False, stop=True), reads=rd, writes=[(pA, None)])
            yield cp(2 * lvl + 1, UUc[:, 0:128], pA[:, 256:384], [(pA, None)], [(UUc, None)])
            AAp, UUp = AAc, UUc
        pUT = pA.ap.bitcast(BF16)[:, 0:128]
        yield T.op(T.pe, lambda: nc.tensor.transpose(out=pUT, in_=UUp[:, 0:128], identity=idb[:]),
             reads=[(UUp, None), (idb, None)], writes=[(pA, None)])
        yield T.op(T.act, lambda: nc.scalar.copy(out=UUp[:, 128:256], in_=pUT), reads=[(pA, None)], writes=[(UUp, None)])
        for li, msn in enumerate(("ms16", "ms32", "ms64")):
            last = (li == 2)
            UUc = D.UU[u][li % 2]
            if UUc is UUp:
                UUc = D.UU[u][(li + 1) % 2]
            yield T.op(T.pe, lambda: nc.tensor.matmul(pA[:, 0:128], lhsT=NN[:, 128:256], rhs=UUp[:, 0:128], start=True, stop=True),
                 reads=[(NN, "T"), (UUp, None)], writes=[(pA, None)])
            yield T.op(T.dve, lambda: nc.vector.tensor_tensor(out=T1[:], in0=pA[:, 0:128], in1=D.c2[msn][:, 0:128], op=ALU.mult),
                 reads=[(pA, None), (D.c2[msn], None)], writes=[(T1, None)])
            if not last:
                T.op(T.pe, lambda: nc.tensor.matmul(pA[:, 256:384], lhsT=UUp[:, 128:256], rhs=T1[:], start=True, stop=True),
                     reads=[(UUp, None), (T1, None)], writes=[(pA, None)], inc=False)
                yield T.op(T.pe, lambda: nc.tensor.matmul(pA[:, 384:512], lhsT=T1[:], rhs=UUp[:, 128:256], start=True, stop=True),
                     reads=[(UUp, None), (T1, None)], writes=[(pA, None)])
                yield T.op(T.dve, lambda: nc.vector.tensor_tensor(out=UUc[:], in0=UUp[:], in1=pA[:, 256:512], op=ALU.subtract),
                     reads=[(pA, None), (UUp, None)], writes=[(UUc, None)])
                UUp = UUc
            else:
                Uf = D.U[d][c % D.R]
                yield T.op(T.pe, lambda: nc.tensor.matmul(pA[:, 256:384], lhsT=UUp[:, 128:256], rhs=T1[:], start=True, stop=True),
                     reads=[(UUp, None), (T1, None)], writes=[(pA, None)])
                yield T.op(T.dve, lambda: nc.vector.tensor_tensor(out=Uf[:], in0=UUp[:, 0:128], in1=pA[:, 256:384], op=ALU.subtract),
                     reads=[(pA, None), (UUp, None)], writes=[(Uf, None)])

    def scan_step(c, d):
        col = d * 8 + hd
        ch = slice(c * 128, (c + 1) * 128)
        p1 = ps[4 + 2 * d]
        p2 = ps[5 + 2 * d]
        S, Sb = D.S[d], D.Sb[d]
        U, iT, kd = D.U[d][c % D.R], D.iT[d][c % D.R], D.kd[d][c % D.R]
        kq = [(D.kT, ("b", c // 4)), (D.qT, ("b", c // 4))]
        T.op(T.pe, lambda: nc.tensor.matmul(p1[:, 0:128], lhsT=D.kT[:, ch], rhs=Sb[:], start=True, stop=True), reads=kq + [(Sb, None)], writes=[(p1, None)], inc=False)
        yield T.op(T.pe, lambda: nc.tensor.matmul(p1[:, 128:256], lhsT=D.qT[:, ch], rhs=Sb[:], start=True, stop=True), reads=kq + [(Sb, None)], writes=[(p1, None)])
        yield T.op(T.dve, lambda: nc.vector.scalar_tensor_tensor(out=D.Rp[d][:], in0=p1[:, 0:128], scalar=D.negegc[:, c, col:col + 1], in1=D.vtm[:, c, :], op0=ALU.mult, op1=ALU.add),
             reads=[(p1, None), (D.negegc, None), (D.vtm, ("c", c))], writes=[(D.Rp[d], None)])
        yield T.op(T.pe, lambda: nc.tensor.matmul(p1[:, 256:384], lhsT=U[:], rhs=D.Rp[d][:], start=True, stop=True), reads=[(U, None), (D.Rp[d], None)], writes=[(p1, None)])
        yield T.op(T.act, lambda: nc.scalar.activation(out=D.vn[d][:], in_=p1[:, 256:384], func=AF.Copy, scale=D.beta[:, c, col:col + 1]),
             reads=[(p1, None), (D.beta, None)], writes=[(D.vn[d], None)])
        T.op(T.pe, lambda: nc.tensor.matmul(p2[:, 0:128], lhsT=iT[:], rhs=D.vn[d][:], start=True, stop=True), reads=[(iT, None), (D.vn[d], None)], writes=[(p2, None)], inc=False)
        yield T.op(T.pe, lambda: nc.tensor.matmul(p2[:, 128:256], lhsT=kd[:], rhs=D.vn[d][:], start=True, stop=True), reads=[(kd, None), (D.vn[d], None)], writes=[(p2, None)])
        yield T.op(T.dve, lambda: nc.vector.scalar_tensor_tensor(out=S[:], in0=S[:], scalar=D.egtot[:, c, col:col + 1], in1=p2[:, 128:256], op0=ALU.mult, op1=ALU.add),
             reads=[(S, None), (D.egtot, None), (p2, None)], writes=[(S, None)])
        seg_edge = (nseg == 2) and ((d == 0 and c == 15) or (d == 1 and c == 16))
        if seg_edge:
            yield T.op(T.dve, lambda: nc.vector.tensor_scalar(out=S[:], in0=S[:], scalar1=linkcol[:, 0:1], scalar2=None, op0=ALU.mult),
                 reads=[(S, None), (linkcol, None)], writes=[(S, None)])
        yield T.op(T.act, lambda: nc.scalar.copy(out=Sb[:], in_=S[:]), reads=[(S, None)], writes=[(Sb, None)])
        yield T.op(T.dve, lambda: nc.vector.scalar_tensor_tensor(out=D.ot[d][:], in0=p1[:, 128:256], scalar=D.egc[:, c, col:col + 1], in1=D.oacc[:, c, :], op0=ALU.mult, op1=ALU.add),
             reads=[(p1, None), (D.egc, None), (D.oacc, ("c", c))], writes=[(D.ot[d], None)])
        yield T.op(T.dve, lambda: nc.vector.tensor_tensor(out=D.oacc[:, c, :], in0=p2[:, 0:128], in1=D.ot[d][:], op=ALU.add),
             reads=[(p2, None), (D.ot[d], None)], writes=[(D.oacc, ("c", c))])


    orders = [list(range(NC)), list(range(NC - 1, -1, -1))]
    pre_done, scan_done = set(), set()

    def scan_chain(d):
        for c in orders[d]:
            while (c, d) not in pre_done:
                yield "blocked"
            yield from scan_step(c, d)
            scan_done.add((c, d))

    free_banks = [0, 1, 2, 3]

    def pre_wrap(u, c, d):
        yield from pre_unit(u, c, d)
        pre_done.add((c, d))
        free_banks.append(u)

    pq = []
    for i in range(NC):
        pq.append((orders[0][i], 0, i))
        pq.append((orders[1][i], 1, i))
    gens = [scan_chain(0), scan_chain(1)]
    pqi = 0
    idle = 0
    rnd = 0
    last_start = -100
    while gens:
        rnd += 1
        if pqi < len(pq) and free_banks and rnd - last_start >= START_GAP:
            c, d, i = pq[pqi]
            if not (i >= D.R and (orders[d][i - D.R], d) not in scan_done):
                u = free_banks.pop(0)
                gens.append(pre_wrap(u, c, d))
                pqi += 1
                last_start = rnd
        progressed = False
        for g_ in list(gens):
            try:
                r = next(g_)
                if r != "blocked":
                    progressed = True
            except StopIteration:
                gens.remove(g_)
                progressed = True
        idle = 0 if progressed else idle + 1
        assert idle < 4, "DN scheduler stuck"
    return None


def out_gen_factory(T, nc, D, hT, Ts, hd, ps, out_cb, load_weights):
    NC = Ts // 128

    def gen():
        load_weights(("z",))
        yield None
        for c in range(NC):
            yield T.op(T.act, lambda: nc.scalar.activation(out=D.tmp[0][:], in_=D.oacc[:, c, :], func=AF.Square, accum_out=D.nss[:, c:c + 1]),
                 reads=[(D.oacc, ("c", c))], writes=[(D.tmp[0], None), (D.nss, ("c", c))])
        yield T.op(T.dve, lambda: nc.vector.tensor_scalar(out=D.nrs[:, 0:NC], in0=D.nss[:, 0:NC], scalar1=1.0 / 128, scalar2=1e-6, op0=ALU.mult, op1=ALU.add),
             reads=[(D.nss, None)], writes=[(D.nrs, None)])
        yield T.op(T.act, lambda: nc.scalar.activation(out=D.nrs[:, 0:NC], in_=D.nrs[:, 0:NC], func=AF.Sqrt), reads=[(D.nrs, None)], writes=[(D.nrs, None)])
        yield T.op(T.dve, lambda: nc.vector.reciprocal(out=D.nrs[:, 0:NC], in_=D.nrs[:, 0:NC]), reads=[(D.nrs, None)], writes=[(D.nrs, None)])
        for c in range(NC):
            pz = ps[4 + c % 2]
            for k in range(8):
                T.op(T.pe, lambda: nc.tensor.matmul(pz[:, 0:128], lhsT=hT[:, k, c * 128:(c + 1) * 128], rhs=D.w["z"][:, k, :], start=(k == 0), stop=(k == 7)),
                     reads=[(hT, ("t", c)), (D.w["z"], None)], writes=[(pz, None)], inc=(k == 7))
            yield None
            zs = D.zs[c % 2]
            yield T.op(T.act, lambda: nc.scalar.activation(out=zs[:], in_=pz[:, 0:128], func=AF.Silu), reads=[(pz, None)], writes=[(zs, None)])
            yield T.op(T.pool, lambda: nc.gpsimd.tensor_tensor(out=zs[:], in0=zs[:], in1=D.gob[:], op=ALU.mult), reads=[(zs, None), (D.gob, None)], writes=[(zs, None)])
            on = D.on[c % 2]
            yield T.op(T.dve, lambda: nc.vector.scalar_tensor_tensor(out=on[:], in0=D.oacc[:, c, :], scalar=D.nrs[:, c:c + 1], in1=zs[:], op0=ALU.mult, op1=ALU.mult),
                 reads=[(D.oacc, ("c", c)), (D.nrs, None), (zs, None)], writes=[(on, None)])
            b4 = c // 4
            pT = ps[6 + b4 % 2]
            pTv = pT.ap.bitcast(BF16)
            onT = D.onT[b4 % 2]
            yield T.op(T.pe, lambda: nc.tensor.transpose(out=pTv[:, (c % 4) * 128:(c % 4 + 1) * 128], in_=on[:], identity=D.identb[:]),
                 reads=[(on, None), (D.identb, None)], writes=[(pT, None)])
            if c % 4 == 3:
                yield T.op(T.act, lambda: nc.scalar.copy(out=onT[:], in_=pTv[:, 0:512]), reads=[(pT, None)], writes=[(onT, None)])
                out_cb(hd, b4, onT)

    return gen()


class P3Bufs:
    def __init__(self, T):
        sb = T.sbuf
        self.stg = [sb(f"p3stg{i}", [128, 8, 128], F32) for i in range(3)]
        self.wg = sb("p3wg", [128, 8, 2048], BF16)
        self.wa = sb("p3wa", [128, 4, 1024], BF16)
        self.wb = sb("p3wb", [128, 8, 1024], BF16)
        self.wo = sb("p3wo", [128, 8, 1024], BF16)
        self.attb = sb("p3attb", [128, 4, 512], BF16)
        self.dnb = sb("p3dnb", [128, 8, 512], BF16)
        self.ga = [sb(f"p3ga{i}", [128, 512], F32) for i in range(2)]
        self.gb = [sb(f"p3gb{i}", [128, 512], F32) for i in range(2)]
        self.m1 = [sb(f"p3m1{i}", [128, 512], F32) for i in range(1)]
        self.m2 = [sb(f"p3m2{i}", [128, 512], F32) for i in range(1)]
        self.mix = sb("p3mix", [128, 8, 512], BF16)
        self.xt = [sb(f"p3xt{i}", [128, 1024], F32) for i in range(1)]
        self.x1 = [sb(f"p3x1{i}", [128, 1024], F32) for i in range(2)]
        self.xn = [sb(f"p3xn{i}", [128, 1024], BF16) for i in range(2)]
        self.ssq = sb("p3ssq", [128, 16], F32)
        self.gB = sb("p3gB", [128, 8, 128], F32)
        self.gcol = sb("p3gcol", [128, 8], F32)


def load_w_big(T, nc, w_d, nk, ncols, stg, dst, col0=0, dcol0=0):
    i = 0
    W = 128
    for k0 in range(0, nk, 8):
        kn = min(8, nk - k0)
        for c in range(0, ncols, W):
            n = min(W, ncols - c)
            st = stg[i % len(stg)]; i += 1
            T.op(T.sp, lambda: nc.sync.dma_start(out=st[:, 0:kn, 0:n], in_=w_d[k0 * 128:(k0 + kn) * 128, col0 + c:col0 + c + n].rearrange("(k p) c -> p k c", p=128)),
                 writes=[(st, None)])
            oo, ii_ = dst[:, k0:k0 + kn, dcol0 + c:dcol0 + c + n], st[:, 0:kn, 0:n]
            if i % 3 == 0:
                T.op(T.pool, lambda: nc.gpsimd.tensor_copy(out=oo, in_=ii_), reads=[(st, None)], writes=[(dst, ("c", i))])
            elif i % 3 == 1:
                T.op(T.act, lambda: nc.scalar.copy(out=oo, in_=ii_), reads=[(st, None)], writes=[(dst, ("c", i))])
            else:
                T.op(T.dve, lambda: nc.vector.tensor_copy(out=oo, in_=ii_), reads=[(st, None)], writes=[(dst, ("c", i))])


def rms_to_T(T, nc, src, B, gB, ident, pt, hT, tile_idx):
    i = tile_idx
    xn = B.xn[i % 2]
    o = 4 * (i % 4)
    T.op(T.act, lambda: nc.scalar.activation(out=xn[:], in_=src[:], func=AF.Square, accum_out=B.ssq[:, o + 0:o + 1]),
         reads=[(src, None)], writes=[(xn, None), (B.ssq, o + 0)])
    T.op(T.dve, lambda: nc.vector.tensor_scalar(out=B.ssq[:, o + 1:o + 2], in0=B.ssq[:, o + 0:o + 1], scalar1=1.0 / 1024, scalar2=1e-6, op0=ALU.mult, op1=ALU.add),
         reads=[(B.ssq, o + 0)], writes=[(B.ssq, o + 1)])
    T.op(T.act, lambda: nc.scalar.activation(out=B.ssq[:, o + 2:o + 3], in_=B.ssq[:, o + 1:o + 2], func=AF.Sqrt), reads=[(B.ssq, o + 1)], writes=[(B.ssq, o + 2)])
    T.op(T.dve, lambda: nc.vector.reciprocal(out=B.ssq[:, o + 3:o + 4], in_=B.ssq[:, o + 2:o + 3]), reads=[(B.ssq, o + 2)], writes=[(B.ssq, o + 3)])
    T.op(T.dve, lambda: nc.vector.tensor_scalar(out=xn[:], in0=src[:], scalar1=B.ssq[:, o + 3:o + 4], scalar2=None, op0=ALU.mult),
         reads=[(src, None), (B.ssq, o + 3)], writes=[(xn, None)])
    ptv = pt.ap.bitcast(BF16)
    for k in range(8):
        T.op(T.pe, lambda: nc.tensor.transpose(out=ptv[:, k * 128:(k + 1) * 128], in_=xn[:, k * 128:(k + 1) * 128], identity=ident[:]),
             reads=[(xn, None), (ident, None)], writes=[(pt, None)], inc=(k == 7))
    T.op(T.dve, lambda: nc.vector.tensor_tensor(out=hT[:, :, 128 * i:128 * (i + 1)], in0=ptv.rearrange("p (k t) -> p k t", k=8), in1=gB[:], op=ALU.mult),
         reads=[(pt, None), (gB, None)], writes=[(hT, ("t", i))])


def load_gB(T, nc, g_d, gcol, gB):
    with nc.allow_non_contiguous_dma(reason="small"):
        T.op(T.sp, lambda: nc.sync.dma_start(out=gcol[:], in_=g_d.rearrange("(k p) -> p k", p=128)), writes=[(gcol, None)])
    T.op(T.dve, lambda: nc.vector.tensor_copy(out=gB[:], in_=gcol[:].unsqueeze(2).to_broadcast([128, 8, 128])), reads=[(gcol, None)], writes=[(gB, None)])


def phase3(T, nc, B, hT, Ts, tok0, x_d, y_d, yb, w_in, w_a, w_b, w_o, g_ffn, att_sc, attb_buf, dn_sc, dnb_buf, ident, ps, side=None):
    load_w_big(T, nc, w_in, 8, 2048, B.stg, B.wg, col0=8736)
    load_w_big(T, nc, w_a, 4, 1024, B.stg, B.wa)
    load_w_big(T, nc, w_b, 8, 1024, B.stg, B.wb)
    load_w_big(T, nc, w_o, 8, 1024, B.stg, B.wo)
    load_gB(T, nc, g_ffn, B.gcol, B.gB)
    nblk = Ts // 512
    for b in range(nblk):
        bs = slice(512 * b, 512 * (b + 1))
        T.op(T.sp, lambda: nc.sync.dma_start(out=B.attb[:], in_=att_sc[:, :, bs].rearrange("k p t -> p k t")),
             reads=[(attb_buf, (k, b)) for k in range(4)], writes=[(B.attb, None)])
        T.op(T.sp, lambda: nc.sync.dma_start(out=B.dnb[:], in_=dn_sc[:, :, bs].rearrange("k p t -> p k t")),
             reads=[(dnb_buf, (k, b)) for k in range(8)], writes=[(B.dnb, None)])
        hk = [(hT, ("t", 4 * b + j)) for j in range(4)]
        for oc in range(8):
            pya, pyb, pga, pgb = ps[0 + 4 * (oc % 2)], ps[1 + 4 * (oc % 2)], ps[2 + 4 * (oc % 2)], ps[3 + 4 * (oc % 2)]
            os_ = slice(oc * 128, (oc + 1) * 128)
            for k in range(4):
                T.op(T.pe, lambda: nc.tensor.matmul(pya[:], lhsT=B.wa[:, k, os_], rhs=B.attb[:, k, :], start=(k == 0), stop=(k == 3)),
                     reads=[(B.wa, None), (B.attb, None)], writes=[(pya, None)], inc=(k == 3))
            for k in range(8):
                T.op(T.pe, lambda: nc.tensor.matmul(pyb[:], lhsT=B.wb[:, k, os_], rhs=B.dnb[:, k, :], start=(k == 0), stop=(k == 7)),
                     reads=[(B.wb, None), (B.dnb, None)], writes=[(pyb, None)], inc=(k == 7))
            for k in range(8):
                T.op(T.pe, lambda: nc.tensor.matmul(pga[:], lhsT=B.wg[:, k, os_], rhs=hT[:, k, bs], start=(k == 0), stop=(k == 7)),
                     reads=[(B.wg, None)] + hk, writes=[(pga, None)], inc=(k == 7))
            for k in range(8):
                T.op(T.pe, lambda: nc.tensor.matmul(pgb[:], lhsT=B.wg[:, k, 1024 + oc * 128:1024 + (oc + 1) * 128], rhs=hT[:, k, bs], start=(k == 0), stop=(k == 7)),
                     reads=[(B.wg, None)] + hk, writes=[(pgb, None)], inc=(k == 7))
            ga, gb, m1, m2 = B.ga[oc % 2], B.gb[oc % 2], B.m1[0], B.m2[0]
            T.op(T.act, lambda: nc.scalar.activation(out=ga[:], in_=pga[:], func=AF.Sigmoid), reads=[(pga, None)], writes=[(ga, None)])
            T.op(T.act, lambda: nc.scalar.activation(out=gb[:], in_=pgb[:], func=AF.Sigmoid), reads=[(pgb, None)], writes=[(gb, None)])
            T.op(T.dve, lambda: nc.vector.tensor_tensor(out=m1[:], in0=pya[:], in1=ga[:], op=ALU.mult), reads=[(pya, None), (ga, None)], writes=[(m1, None)])
            T.op(T.dve, lambda: nc.vector.tensor_tensor(out=m2[:], in0=pyb[:], in1=gb[:], op=ALU.mult), reads=[(pyb, None), (gb, None)], writes=[(m2, None)])
            T.op(T.pool, lambda: nc.gpsimd.tensor_tensor(out=B.mix[:, oc, :], in0=m1[:], in1=m2[:], op=ALU.add),
                 reads=[(m1, None), (m2, None)], writes=[(B.mix, ("o", oc))])
            if side is not None:
                try:
                    next(side)
                except StopIteration:
                    side = None
        for tt in range(4):
            ti = 4 * b + tt
            xt, x1 = B.xt[0], B.x1[tt % 2]
            r0 = tok0 + 128 * ti
            T.op(T.sp, lambda: nc.sync.dma_start(out=xt[:], in_=x_d[r0:r0 + 128, :]), writes=[(xt, None)])
            for half in range(2):
                po = ps[half + 4 * (tt % 2)]
                for k in range(8):
                    T.op(T.pe, lambda: nc.tensor.matmul(po[:], lhsT=B.mix[:, k, tt * 128:(tt + 1) * 128], rhs=B.wo[:, k, half * 512:(half + 1) * 512], start=(k == 0), stop=(k == 7)),
                         reads=[(B.mix, None), (B.wo, None)], writes=[(po, None)], inc=(k == 7))
                T.op(T.dve, lambda: nc.vector.tensor_tensor(out=x1[:, half * 512:(half + 1) * 512], in0=po[:], in1=xt[:, half * 512:(half + 1) * 512], op=ALU.add),
                     reads=[(po, None), (xt, None)], writes=[(x1, None)])
            T.op(T.sp, lambda: nc.sync.dma_start(out=y_d[r0:r0 + 128, :], in_=x1[:]), reads=[(x1, None)], writes=[(yb, ("r", r0 // 128))])
            rms_to_T(T, nc, x1, B, B.gB, ident, ps[2 + 4 * (tt % 2)], hT, ti)


    if side is not None:
        for _ in side:
            pass


class P4Bufs:
    def __init__(self, T):
        sb = T.sbuf
        self.wd = sb("p4wd", [128, 22, 1024], BF16)
        self.stg = [sb(f"p4stg{i}", [128, 8, 128], F32) for i in range(3)]
        self.wgc = [sb(f"p4wg{i}", [128, 8, 128], BF16) for i in range(4)]
        self.wvc = [sb(f"p4wv{i}", [128, 8, 128], BF16) for i in range(4)]
        self.graw = [sb(f"p4graw{i}", [128, 514], F32) for i in range(2)]
        self.gprev = sb("p4gprev", [128, 22, 2], F32)
        self.cw = sb("p4cw", [128, 22, 4], F32)
        self.gcv = [sb(f"p4gcv{i}", [128, 512], F32) for i in range(2)]
        self.act = sb("p4act", [128, 22, 512], BF16)
        self.x1 = [sb(f"p4x1{i}", [128, 1024], F32) for i in range(2)]
        self.yo = [sb(f"p4yo{i}", [128, 1024], F32) for i in range(2)]
        self.ssq = sb("p4ssq", [128, 8], F32)
        self.gfin = sb("p4gfin", [128, 1024], F32)


def phase4(T, nc, B, hT, Ts, tok0, y_d, yb, wup_bf, wupb_buf, w_d, fcw, fcb, g_fin, linkcol, ps):
    load_w_big(T, nc, w_d, 22, 1024, B.stg, B.wd)
    with nc.allow_non_contiguous_dma(reason="small"):
        for kk in range(3):
            T.op(T.sp, lambda: nc.sync.dma_start(out=B.cw[:, :, kk:kk + 1], in_=fcw[kk:kk + 1, :].rearrange("o (c p) -> p c o", p=128)), writes=[(B.cw, None)])
        T.op(T.sp, lambda: nc.sync.dma_start(out=B.cw[:, :, 3:4], in_=fcb.rearrange("(c p o) -> p c o", p=128, o=1)), writes=[(B.cw, None)])
    T.op(T.sp, lambda: nc.sync.dma_start(out=B.gfin[:], in_=g_fin.partition_broadcast(128)), writes=[(B.gfin, None)])
    nblk = Ts // 512
    wi = 0
    for b in range(nblk):
        bs = slice(512 * b, 512 * (b + 1))
        hk = [(hT, ("t", 4 * b + j)) for j in range(4)]
        t_start, t_end = 512 * b, 512 * (b + 1)
        pf = 0 if t_start == 0 else ("L" if t_start % 2048 == 0 else 1)
        nf = 0 if t_end == Ts else ("L" if t_end % 2048 == 0 else 1)
        for c in range(22):
            wg, wv = B.wgc[wi % 4], B.wvc[wi % 4]; wi += 1
            T.op(T.sp, lambda: nc.sync.dma_start(out=wg[:], in_=wup_bf[:, c * 128:(c + 1) * 128].rearrange("(k p) c -> p k c", p=128)),
                 reads=[(wupb_buf, None)], writes=[(wg, None)])
            T.op(T.sp, lambda: nc.sync.dma_start(out=wv[:], in_=wup_bf[:, 2816 + c * 128:2816 + (c + 1) * 128].rearrange("(k p) c -> p k c", p=128)),
                 reads=[(wupb_buf, None)], writes=[(wv, None)])
            pg, pv, pn = ps[0 + 3 * (c % 2)], ps[1 + 3 * (c % 2)], ps[2 + 3 * (c % 2)]
            gr = B.graw[c % 2]
            nw = 512 if nf != 0 else 511
            hkw = hk + ([(hT, ("t", 4 * b + 4))] if nf != 0 else [])
            for k in range(8):
                T.op(T.pe, lambda: nc.tensor.matmul(pg[:, 0:nw], lhsT=wg[:, k, :], rhs=hT[:, k, t_start + 1:t_start + 1 + nw], start=(k == 0), stop=(k == 7)),
                     reads=[(wg, None)] + hkw, writes=[(pg, None)], inc=(k == 7))
            for k in range(8):
                T.op(T.pe, lambda: nc.tensor.matmul(pv[:], lhsT=wv[:, k, :], rhs=hT[:, k, bs], start=(k == 0), stop=(k == 7)),
                     reads=[(wv, None)] + hk, writes=[(pv, None)], inc=(k == 7))
            if pf == 0:
                for k in range(8):
                    T.op(T.pe, lambda: nc.tensor.matmul(pn[:, 0:1], lhsT=wg[:, k, :], rhs=hT[:, k, 0:1], start=(k == 0), stop=(k == 7)),
                         reads=[(wg, None), (hT, ("t", 0))], writes=[(pn, None)], inc=(k == 7))
                T.op(T.pool, lambda: nc.gpsimd.memset(gr[:, 0:1], 0.0), writes=[(gr, "p")])
                T.op(T.act, lambda: nc.scalar.copy(out=gr[:, 1:2], in_=pn[:, 0:1]), reads=[(pn, None)], writes=[(gr, "q")])
            else:
                if pf == 1:
                    T.op(T.pool, lambda: nc.gpsimd.tensor_copy(out=gr[:, 0:1], in_=B.gprev[:, c, 0:1]), reads=[(B.gprev, c)], writes=[(gr, "p")])
                else:
                    T.op(T.pool, lambda: nc.gpsimd.tensor_tensor(out=gr[:, 0:1], in0=B.gprev[:, c, 0:1], in1=linkcol[:, 0:1], op=ALU.mult),
                         reads=[(B.gprev, c), (linkcol, None)], writes=[(gr, "p")])
                T.op(T.pool, lambda: nc.gpsimd.tensor_copy(out=gr[:, 1:2], in_=B.gprev[:, c, 1:2]), reads=[(B.gprev, c)], writes=[(gr, "q")])
            T.op(T.act, lambda: nc.scalar.copy(out=gr[:, 2:2 + nw], in_=pg[:, 0:nw]), reads=[(pg, None)], writes=[(gr, "m")])
            if nf == 0:
                T.op(T.pool, lambda: nc.gpsimd.memset(gr[:, 513:514], 0.0), writes=[(gr, "n")])
            else:
                T.op(T.pool, lambda: nc.gpsimd.tensor_copy(out=B.gprev[:, c, :], in_=gr[:, 512:514]), reads=[(gr, "m")], writes=[(B.gprev, c)])
                if nf == "L":
                    T.op(T.pool, lambda: nc.gpsimd.tensor_tensor(out=gr[:, 513:514], in0=gr[:, 513:514], in1=linkcol[:, 0:1], op=ALU.mult),
                         reads=[(gr, "m"), (linkcol, None)], writes=[(gr, "m")])
            gcv = B.gcv[c % 2]
            T.op(T.dve, lambda: nc.vector.tensor_scalar(out=gcv[:], in0=gr[:, 0:512], scalar1=B.cw[:, c, 0:1], scalar2=B.cw[:, c, 3:4], op0=ALU.mult, op1=ALU.add),
                 reads=[(gr, None), (B.cw, None)], writes=[(gcv, None)])
            T.op(T.dve, lambda: nc.vector.scalar_tensor_tensor(out=gcv[:], in0=gr[:, 1:513], scalar=B.cw[:, c, 1:2], in1=gcv[:], op0=ALU.mult, op1=ALU.add),
                 reads=[(gr, None), (B.cw, None), (gcv, None)], writes=[(gcv, None)])
            T.op(T.dve, lambda: nc.vector.scalar_tensor_tensor(out=gcv[:], in0=gr[:, 2:514], scalar=B.cw[:, c, 2:3], in1=gcv[:], op0=ALU.mult, op1=ALU.add),
                 reads=[(gr, None), (B.cw, None), (gcv, None)], writes=[(gcv, None)])
            T.op(T.act, lambda: nc.scalar.activation(out=gcv[:], in_=gcv[:], func=AF.Gelu), reads=[(gcv, None)], writes=[(gcv, None)])
            T.op(T.dve, lambda: nc.vector.tensor_tensor(out=B.act[:, c, :], in0=pv[:], in1=gcv[:], op=ALU.mult),
                 reads=[(pv, None), (gcv, None)], writes=[(B.act, ("c", c))])
        for tt in range(4):
            ti = 4 * b + tt
            r0 = tok0 + 128 * ti
            x1, yo = B.x1[tt % 2], B.yo[tt % 2]
            o = 4 * (tt % 2)
            x2 = x1
            T.op(T.sp, lambda: nc.sync.dma_start(out=x1[:], in_=y_d[r0:r0 + 128, :]), reads=[(yb, ("r", r0 // 128))], writes=[(x1, None)])
            for half in range(2):
                po = ps[6 + half]
                for c in range(22):
                    T.op(T.pe, lambda: nc.tensor.matmul(po[:], lhsT=B.act[:, c, tt * 128:(tt + 1) * 128], rhs=B.wd[:, c, half * 512:(half + 1) * 512], start=(c == 0), stop=(c == 21)),
                         reads=[(B.act, None), (B.wd, None)], writes=[(po, None)], inc=(c == 21))
                T.op(T.dve, lambda: nc.vector.tensor_tensor(out=x2[:, half * 512:(half + 1) * 512], in0=po[:], in1=x1[:, half * 512:(half + 1) * 512], op=ALU.add),
                     reads=[(po, None), (x1, None)], writes=[(x1, None)])
            T.op(T.act, lambda: nc.scalar.activation(out=yo[:], in_=x2[:], func=AF.Square, accum_out=B.ssq[:, o + 0:o + 1]),
                 reads=[(x2, None)], writes=[(yo, None), (B.ssq, o + 0)])
            T.op(T.dve, lambda: nc.vector.tensor_scalar(out=B.ssq[:, o + 1:o + 2], in0=B.ssq[:, o + 0:o + 1], scalar1=1.0 / 1024, scalar2=1e-6, op0=ALU.mult, op1=ALU.add),
                 reads=[(B.ssq, o + 0)], writes=[(B.ssq, o + 1)])
            T.op(T.act, lambda: nc.scalar.activation(out=B.ssq[:, o + 2:o + 3], in_=B.ssq[:, o + 1:o + 2], func=AF.Sqrt), reads=[(B.ssq, o + 1)], writes=[(B.ssq, o + 2)])
            T.op(T.dve, lambda: nc.vector.reciprocal(out=B.ssq[:, o + 3:o + 4], in_=B.ssq[:, o + 2:o + 3]), reads=[(B.ssq, o + 2)], writes=[(B.ssq, o + 3)])
            T.op(T.dve, lambda: nc.vector.scalar_tensor_tensor(out=yo[:], in0=x2[:], scalar=B.ssq[:, o + 3:o + 4], in1=B.gfin[:], op0=ALU.mult, op1=ALU.mult),
                 reads=[(x2, None), (B.ssq, o + 3), (B.gfin, None)], writes=[(yo, None)])
            T.op(T.sp, lambda: nc.sync.dma_start(out=y_d[r0:r0 + 128, :], in_=yo[:]), reads=[(yo, None)], writes=[(yb, ("r", r0 // 128))])


def cast_wup_gen(T, nc, w_up, wup_bf, wupb_buf, stg, cst):
    i = 0
    for c in range(0, 5632, 128):
        st, cs = stg[i % len(stg)], cst[i % len(cst)]; i += 1
        T.op(T.sp, lambda: nc.sync.dma_start(out=st[:], in_=w_up[:, c:c + 128].rearrange("(k p) c -> p k c", p=128)), writes=[(st, None)])
        if i % 3 == 0:
            T.op(T.pool, lambda: nc.gpsimd.tensor_copy(out=cs[:], in_=st[:]), reads=[(st, None)], writes=[(cs, None)])
        elif i % 3 == 1:
            T.op(T.act, lambda: nc.scalar.copy(out=cs[:], in_=st[:]), reads=[(st, None)], writes=[(cs, None)])
        else:
            T.op(T.dve, lambda: nc.vector.tensor_copy(out=cs[:], in_=st[:]), reads=[(st, None)], writes=[(cs, None)])
        T.op(T.sp, lambda: nc.sync.dma_start(out=wup_bf[:, c:c + 128].rearrange("(k p) c -> p k c", p=128), in_=cs[:]), reads=[(cs, None)], writes=[(wupb_buf, ("c", c))])
        yield None


TOK = 6144
STREAMS = ((0, 4096), (4096, 2048))
W_NAMES = ("norm_mix_g", "w_in", "conv_qkv_w", "a_log_f", "a_log_b", "dt_bias_f", "dt_bias_b", "out_norm_g", "w_branch_a", "w_branch_b",
           "w_out", "norm_ffn_g", "w_up", "ffn_conv_w", "ffn_conv_b", "w_down", "norm_final_g")
W_SHAPES = {"norm_mix_g": [1024], "w_in": [1024, 10784], "conv_qkv_w": [5, 3072], "a_log_f": [8], "a_log_b": [8], "dt_bias_f": [8], "dt_bias_b": [8],
            "out_norm_g": [128], "w_branch_a": [512, 1024], "w_branch_b": [1024, 1024], "w_out": [1024, 1024], "norm_ffn_g": [1024],
            "w_up": [1024, 5632], "ffn_conv_w": [3, 2816], "ffn_conv_b": [2816], "w_down": [2816, 1024], "norm_final_g": [1024]}


def host_consts():
    c = {}
    c.update(att_consts_host())
    c.update(dn_consts_host())
    c["idn"] = np.eye(128, dtype=np.float32)
    return c


def build_program(streams=STREAMS, tok=TOK, dbg=False):
    nc = bass.Bass("TRN2", target_bir_lowering=False)

    def din(name, shape, dt=F32):
        return nc.dram_tensor(name, list(shape), dt, kind="ExternalInput").ap()
    x = din("x", [tok, 1024])
    linkd = din("link", [128, 1])
    W = {n: din(n, W_SHAPES[n]) for n in W_NAMES}
    HC = host_consts()
    C = {n: din(n, list(v.shape)) for n, v in HC.items()}
    y = nc.dram_tensor("y", [tok, 1024], F32, kind="ExternalOutput").ap()
    Tmax = max(t for _, t in streams)
    kw = dict(kind="ExternalOutput") if dbg else {}
    att_sc = nc.dram_tensor("att_sc", [4, 128, Tmax], BF16, **kw).ap()
    dn_sc = nc.dram_tensor("dn_sc", [8, 128, Tmax], BF16, **kw).ap()
    wup_bf = nc.dram_tensor("wup_bf", [1024, 5632], BF16).ap()

    def dbuf(ap, name):
        b = Buf(ap.tensor, name); b.ap = ap
        return b
    yb, attb_buf, dnb_buf, wupb_buf = dbuf(y, "y"), dbuf(att_sc, "att_sc"), dbuf(dn_sc, "dn_sc"), dbuf(wup_bf, "wup_bf")
    with ExitStack() as es:
        T = Trk(nc, es)
        hT = T.sbuf("hT", [128, 8, Tmax], BF16)
        identf = T.sbuf("identf", [128, 128], F32)
        ident = T.sbuf("ident", [128, 128], BF16)
        linkcol = T.sbuf("linkcol", [128, 1], F32)
        ps = [T.psum(f"ps{i}", [128, 512], F32) for i in range(8)]
        T.op(T.sp, lambda: nc.sync.dma_start(out=identf[:], in_=C["idn"][:, :]), writes=[(identf, None)])
        T.op(T.sp, lambda: nc.sync.dma_start(out=linkcol[:], in_=linkd[:, :]), writes=[(linkcol, None)])
        T.op(T.dve, lambda: nc.vector.tensor_copy(out=ident[:], in_=identf[:]), reads=[(identf, None)], writes=[(ident, None)])
        for (tok0, Ts) in streams:
            with ExitStack() as e1:
                T.es = e1
                gcol = T.sbuf("gcol", [128, 8], F32); gB = T.sbuf("gB", [128, 8, 128], F32)
                xbufs = [T.sbuf(f"xb{i}", [128, 1024], F32) for i in range(4)]
                xnbufs = [T.sbuf(f"xn{i}", [128, 1024], BF16) for i in range(2)]
                junk = T.sbuf("junk", [128, 1024], BF16); ssq = T.sbuf("ssq", [128, 16], F32)
                load_gB(T, nc, W["norm_mix_g"], gcol, gB)
                pst = []
                for i in (6, 7):
                    b = Buf(ps[i].t, f"pst{i}"); b.ap = ps[i].ap.bitcast(BF16); b.st = ps[i].st
                    pst.append(b)
                phase1(T, nc, x, tok0, Ts, hT, gB, ident, pst, xbufs, xnbufs, ssq, junk)
                T.barrier()
            with ExitStack() as e2:
                T.es = e2
                A = AttBufs(T, Ts)
                att_setup(T, nc, A, C["amask"], C["onesbd"], linkcol)

                def cb_a(hp, b, ao):
                    T.op(T.sp, lambda: nc.sync.dma_start(out=att_sc[hp, :, 512 * b:512 * (b + 1)], in_=ao[:]), reads=[(ao, None)], writes=[(attb_buf, (hp, b))])
                attention_phase(T, nc, A, hT, Ts, W["w_in"], C["ropec"], C["ropes"], ps, cb_a, ident)
                T.barrier()
            with ExitStack() as e3:
                T.es = e3
                D = DnBufs(T, Ts)
                dn_setup(T, nc, D, C, C["idn"], W["dt_bias_f"], W["dt_bias_b"], W["a_log_f"], W["a_log_b"], W["out_norm_g"])
                dn_scalars(T, nc, D, hT, Ts, W["w_in"], ps)

                def cb_d(hd, b, onT):
                    T.op(T.sp, lambda: nc.sync.dma_start(out=dn_sc[hd, :, 512 * b:512 * (b + 1)], in_=onT[:]), reads=[(onT, None)], writes=[(dnb_buf, (hd, b))])
                def dnh(hd, part):
                    return dn_head(T, nc, D, hT, Ts, hd, W["w_in"], W["conv_qkv_w"], linkcol, ps, cb_d, part)
                for _ in dnh(0, "pro"):
                    pass
                for hd in range(8):
                    dnh(hd, "scan")
                    gl = [dnh(hd, "out")] + ([dnh(hd + 1, "pro")] if hd < 7 else [])
                    while gl:
                        for g_ in list(gl):
                            try:
                                next(g_)
                            except StopIteration:
                                gl.remove(g_)
                T.barrier()
            with ExitStack() as e4:
                T.es = e4
                B3 = P3Bufs(T)
                side = None
                if tok0 == streams[0][0]:
                    stg = [T.sbuf("pstg0", [128, 8, 128], F32)]
                    cst = [T.sbuf("pcst0", [128, 8, 128], BF16)]
                    side = cast_wup_gen(T, nc, W["w_up"], wup_bf, wupb_buf, stg, cst)
                phase3(T, nc, B3, hT, Ts, tok0, x, y, yb, W["w_in"], W["w_branch_a"], W["w_branch_b"], W["w_out"], W["norm_ffn_g"],
                       att_sc, attb_buf, dn_sc, dnb_buf, ident, ps, side)
                T.barrier()
            with ExitStack() as e5:
                T.es = e5
                B4 = P4Bufs(T)
                phase4(T, nc, B4, hT, Ts, tok0, y, yb, wup_bf, wupb_buf, W["w_down"], W["ffn_conv_w"], W["ffn_conv_b"], W["norm_final_g"], linkcol, ps)
                T.barrier()
        T.es = es
        T.finish(T.sp, [(yb, None)])
    return nc, HC, T


def core_streams(x_prompt, x_sample):
    xs, links = [], []
    for c in range(8):
        if c < 4:
            a = x_sample[c]
            b = x_prompt[c]
            links.append(1.0)
        else:
            j = c - 4
            a = np.concatenate([x_prompt[4 + 2 * j], x_prompt[5 + 2 * j]], 0)
            b = x_prompt[12 + j]
            links.append(0.0)
        xs.append(np.ascontiguousarray(np.concatenate([a, b], 0)))
    return xs, links


def kernel(**inputs):
    x_prompt = np.asarray(inputs["x_prompt"], dtype=np.float32)
    x_sample = np.asarray(inputs["x_sample"], dtype=np.float32)
    nc, HC, _ = build_program()
    xs, links = core_streams(x_prompt, x_sample)
    wts = {n: np.ascontiguousarray(np.asarray(inputs[n], dtype=np.float32)) for n in W_NAMES}
    in_maps = []
    for c in range(8):
        m = {"x": xs[c], "link": np.full((128, 1), links[c], np.float32)}
        m.update(wts)
        m.update(HC)
        in_maps.append(m)
    res = run_bass_kernel_spmd(nc, in_maps, core_ids=list(range(8)))
    y_prompt = np.empty((16, 2048, 1024), np.float32)
    y_sample = np.empty((4, 4096, 1024), np.float32)
    for c in range(8):
        yc = np.asarray(res.results[c]["y"], dtype=np.float32)
        if c < 4:
            y_sample[c] = yc[0:4096]
            y_prompt[c] = yc[4096:6144]
        else:
            j = c - 4
            y_prompt[4 + 2 * j] = yc[0:2048]
            y_prompt[5 + 2 * j] = yc[2048:4096]
            y_prompt[12 + j] = yc[4096:6144]
    return (y_prompt, y_sample)
```

```python
import numpy as np
from contextlib import ExitStack
import concourse.bass as bass
import concourse.mybir as mybir
from concourse.bass_utils import run_bass_kernel_spmd

F32 = mybir.dt.float32
BF16 = mybir.dt.bfloat16
AF = mybir.ActivationFunctionType
ALU = mybir.AluOpType
AX = mybir.AxisListType

EPOCH = 12000
NRING = 6


class Ev:
    __slots__ = ("lane", "sem", "val")

    def __init__(self, lane, sem, val):
        self.lane, self.sem, self.val = lane, sem, val


class Lane:
    def __init__(self, trk, name, eng, dma, seen):
        self.trk, self.name, self.eng, self.dma, self.seen = trk, name, eng, dma, seen
        self.sem = None
        self.count = 0
        self.nsem = 0
        self.ring = []
        self.ndma = 0
        self.pending = False
        if dma:
            self.ring = [trk.new_sem(f"{name}r{i}") for i in range(NRING)]
        else:
            self._newsem()

    def _newsem(self):
        self.sem = self.trk.new_sem(f"{self.name}e{self.nsem}")
        self.nsem += 1
        self.count = 0

    def wait(self, ev):
        k = id(ev.sem)
        if self.seen.get(k, 0) >= ev.val:
            return
        self.eng.wait_ge(ev.sem, ev.val)
        self.seen[k] = ev.val

    def mark(self, ins, inc):
        if self.dma:
            i = self.ndma
            self.ndma += 1
            k = i % NRING
            sem = self.ring[k]
            ins.then_inc(sem, 16)
            return Ev(self, sem, 16 * (i // NRING + 1))
        if inc:
            self.count += 1
            ins.then_inc(self.sem, 1)
            ev = Ev(self, self.sem, self.count)
            self.pending = False
            if self.count >= EPOCH:
                self._newsem()
            return ev
        self.pending = True
        return Ev(self, self.sem, self.count + 1)

    def pre_dma(self):
        i = self.ndma
        if i >= NRING:
            k = i % NRING
            self.wait(Ev(self, self.ring[k], 16 * (i // NRING)))

    def latest(self):
        if self.dma:
            out = []
            for j in range(max(0, self.ndma - NRING), self.ndma):
                out.append(Ev(self, self.ring[j % NRING], 16 * (j // NRING + 1)))
            return out
        assert not self.pending
        if self.count == 0:
            return []
        return [Ev(self, self.sem, self.count)]


class St:
    __slots__ = ("w", "r")

    def __init__(self):
        self.w = None
        self.r = []


class Buf:
    def __init__(self, t, name):
        self.t = t
        self.name = name
        self.ap = t.ap() if hasattr(t, "ap") and callable(getattr(t, "ap")) else t
        self.st = {}

    def __getitem__(self, idx):
        return self.ap[idx]

    def states(self, key):
        if key is None:
            if None not in self.st:
                self.st[None] = St()
            return list(self.st.values())
        out = []
        if key not in self.st:
            self.st[key] = St()
        out.append(self.st[key])
        if None in self.st:
            out.append(self.st[None])
        return out


class Trk:
    def __init__(self, nc, es):
        self.nc, self.es = nc, es
        self.sem_es = es
        self.nsems = 0
        s_act, s_pool, s_sp = {}, {}, {}
        self.pe = Lane(self, "pe", nc.tensor, False, {})
        self.act = Lane(self, "act", nc.scalar, False, s_act)
        self.dve = Lane(self, "dve", nc.vector, False, {})
        self.pool = Lane(self, "pool", nc.gpsimd, False, s_pool)
        self.sp = Lane(self, "sp", nc.sync, True, s_sp)
        self.actq = Lane(self, "actq", nc.scalar, True, s_act)
        self.poolq = Lane(self, "poolq", nc.gpsimd, True, s_pool)
        self.lanes = [self.pe, self.act, self.dve, self.pool, self.sp, self.actq, self.poolq]
        self.nops = 0

    def new_sem(self, name):
        self.nsems += 1
        return self.sem_es.enter_context(self.nc.semaphore(name))

    def sbuf(self, name, shape, dt):
        self.nbuf = getattr(self, "nbuf", 0) + 1
        name = f"{name}_{self.nbuf}"
        return Buf(self.es.enter_context(self.nc.sbuf_tensor(name, list(shape), dt)), name)

    def psum(self, name, shape, dt):
        return Buf(self.es.enter_context(self.nc.psum_tensor(name, list(shape), dt)), name)

    def op(self, lane, fn, reads=(), writes=(), inc=True):
        evs = []
        for (b, k) in reads:
            for st in b.states(k):
                if st.w is not None:
                    evs.append((st.w, True))
        for (b, k) in writes:
            for st in b.states(k):
                if st.w is not None:
                    evs.append((st.w, False))
                for r in st.r:
                    evs.append((r, False))
        if lane.dma:
            lane.pre_dma()
        for ev, raw in evs:
            if ev.lane is lane and (not lane.dma):
                if lane is self.pe and not raw:
                    continue
                assert ev.val <= lane.count or ev.sem is not lane.sem, "same-lane wait on a pending event"
            lane.wait(ev)
        ins = fn()
        ev = lane.mark(ins, inc)
        for (b, k) in reads:
            for st in (b.states(k)[:1] if k is not None else [b.st[None]]):
                if not lane.dma:
                    st.r = [r for r in st.r if r.lane is not lane]
                st.r.append(ev)
        for (b, k) in writes:
            if k is None:
                for st in b.st.values():
                    st.w = None
                    st.r = []
                b.st[None].w = ev
            else:
                st = b.states(k)[0]
                st.w = ev
                st.r = []
        self.nops += 1
        return ev

    def barrier(self):
        evs = []
        for l in self.lanes:
            evs += l.latest()
        for l in self.lanes:
            for ev in evs:
                if ev.lane is l and not l.dma:
                    continue
                l.wait(ev)

    def finish(self, lane, bufs):
        for (b, k) in bufs:
            for st in b.states(k):
                if st.w is not None:
                    lane.wait(st.w)


def phase1(T, nc, x_ap, tok0, Ts, hT, gB, ident, ps_t, xbufs, xnbufs, ssq, junk):
    nt = Ts // 128
    for i in range(nt):
        xb = xbufs[i % len(xbufs)]
        xn = xnbufs[i % len(xnbufs)]
        pt = ps_t[i % len(ps_t)]
        o = 4 * (i % 4)
        T.op(T.sp, lambda: nc.sync.dma_start(out=xb[:], in_=x_ap[tok0 + 128 * i: tok0 + 128 * (i + 1), :]),
             writes=[(xb, None)])
        T.op(T.act, lambda: nc.scalar.activation(out=junk[:], in_=xb[:], func=AF.Square, accum_out=ssq[:, o + 0:o + 1]),
             reads=[(xb, None)], writes=[(junk, None), (ssq, o + 0)])
        T.op(T.dve, lambda: nc.vector.tensor_scalar(out=ssq[:, o + 1:o + 2], in0=ssq[:, o + 0:o + 1], scalar1=1.0 / 1024, scalar2=1e-6,
                                                    op0=ALU.mult, op1=ALU.add),
             reads=[(ssq, o + 0)], writes=[(ssq, o + 1)])
        T.op(T.act, lambda: nc.scalar.activation(out=ssq[:, o + 2:o + 3], in_=ssq[:, o + 1:o + 2], func=AF.Sqrt),
             reads=[(ssq, o + 1)], writes=[(ssq, o + 2)])
        T.op(T.dve, lambda: nc.vector.reciprocal(out=ssq[:, o + 3:o + 4], in_=ssq[:, o + 2:o + 3]),
             reads=[(ssq, o + 2)], writes=[(ssq, o + 3)])
        T.op(T.dve, lambda: nc.vector.tensor_scalar(out=xn[:], in0=xb[:], scalar1=ssq[:, o + 3:o + 4], scalar2=None, op0=ALU.mult),
             reads=[(xb, None), (ssq, o + 3)], writes=[(xn, None)])
        for k in range(8):
            T.op(T.pe, lambda: nc.tensor.transpose(out=pt[:, k * 128:(k + 1) * 128], in_=xn[:, k * 128:(k + 1) * 128], identity=ident[:]),
                 reads=[(xn, None), (ident, None)], writes=[(pt, None)], inc=(k == 7))
        T.op(T.dve, lambda: nc.vector.tensor_tensor(out=hT[:, :, 128 * i:128 * (i + 1)],
                                                    in0=pt[:].rearrange("p (k t) -> p k t", k=8), in1=gB[:], op=ALU.mult),
             reads=[(pt, None), (gB, None)], writes=[(hT, ("t", i))])


DILS = (1, 4, 16)


def att_consts_host():
    p = np.arange(128)[:, None, None] % 64
    t = np.arange(4)[None, :, None]
    n = np.arange(128)[None, None, :]
    mask = (np.abs(64 * (t - 1) + p - n) <= 64).astype(np.float32).reshape(128, 512)
    half = 8
    inv = (np.float32(500000.0) ** (-np.arange(half, dtype=np.float32) / np.float32(half))).astype(np.float32)
    pos = np.arange(4096, dtype=np.float32)
    ang = (pos[:, None] * inv[None, :]).astype(np.float32)
    cos = np.cos(ang).astype(np.float32).T
    sin = np.sin(ang).astype(np.float32).T
    C = np.ones((64, 4096), np.float32)
    S = np.zeros((64, 4096), np.float32)
    C[0:8] = cos; C[8:16] = cos
    S[0:8] = -sin; S[8:16] = sin
    C = np.concatenate([C, C], 0); S = np.concatenate([S, S], 0)
    onesbd = np.zeros((128, 128), np.float32)
    onesbd[0:64, 0:64] = 1; onesbd[64:128, 64:128] = 1
    return dict(amask=mask, ropec=np.ascontiguousarray(C), ropes=np.ascontiguousarray(S), onesbd=onesbd)


class AttBufs:
    def __init__(self, T, Tmax):
        self.qTs = [T.sbuf(f"qTc{i}", [128, Tmax], BF16) for i in range(2)]
        self.kTs = [T.sbuf(f"kTc{i}", [128, Tmax], BF16) for i in range(2)]
        self.vbd = T.sbuf("vbd", [128, Tmax // 64, 128], BF16)
        self.vT = T.sbuf("vTfm", [128, Tmax], BF16)
        self.acc = T.sbuf("aacc", [128, 2, Tmax], F32)
        self.wst = [T.sbuf(f"awst{i}", [128, 8, 128], F32) for i in range(2)]
        self.w = {nm: T.sbuf(f"aw_{nm}", [128, 8, 128], BF16) for nm in ("q", "k", "v", "qs", "ks")}
        self.ct = [T.sbuf(f"act{i}", [128, 512], F32) for i in range(2)]
        self.sg = [T.sbuf(f"asg{i}", [128, 512], F32) for i in range(2)]
        self.t1 = [T.sbuf(f"at1{i}", [128, 512], F32) for i in range(2)]
        self.t2 = [T.sbuf(f"at2{i}", [128, 512], F32) for i in range(2)]
        self.pe = [T.sbuf(f"ape{i}", [128, 512], BF16) for i in range(4)]
        self.pm = [T.sbuf(f"apm{i}", [128, 512], BF16) for i in range(4)]
        self.m_norm = T.sbuf("am_n", [128, 512], BF16)
        self.m_lo = T.sbuf("am_lo", [128, 512], BF16)
        self.m_hi = T.sbuf("am_hi", [128, 512], BF16)
        self.mf = T.sbuf("am_f", [128, 512], F32)
        self.onesbd = T.sbuf("aonesbd", [128, 128], BF16)
        self.onesf = T.sbuf("aonesf", [128, 128], F32)
        self.rec = [T.sbuf(f"arec{i}", [128, 512], F32) for i in range(2)]
        self.ao = [T.sbuf(f"aao{i}", [128, 512], BF16) for i in range(2)]


def att_setup(T, nc, A, amask_d, onesbd_d, linkcol):
    T.op(T.sp, lambda: nc.sync.dma_start(out=A.mf[:], in_=amask_d[:, :]), writes=[(A.mf, None)])
    T.op(T.sp, lambda: nc.sync.dma_start(out=A.onesf[:], in_=onesbd_d[:, :]), writes=[(A.onesf, None)])
    T.op(T.dve, lambda: nc.vector.tensor_copy(out=A.onesbd[:], in_=A.onesf[:]), reads=[(A.onesf, None)], writes=[(A.onesbd, None)])
    T.op(T.dve, lambda: nc.vector.tensor_copy(out=A.m_norm[:], in_=A.mf[:]), reads=[(A.mf, None)], writes=[(A.m_norm, None)])
    T.op(T.dve, lambda: nc.vector.tensor_copy(out=A.m_lo[:], in_=A.mf[:]), reads=[(A.mf, None)], writes=[(A.m_lo, None)])
    T.op(T.dve, lambda: nc.vector.tensor_copy(out=A.m_hi[:], in_=A.mf[:]), reads=[(A.mf, None)], writes=[(A.m_hi, None)])
    T.op(T.dve, lambda: nc.vector.tensor_scalar(out=A.m_lo[:, 0:128], in0=A.mf[:, 0:128], scalar1=linkcol[:, 0:1], scalar2=None, op0=ALU.mult),
         reads=[(A.mf, None), (linkcol, None)], writes=[(A.m_lo, None)])
    T.op(T.dve, lambda: nc.vector.tensor_scalar(out=A.m_hi[:, 384:512], in0=A.mf[:, 384:512], scalar1=linkcol[:, 0:1], scalar2=None, op0=ALU.mult),
         reads=[(A.mf, None), (linkcol, None)], writes=[(A.m_hi, None)])
    T.op(T.pool, lambda: nc.gpsimd.memset(A.vbd[:], 0.0), writes=[(A.vbd, None)])


def load_w_bf16(T, nc, w_d, col0, ncols, stage, dst, cast_lane, cast_eng, kchunks=8):
    T.op(T.sp, lambda: nc.sync.dma_start(out=stage[:, 0:kchunks, 0:ncols],
                                         in_=w_d[0:kchunks * 128, col0:col0 + ncols].rearrange("(k p) c -> p k c", p=128)),
         writes=[(stage, None)])
    T.op(cast_lane, lambda: cast_eng.tensor_copy(out=dst[:, 0:kchunks, 0:ncols], in_=stage[:, 0:kchunks, 0:ncols]),
         reads=[(stage, None)], writes=[(dst, None)])


def attention_phase(T, nc, A, hT, Ts, w_in, ropec_d, ropes_d, ps, att_out_cb, ident):
    nblk = Ts // 512
    cnt = {"w": 0, "b": 0, "q": 0}
    units = [(hp, g) for hp in range(4) for g in range(3)]

    def proj_gen(n):
        hp, g = units[n]
        dil = DILS[g]
        Lc = Ts // dil
        col_q = (g * 8 + 2 * hp) * 64
        qT, kT = A.qTs[n % 2], A.kTs[n % 2]
        for nm, off in (("q", 0), ("k", 1536), ("v", 3072)):
            st = A.wst[cnt["w"] % 2]; cnt["w"] += 1
            load_w_bf16(T, nc, w_in, off + col_q, 128, st, A.w[nm], T.pool, nc.gpsimd)
            yield None
        for nm in ("q", "k"):
            src, dst = A.w[nm], A.w[nm + "s"]
            sv = src[:].rearrange("p k (h d) -> p k h d", h=2)
            dv = dst[:].rearrange("p k (h d) -> p k h d", h=2)
            T.op(T.pool, lambda: nc.gpsimd.tensor_copy(out=dst[:], in_=src[:]), reads=[(src, None)], writes=[(dst, None)])
            T.op(T.pool, lambda: nc.gpsimd.tensor_copy(out=dv[:, :, :, 0:8], in_=sv[:, :, :, 8:16]), reads=[(src, None)], writes=[(dst, None)])
            T.op(T.pool, lambda: nc.gpsimd.tensor_copy(out=dv[:, :, :, 8:16], in_=sv[:, :, :, 0:8]), reads=[(src, None)], writes=[(dst, None)])
            yield None
        for b in range(nblk):
            i2 = cnt["b"] % 2; cnt["b"] += 1
            ct, sg = A.ct[i2], A.sg[i2]
            T.op(T.sp, lambda: nc.sync.dma_start(out=ct[:], in_=ropec_d[:, 512 * b:512 * (b + 1)]), writes=[(ct, None)])
            T.op(T.sp, lambda: nc.sync.dma_start(out=sg[:], in_=ropes_d[:, 512 * b:512 * (b + 1)]), writes=[(sg, None)])
            hk = [(hT, ("t", 4 * b + j)) for j in range(4)]
            for xi, nm in enumerate(("q", "k")):
                p1, p2 = ps[6], ps[7]
                for k in range(8):
                    T.op(T.pe, lambda: nc.tensor.matmul(p1[:], lhsT=A.w[nm][:, k, :], rhs=hT[:, k, 512 * b:512 * (b + 1)], start=(k == 0), stop=(k == 7)),
                         reads=[(A.w[nm], None)] + hk, writes=[(p1, None)], inc=(k == 7))
                    if k % 2 == 1:
                        yield None
                for k in range(8):
                    T.op(T.pe, lambda: nc.tensor.matmul(p2[:], lhsT=A.w[nm + "s"][:, k, :], rhs=hT[:, k, 512 * b:512 * (b + 1)], start=(k == 0), stop=(k == 7)),
                         reads=[(A.w[nm + "s"], None)] + hk, writes=[(p2, None)], inc=(k == 7))
                    if k % 2 == 1:
                        yield None
                t1, t2 = A.t1[xi], A.t2[xi]
                yield T.op(T.dve, lambda: nc.vector.tensor_tensor(out=t1[:], in0=p1[:], in1=ct[:], op=ALU.mult),
                           reads=[(p1, None), (ct, None)], writes=[(t1, None)])
                yield T.op(T.dve, lambda: nc.vector.tensor_tensor(out=t2[:], in0=p2[:], in1=sg[:], op=ALU.mult),
                           reads=[(p2, None), (sg, None)], writes=[(t2, None)])
                dstb = qT if nm == "q" else kT
                cl = 512 // dil
                oap = bass.AP(tensor=dstb.t, offset=(512 * b) // dil, ap=[[dstb.ap.ap[0][0], 128], [1, cl], [Lc, dil]])
                yield T.op(T.pool, lambda: nc.gpsimd.tensor_tensor(out=oap, in0=t1[:].rearrange("p (c r) -> p c r", r=dil),
                                                                   in1=t2[:].rearrange("p (c r) -> p c r", r=dil), op=ALU.add),
                           reads=[(t1, None), (t2, None)], writes=[(dstb, ("b", b))])

    def v1_gen(n):
        for b in range(nblk):
            pvt = ps[6 + (b % 2)]
            hk = [(hT, ("t", 4 * b + j)) for j in range(4)]
            for k in range(8):
                T.op(T.pe, lambda: nc.tensor.matmul(pvt[:], lhsT=A.w["v"][:, k, :], rhs=hT[:, k, 512 * b:512 * (b + 1)], start=(k == 0), stop=(k == 7)),
                     reads=[(A.w["v"], None)] + hk, writes=[(pvt, None)], inc=(k == 7))
            yield T.op(T.act, lambda: nc.scalar.copy(out=A.vT[:, 512 * b:512 * (b + 1)], in_=pvt[:]), reads=[(pvt, None)], writes=[(A.vT, ("b", b))])

    def v_gen(n):
        hp, g = units[n]
        dil = DILS[g]
        Lc = Ts // dil
        ntile = Ts // 64
        pstep = A.vT.ap.ap[0][0]
        for gq in range(ntile // 4):
            pv = ps[4 + (gq % 2)]
            pvb = pv.ap.bitcast(BF16)
            for u in range(4):
                vt = gq * 4 + u
                r, j = divmod(vt, Lc // 64)
                tok = r + dil * 64 * j
                apA = bass.AP(tensor=A.vT.t, offset=tok, ap=[[pstep, 64], [dil, 64]])
                apB = bass.AP(tensor=A.vT.t, offset=64 * pstep + tok, ap=[[pstep, 64], [dil, 64]])
                T.op(T.pe, lambda: nc.tensor.transpose(out=pvb[0:64, u * 128:u * 128 + 64], in_=apA, identity=ident[0:64, 0:64], tile_position=(0, 0)),
                     reads=[(A.vT, None), (ident, None)], writes=[(pv, None)], inc=False)
                T.op(T.pe, lambda: nc.tensor.transpose(out=pvb[64:128, u * 128 + 64:u * 128 + 128], in_=apB, identity=ident[64:128, 64:128], tile_position=(64, 64)),
                     reads=[(A.vT, None), (ident, None)], writes=[(pv, None)], inc=(u == 3))
            pvv = pvb[:, 0:512].rearrange("p (u c) -> p u c", u=4)
            yield T.op(T.act, lambda: nc.scalar.copy(out=A.vbd[0:64, gq * 4:gq * 4 + 4, 0:64], in_=pvv[0:64, :, 0:64]),
                       reads=[(pv, None)], writes=[(A.vbd, ("v", gq, 0))])
            yield T.op(T.dve, lambda: nc.vector.tensor_copy(out=A.vbd[64:128, gq * 4:gq * 4 + 4, 64:128], in_=pvv[64:128, :, 64:128]),
                       reads=[(pv, None)], writes=[(A.vbd, ("v", gq, 1))])

    def qblock(n, r, qb, qi):
        hp, g = units[n]
        dil = DILS[g]
        Lc = Ts // dil
        Lg = 2048 // dil
        qT, kT = A.qTs[n % 2], A.kTs[n % 2]
        sq = (128 * qb) // Lg
        tiles = []
        mk = A.m_norm
        for t in range(4):
            j = 2 * qb - 1 + t
            if j < 0 or j >= Lc // 64:
                continue
            stg = (64 * j) // Lg
            if stg != sq:
                mk = A.m_lo if t == 0 else A.m_hi
            tiles.append((t, j))
        t_a, t_b = tiles[0][0], tiles[-1][0] + 1
        pss = ps[qi % 3]
        pso = ps[3 + (qi % 3)]
        qsl = slice(r * Lc + 128 * qb, r * Lc + 128 * qb + 128)
        for (t, j) in tiles:
            ksl = slice(r * Lc + 64 * j, r * Lc + 64 * j + 64)
            last = (t == tiles[-1][0])
            T.op(T.pe, lambda: nc.tensor.matmul(pss[0:64, t * 128:(t + 1) * 128], lhsT=kT[0:64, ksl], rhs=qT[0:64, qsl],
                                                start=True, stop=True, tile_position=(0, 0)),
                 reads=[(kT, None), (qT, None)], writes=[(pss, None)], inc=False)
            T.op(T.pe, lambda: nc.tensor.matmul(pss[64:128, t * 128:(t + 1) * 128], lhsT=kT[64:128, ksl], rhs=qT[64:128, qsl],
                                                start=True, stop=True, tile_position=(64, 64)),
                 reads=[(kT, None), (qT, None)], writes=[(pss, None)], inc=last)
        yield None
        pe_, pm_ = A.pe[qi % 4], A.pm[qi % 4]
        fs = slice(t_a * 128, t_b * 128)
        yield T.op(T.act, lambda: nc.scalar.activation(out=pe_[:, fs], in_=pss[:, fs], func=AF.Exp, scale=0.125),
                   reads=[(pss, None)], writes=[(pe_, None)])
        yield T.op(T.dve, lambda: nc.vector.tensor_tensor(out=pm_[:, fs], in0=pe_[:, fs], in1=mk[:, fs], op=ALU.mult),
                   reads=[(pe_, None), (mk, None)], writes=[(pm_, None)])
        for ti, (t, j) in enumerate(tiles):
            vt = r * (Lc // 64) + j
            T.op(T.pe, lambda: nc.tensor.matmul(pso[:, 0:128], lhsT=A.vbd[:, vt, :], rhs=pm_[:, t * 128:(t + 1) * 128],
                                                start=(ti == 0), stop=(ti == len(tiles) - 1)),
                 reads=[(A.vbd, ("v", vt // 4, 0)), (A.vbd, ("v", vt // 4, 1)), (pm_, None)], writes=[(pso, None)], inc=False)
        for ti, (t, j) in enumerate(tiles):
            T.op(T.pe, lambda: nc.tensor.matmul(pso[:, 128:256], lhsT=A.onesbd[:], rhs=pm_[:, t * 128:(t + 1) * 128],
                                                start=(ti == 0), stop=(ti == len(tiles) - 1)),
                 reads=[(A.onesbd, None), (pm_, None)], writes=[(pso, None)], inc=(ti == len(tiles) - 1))
        yield None
        tok0 = r + dil * 128 * qb
        aap = bass.AP(tensor=A.acc.t, offset=tok0, ap=[[A.acc.ap.ap[0][0], 128], [A.acc.ap.ap[1][0], 2], [dil, 128]])
        psv = pso[:, 0:256].rearrange("p (a n) -> p a n", a=2)
        akeys = [(A.acc, ("t", tt)) for tt in range(tok0 // 512, (tok0 + dil * 127) // 512 + 1)]
        if g == 0:
            yield T.op(T.dve, lambda: nc.vector.tensor_copy(out=aap, in_=psv), reads=[(pso, None)], writes=akeys)
        else:
            yield T.op(T.dve, lambda: nc.vector.tensor_tensor(out=aap, in0=psv, in1=aap, op=ALU.add),
                       reads=[(pso, None)] + akeys, writes=akeys)

    def finalize(hp):
        for b in range(nblk):
            rec, ao = A.rec[b % 2], A.ao[b % 2]
            T.op(T.dve, lambda: nc.vector.reciprocal(out=rec[:], in_=A.acc[:, 1, 512 * b:512 * (b + 1)]),
                 reads=[(A.acc, ("t", b))], writes=[(rec, None)])
            T.op(T.pool, lambda: nc.gpsimd.tensor_tensor(out=ao[:], in0=A.acc[:, 0, 512 * b:512 * (b + 1)], in1=rec[:], op=ALU.mult),
                 reads=[(A.acc, ("t", b)), (rec, None)], writes=[(ao, None)])
            att_out_cb(hp, b, ao)

    def drain(g_):
        for _ in g_:
            pass

    def side_gen(n):
        yield from proj_gen(n)
        yield from v1_gen(n)

    drain(side_gen(0))
    drain(v_gen(0))
    for n in range(len(units)):
        hp, g = units[n]
        dil = DILS[g]
        nq = (Ts // dil) // 128
        qlist = [(r, qb) for r in range(dil) for qb in range(nq)]
        side = side_gen(n + 1) if n + 1 < len(units) else None
        qi = 0
        gl = []
        while qi < len(qlist) or gl:
            while len(gl) < 3 and qi < len(qlist):
                gl.append(qblock(n, qlist[qi][0], qlist[qi][1], cnt["q"])); qi += 1; cnt["q"] += 1
            for g_ in list(gl):
                try:
                    next(g_)
                except StopIteration:
                    gl.remove(g_)
            if side is not None:
                try:
                    next(side)
                except StopIteration:
                    side = None
        if side is not None:
            drain(side)
        if g == 2:
            finalize(hp)
        if n + 1 < len(units):
            drain(v_gen(n + 1))


NEG = -30000.0
START_GAP = 1


def dn_consts_host():
    j = np.arange(128)[:, None]; i = np.arange(128)[None, :]
    f = np.float32
    return dict(
        triU=(j <= i).astype(f), triL=(j >= i).astype(f), ones128=np.ones((128, 128), f),
        negmU=np.where(i >= j, 0.0, NEG).astype(f), negmL=np.where(i <= j, 0.0, NEG).astype(f),
        strU=(i > j).astype(f), strL=(i < j).astype(f),
        negbd2=np.tile(-((j // 16) == (i // 16)).astype(f), (1, 2)), ident2=np.tile(np.eye(128, dtype=f), (1, 2)),
        ms16=np.tile((((j // 32) == (i // 32)) & ((j // 16) != (i // 16))).astype(f), (1, 2)),
        ms32=np.tile((((j // 64) == (i // 64)) & ((j // 32) != (i // 32))).astype(f), (1, 2)),
        ms64=np.tile(((j // 64) != (i // 64)).astype(f), (1, 2)))


class DnBufs:
    def __init__(self, T, Tmax):
        NC = Tmax // 128
        self.NC = NC
        sb = T.sbuf
        self.wst = [sb(f"dwst{i}", [128, 8, 128], F32) for i in range(1)]
        self.w = {nm: sb(f"dw_{nm}", [128, 8, 128], BF16) for nm in ("q", "k", "v", "z")}
        self.wsm = sb("dwsm", [128, 8, 32], BF16)
        for nm in ("beta", "gc", "egc", "negegc", "kdec", "egtot"):
            setattr(self, nm, sb("d" + nm, [128, NC, 16], F32))
        self.dtb = sb("ddtb", [128, 16], F32)
        self.negA = sb("dnegA", [128, 16], F32)
        self.cw = sb("dcw", [128, 3, 5], F32)
        self.raws = [sb(f"draw{i}", [128, 2052], F32) for i in range(2)]
        self.caccs = [sb(f"dcacc{i}", [128, 2048], F32) for i in range(2)]

        def view(base, off, shape, name):
            b = Buf(base.t, name)
            pstep = base.ap.ap[0][0]
            b.ap = bass.AP(tensor=base.t, offset=off, ap=[[pstep, 128], [shape[2], shape[1]], [1, shape[2]]])
            b.st = base.st
            return b
        self.small = view(self.raws[1], 0, [128, NC, 32], "dsmall")
        self.xs = view(self.caccs[1], 0, [128, NC, 16], "dxs")
        self.gval = view(self.caccs[1], 1024, [128, NC, 16], "dgval")
        self.c2stg3 = view(self.raws[0], 0, [128, 1, 256], "dc2stg")
        self.sq = [sb(f"dsq{i}", [128, 512], F32) for i in range(2)]
        self.rs = [sb(f"drs{i}", [128, 512], F32) for i in range(2)]
        self.epsc = sb("depsc", [128, 1], F32)
        self.qT = sb("dqT", [128, Tmax], BF16)
        self.kT = sb("dkT", [128, Tmax], BF16)
        self.vtm = sb("dvtm", [128, NC, 128], BF16)
        self.oacc = sb("doacc", [128, NC, 128], F32)
        self.cst = {nm: sb("dc_" + nm, [128, 128], F32) for nm in ("triU", "triL", "ones128", "negmU", "negmL", "strU", "strL", "identf")}
        self.identb = sb("dc_identb", [128, 128], BF16)
        R = 4
        self.R = R
        self.U = [[sb(f"dU{d}{r}", [128, 128], BF16) for r in range(R)] for d in range(2)]
        self.iT = [[sb(f"diT{d}{r}", [128, 128], BF16) for r in range(R)] for d in range(2)]
        self.kd = [[sb(f"dkd{d}{r}", [128, 128], BF16) for r in range(R)] for d in range(2)]
        NU = 4
        self.NU = NU
        self.diag = [sb(f"ddiag{u}", [128, 128], F32) for u in range(NU)]
        self.dd = [sb(f"ddd{u}", [128, 128], F32) for u in range(NU)]
        self.dT = self.dd
        self.tmp = [sb(f"dtmp{u}", [128, 128], F32) for u in range(NU)]
        self.AA = [[sb(f"dAA{u}{i}", [128, 256], BF16) for i in range(2)] for u in range(NU)]
        self.UU = [[sb(f"dUU{u}{i}", [128, 256], BF16) for i in range(2)] for u in range(NU)]
        self.NN = [sb(f"dNN{u}", [128, 256], BF16) for u in range(NU)]
        self.T1 = [sb(f"dT1{u}", [128, 128], BF16) for u in range(NU)]
        self.c2 = {nm: sb("dc2_" + nm, [128, 256], BF16) for nm in ("negbd2", "ident2", "ms16", "ms32", "ms64")}
        self.S = [sb(f"dS{d}", [128, 128], F32) for d in range(2)]
        self.Sb = [sb(f"dSb{d}", [128, 128], BF16) for d in range(2)]
        self.Rp = [sb(f"dRp{d}", [128, 128], BF16) for d in range(2)]
        self.vn = [sb(f"dvn{d}", [128, 128], BF16) for d in range(2)]
        self.ot = [sb(f"dot{d}", [128, 128], F32) for d in range(2)]
        self.gob = sb("dgob", [128, 128], F32)
        self.zs = [sb(f"dzs{i}", [128, 128], F32) for i in range(2)]
        self.nss = sb("dnss", [128, NC], F32)
        self.nrs = sb("dnrs", [128, NC], F32)
        self.on = [sb(f"don{i}", [128, 128], BF16) for i in range(2)]
        self.onT = [sb(f"donT{i}", [128, 512], BF16) for i in range(2)]


def dn_setup(T, nc, D, cd, ident_d, dtbf, dtbb, alf, alb, gout):
    for nm in ("triU", "triL", "ones128", "negmU", "negmL", "strU", "strL"):
        T.op(T.sp, lambda: nc.sync.dma_start(out=D.cst[nm][:], in_=cd[nm][:, :]), writes=[(D.cst[nm], None)])
    T.op(T.sp, lambda: nc.sync.dma_start(out=D.cst["identf"][:], in_=ident_d[:, :]), writes=[(D.cst["identf"], None)])
    for nm in ("negbd2", "ident2", "ms16", "ms32", "ms64"):
        T.op(T.sp, lambda: nc.sync.dma_start(out=D.c2stg3[:, 0, :], in_=cd[nm][:, :]), writes=[(D.c2stg3, None)])
        T.op(T.dve, lambda: nc.vector.tensor_copy(out=D.c2[nm][:], in_=D.c2stg3[:, 0, :]), reads=[(D.c2stg3, None)], writes=[(D.c2[nm], None)])
    T.op(T.dve, lambda: nc.vector.tensor_copy(out=D.identb[:], in_=D.cst["identf"][:]), reads=[(D.cst["identf"], None)], writes=[(D.identb, None)])
    T.op(T.sp, lambda: nc.sync.dma_start(out=D.dtb[:, 0:8], in_=dtbf.partition_broadcast(128)), writes=[(D.dtb, None)])
    T.op(T.sp, lambda: nc.sync.dma_start(out=D.dtb[:, 8:16], in_=dtbb.partition_broadcast(128)), writes=[(D.dtb, None)])
    T.op(T.sp, lambda: nc.sync.dma_start(out=D.negA[:, 0:8], in_=alf.partition_broadcast(128)), writes=[(D.negA, None)])
    T.op(T.sp, lambda: nc.sync.dma_start(out=D.negA[:, 8:16], in_=alb.partition_broadcast(128)), writes=[(D.negA, None)])
    T.op(T.sp, lambda: nc.sync.dma_start(out=D.gob[:], in_=gout.partition_broadcast(128)), writes=[(D.gob, None)])
    T.op(T.act, lambda: nc.scalar.activation(out=D.negA[:], in_=D.negA[:], func=AF.Exp), reads=[(D.negA, None)], writes=[(D.negA, None)])
    T.op(T.dve, lambda: nc.vector.tensor_scalar(out=D.negA[:], in0=D.negA[:], scalar1=-1.0, scalar2=None, op0=ALU.mult), reads=[(D.negA, None)], writes=[(D.negA, None)])
    T.op(T.dve, lambda: nc.vector.memset(D.epsc[:], 1e-6), writes=[(D.epsc, None)])


def dn_scalars(T, nc, D, hT, Ts, w_in, ps):
    NC = Ts // 128
    load_w_bf16(T, nc, w_in, 8704, 32, D.wst[0], D.wsm, T.pool, nc.gpsimd)
    for c0 in range(0, NC, 16):
        pb = ps[0]
        for c in range(c0, min(NC, c0 + 16)):
            for k in range(8):
                T.op(T.pe, lambda: nc.tensor.matmul(pb[:, (c - c0) * 32:(c - c0 + 1) * 32], lhsT=hT[:, k, c * 128:(c + 1) * 128], rhs=D.wsm[:, k, :],
                                                    start=(k == 0), stop=(k == 7)),
                     reads=[(hT, ("t", c)), (D.wsm, None)], writes=[(pb, None)], inc=(k == 7))
        n = min(NC, c0 + 16) - c0
        T.op(T.act, lambda: nc.scalar.copy(out=D.small[:, c0:c0 + n, :], in_=pb[:, 0:n * 32].rearrange("p (c s) -> p c s", s=32)),
             reads=[(pb, None)], writes=[(D.small, None)])
    sl = slice(0, NC)
    T.op(T.act, lambda: nc.scalar.activation(out=D.beta[:, sl, :], in_=D.small[:, sl, 0:16], func=AF.Sigmoid), reads=[(D.small, None)], writes=[(D.beta, None)])
    T.op(T.dve, lambda: nc.vector.tensor_tensor(out=D.xs[:, sl, :], in0=D.small[:, sl, 16:32], in1=D.dtb[:].unsqueeze(1).to_broadcast([128, NC, 16]), op=ALU.add),
         reads=[(D.small, None), (D.dtb, None)], writes=[(D.xs, None)])
    T.op(T.act, lambda: nc.scalar.activation(out=D.xs[:, sl, :], in_=D.xs[:, sl, :], func=AF.Exp), reads=[(D.xs, None)], writes=[(D.xs, None)])
    T.op(T.act, lambda: nc.scalar.activation(out=D.xs[:, sl, :], in_=D.xs[:, sl, :], func=AF.Ln, bias=1.0), reads=[(D.xs, None)], writes=[(D.xs, None)])
    T.op(T.dve, lambda: nc.vector.tensor_tensor(out=D.gval[:, sl, :], in0=D.xs[:, sl, :], in1=D.negA[:].unsqueeze(1).to_broadcast([128, NC, 16]), op=ALU.mult),
         reads=[(D.xs, None), (D.negA, None)], writes=[(D.gval, None)])
    pg, pt = ps[1], ps[2]
    for c in range(NC):
        T.op(T.pe, lambda: nc.tensor.matmul(pg[:, c * 16:c * 16 + 8], lhsT=D.cst["triU"][:], rhs=D.gval[:, c, 0:8], start=True, stop=True),
             reads=[(D.cst["triU"], None), (D.gval, None)], writes=[(pg, None)], inc=False)
        T.op(T.pe, lambda: nc.tensor.matmul(pg[:, c * 16 + 8:c * 16 + 16], lhsT=D.cst["triL"][:], rhs=D.gval[:, c, 8:16], start=True, stop=True),
             reads=[(D.cst["triL"], None), (D.gval, None)], writes=[(pg, None)], inc=False)
        T.op(T.pe, lambda: nc.tensor.matmul(pt[:, c * 16:c * 16 + 16], lhsT=D.cst["ones128"][:], rhs=D.gval[:, c, 0:16], start=True, stop=True),
             reads=[(D.cst["ones128"], None), (D.gval, None)], writes=[(pt, None)], inc=(c == NC - 1))
    pgv = pg[:, 0:NC * 16].rearrange("p (c s) -> p c s", s=16)
    ptv = pt[:, 0:NC * 16].rearrange("p (c s) -> p c s", s=16)
    T.op(T.act, lambda: nc.scalar.copy(out=D.gc[:, sl, :], in_=pgv), reads=[(pg, None)], writes=[(D.gc, None)])
    T.op(T.act, lambda: nc.scalar.activation(out=D.egc[:, sl, :], in_=pgv, func=AF.Exp), reads=[(pg, None)], writes=[(D.egc, None)])
    T.op(T.dve, lambda: nc.vector.tensor_scalar(out=D.negegc[:, sl, :], in0=D.egc[:, sl, :], scalar1=-1.0, scalar2=None, op0=ALU.mult),
         reads=[(D.egc, None)], writes=[(D.negegc, None)])
    T.op(T.dve, lambda: nc.vector.tensor_tensor(out=D.kdec[:, sl, :], in0=ptv, in1=D.gc[:, sl, :], op=ALU.subtract),
         reads=[(pt, None), (D.gc, None)], writes=[(D.kdec, None)])
    T.op(T.act, lambda: nc.scalar.activation(out=D.kdec[:, sl, :], in_=D.kdec[:, sl, :], func=AF.Exp), reads=[(D.kdec, None)], writes=[(D.kdec, None)])
    T.op(T.act, lambda: nc.scalar.activation(out=D.egtot[:, sl, :], in_=ptv, func=AF.Exp), reads=[(pt, None)], writes=[(D.egtot, None)])


def dn_head(T, nc, D, hT, Ts, hd, w_in, conv_w, linkcol, ps, out_cb, part):
    NC = Ts // 128
    nseg = Ts // 2048
    nblk = Ts // 512
    def load_weights(names):
        for xi, nm in enumerate(("q", "k", "v", "z")):
            if nm not in names:
                continue
            col = 4608 + xi * 1024 + hd * 128
            load_w_bf16(T, nc, w_in, col, 128, D.wst[0], D.w[nm], T.pool, nc.gpsimd)

    def load_cw():
        with nc.allow_non_contiguous_dma(reason="small"):
            for xi in range(3):
                c0 = xi * 1024 + hd * 128
                T.op(T.sp, lambda: nc.sync.dma_start(out=D.cw[:, xi, :], in_=conv_w[:, c0:c0 + 128].rearrange("k c -> c k")), writes=[(D.cw, None)])
    its = [(xi, nm, sg_) for xi, nm in enumerate(("q", "k", "v")) for sg_ in range(nseg)]

    def stageA(ii):
        xi, nm, s = its[ii]
        raw, cacc = D.raws[ii % 2], D.caccs[ii % 2]
        t0 = s * 2048
        lo = t0 - 2 if s > 0 else t0
        hi = t0 + 2050 if s < nseg - 1 else t0 + 2048
        yield T.op(T.pool, lambda: nc.gpsimd.memset(raw[:, 0:2], 0.0), writes=[(raw, "lo")])
        yield T.op(T.pool, lambda: nc.gpsimd.memset(raw[:, 2050:2052], 0.0), writes=[(raw, "hi")])
        pieces = []
        a_ = lo
        while a_ < hi:
            b_ = min(hi, (a_ // 512 + 1) * 512)
            pieces.append((a_, b_)); a_ = b_
        for pi, (a_, b_) in enumerate(pieces):
            pp = ps[pi % 2]
            n = b_ - a_
            tl = [(hT, ("t", tt)) for tt in range(a_ // 128, (b_ - 1) // 128 + 1)]
            for k in range(8):
                yield T.op(T.pe, lambda: nc.tensor.matmul(pp[:, 0:n], lhsT=D.w[nm][:, k, :], rhs=hT[:, k, a_:b_], start=(k == 0), stop=(k == 7)),
                     reads=[(D.w[nm], None)] + tl, writes=[(pp, None)], inc=(k == 7))
            yield None
            is_halo = (b_ <= t0) or (a_ >= t0 + 2048)
            key = ("lo" if b_ <= t0 else "hi") if is_halo else ("m", pi)
            if is_halo:
                yield T.op(T.dve, lambda: nc.vector.tensor_scalar(out=raw[:, a_ - t0 + 2:b_ - t0 + 2], in0=pp[:, 0:n], scalar1=linkcol[:, 0:1], scalar2=None, op0=ALU.mult),
                     reads=[(pp, None), (linkcol, None)], writes=[(raw, key)])
            else:
                yield T.op(T.act, lambda: nc.scalar.copy(out=raw[:, a_ - t0 + 2:b_ - t0 + 2], in_=pp[:, 0:n]), reads=[(pp, None)], writes=[(raw, key)])
        yield T.op(T.dve, lambda: nc.vector.tensor_scalar(out=cacc[:], in0=raw[:, 0:2048], scalar1=D.cw[:, xi, 0:1], scalar2=None, op0=ALU.mult),
             reads=[(raw, None), (D.cw, None)], writes=[(cacc, None)])
        for i in range(1, 5):
            yield T.op(T.dve, lambda: nc.vector.scalar_tensor_tensor(out=cacc[:], in0=raw[:, i:i + 2048], scalar=D.cw[:, xi, i:i + 1], in1=cacc[:],
                                                               op0=ALU.mult, op1=ALU.add),
                 reads=[(raw, None), (D.cw, None), (cacc, None)], writes=[(cacc, None)])
        yield T.op(T.act, lambda: nc.scalar.activation(out=cacc[:], in_=cacc[:], func=AF.Silu), reads=[(cacc, None)], writes=[(cacc, None)])

    def stageN(ii):
        xi, nm, s = its[ii]
        cacc = D.caccs[ii % 2]
        t0 = s * 2048
        if nm == "v":
            for cc in range(16):
                c = s * 16 + cc
                pv = ps[2 + cc % 2]
                yield T.op(T.pe, lambda: nc.tensor.transpose(out=pv[:, 0:128], in_=cacc[:, cc * 128:(cc + 1) * 128], identity=D.cst["identf"][:]),
                     reads=[(cacc, None), (D.cst["identf"], None)], writes=[(pv, None)])
                yield T.op(T.act, lambda: nc.scalar.copy(out=D.vtm[:, c, :], in_=pv[:, 0:128]), reads=[(pv, None)], writes=[(D.vtm, ("c", c))])
        else:
            dst = D.qT if nm == "q" else D.kT
            scl = (128.0 ** -0.5) if nm == "q" else 1.0
            for bb in range(4):
                sq, rs = D.sq[bb % 2], D.rs[bb % 2]
                pn = ps[2 + bb % 2]
                fs = slice(bb * 512, (bb + 1) * 512)
                yield T.op(T.act, lambda: nc.scalar.activation(out=sq[:], in_=cacc[:, fs], func=AF.Square), reads=[(cacc, None)], writes=[(sq, None)])
                yield T.op(T.pe, lambda: nc.tensor.matmul(pn[:], lhsT=D.cst["ones128"][:], rhs=sq[:], start=True, stop=True),
                     reads=[(D.cst["ones128"], None), (sq, None)], writes=[(pn, None)])
                yield T.op(T.act, lambda: nc.scalar.activation(out=rs[:], in_=pn[:], func=AF.Ln, bias=D.epsc[:, 0:1]), reads=[(pn, None), (D.epsc, None)], writes=[(rs, None)])
                yield T.op(T.act, lambda: nc.scalar.activation(out=rs[:], in_=rs[:], func=AF.Exp, scale=-0.5), reads=[(rs, None)], writes=[(rs, None)])
                yield T.op(T.dve, lambda: nc.vector.scalar_tensor_tensor(out=dst[:, t0 + bb * 512:t0 + (bb + 1) * 512], in0=cacc[:, fs], scalar=scl, in1=rs[:],
                                                                   op0=ALU.mult, op1=ALU.mult),
                     reads=[(cacc, None), (rs, None)], writes=[(dst, ("b", s * 4 + bb))])

    def rr2g(gs):
        gs = [g_ for g_ in gs if g_ is not None]
        while gs:
            for g_ in list(gs):
                try:
                    next(g_)
                    yield None
                except StopIteration:
                    gs.remove(g_)

    def pro_gen():
        load_weights(("q", "k", "v"))
        load_cw()
        yield None
        yield from rr2g([stageA(0)])
        for ii in range(len(its)):
            yield from rr2g([stageA(ii + 1) if ii + 1 < len(its) else None, stageN(ii)])

    if part == "pro":
        return pro_gen()
    if part == "out":
        return out_gen_factory(T, nc, D, hT, Ts, hd, ps, out_cb, load_weights)
    T.op(T.pool, lambda: nc.gpsimd.memset(D.oacc[:, 0:NC, :], 0.0), writes=[(D.oacc, None)])
    for d in range(2):
        T.op(T.pool, lambda: nc.gpsimd.memset(D.S[d][:], 0.0), writes=[(D.S[d], None)])
        T.op(T.pool, lambda: nc.gpsimd.memset(D.Sb[d][:], 0.0), writes=[(D.Sb[d], None)])

    def pre_unit(u, c, d):
        pA = ps[u]
        col = d * 8 + hd
        ch = slice(c * 128, (c + 1) * 128)
        pK = pA.ap.bitcast(BF16)[:, 0:128]
        yield T.op(T.pe, lambda: nc.tensor.transpose(out=pK, in_=D.kT[:, ch], identity=D.identb[:]),
             reads=[(D.kT, ("b", c // 4)), (D.identb, None)], writes=[(pA, None)])
        kd = D.kd[d][c % D.R]
        yield T.op(T.act, lambda: nc.scalar.activation(out=kd[:], in_=pK, func=AF.Copy, scale=D.kdec[:, c, col:col + 1]),
             reads=[(pA, None), (D.kdec, None)], writes=[(kd, None)])
        kq = [(D.kT, ("b", c // 4)), (D.qT, ("b", c // 4))]
        T.op(T.pe, lambda: nc.tensor.matmul(pA[:, 0:128], lhsT=D.kT[:, ch], rhs=D.kT[:, ch], start=True, stop=True), reads=kq, writes=[(pA, None)], inc=False)
        yield T.op(T.pe, lambda: nc.tensor.matmul(pA[:, 128:256], lhsT=D.kT[:, ch], rhs=D.qT[:, ch], start=True, stop=True), reads=kq, writes=[(pA, None)])
        yield T.op(T.act, lambda: nc.scalar.activation(out=D.diag[u][:], in_=D.cst["identf"][:], func=AF.Copy, scale=D.gc[:, c, col:col + 1]),
             reads=[(D.cst["identf"], None), (D.gc, None)], writes=[(D.diag[u], None)])
        yield T.op(T.pe, lambda: nc.tensor.matmul(pA[:, 256:384], lhsT=D.cst["ones128"][:], rhs=D.diag[u][:], start=True, stop=True),
             reads=[(D.cst["ones128"], None), (D.diag[u], None)], writes=[(pA, None)])
        nm_ = D.cst["negmU" if d == 0 else "negmL"]
        st_ = D.cst["strU" if d == 0 else "strL"]
        yield T.op(T.dve, lambda: nc.vector.scalar_tensor_tensor(out=D.dd[u][:], in0=pA[:, 256:384], scalar=D.gc[:, c, col:col + 1], in1=nm_[:], op0=ALU.subtract, op1=ALU.add),
             reads=[(pA, None), (D.gc, None), (nm_, None)], writes=[(D.dd[u], None)])
        yield T.op(T.act, lambda: nc.scalar.activation(out=D.dT[u][:], in_=D.dd[u][:], func=AF.Exp), reads=[(D.dd[u], None)], writes=[(D.dT[u], None)])
        yield T.op(T.dve, lambda: nc.vector.tensor_tensor(out=D.tmp[u][:], in0=pA[:, 0:128], in1=D.dT[u][:], op=ALU.mult),
             reads=[(pA, None), (D.dT[u], None)], writes=[(D.tmp[u], None)])
        NN, T1 = D.NN[u], D.T1[u]
        posbeta = D.beta[:, c, col:col + 1]
        yield T.op(T.dve, lambda: nc.vector.scalar_tensor_tensor(out=NN[:, 0:128], in0=D.tmp[u][:], scalar=posbeta, in1=st_[:], op0=ALU.mult, op1=ALU.mult),
             reads=[(D.tmp[u], None), (D.beta, None), (st_, None)], writes=[(NN, "A")])
        iT = D.iT[d][c % D.R]
        yield T.op(T.dve, lambda: nc.vector.tensor_tensor(out=iT[:], in0=pA[:, 128:256], in1=D.dT[u][:], op=ALU.mult),
             reads=[(pA, None), (D.dT[u], None)], writes=[(iT, None)])
        pNT = pA.ap.bitcast(BF16)[:, 0:128]
        yield T.op(T.pe, lambda: nc.tensor.transpose(out=pNT, in_=NN[:, 0:128], identity=D.identb[:]),
             reads=[(NN, "A"), (D.identb, None)], writes=[(pA, None)])
        yield T.op(T.act, lambda: nc.scalar.copy(out=NN[:, 128:256], in_=pNT), reads=[(pA, None)], writes=[(NN, "T")])
        AAp, UUp = D.AA[u][0], D.UU[u][0]
        yield T.op(T.pool, lambda: nc.gpsimd.tensor_tensor(out=AAp[:], in0=NN[:], in1=D.c2["negbd2"][:], op=ALU.mult),
             reads=[(NN, "A"), (NN, "T"), (D.c2["negbd2"], None)], writes=[(AAp, None)])
        yield T.op(T.pool, lambda: nc.gpsimd.tensor_tensor(out=UUp[:], in0=AAp[:], in1=D.c2["ident2"][:], op=ALU.add),
             reads=[(AAp, None), (D.c2["ident2"], None)], writes=[(UUp, None)])
        def cp(stage, out, in_, reads, writes):
            if stage != 7:
                return T.op(T.act, lambda: nc.scalar.copy(out=out, in_=in_), reads=reads, writes=writes)
            return T.op(T.dve, lambda: nc.vector.tensor_copy(out=out, in_=in_), reads=reads, writes=writes)
        idb = D.identb
        for lvl in range(1, 4):
            AAc, UUc = D.AA[u][lvl % 2], D.UU[u][lvl % 2]
            T.op(T.pe, lambda: nc.tensor.matmul(pA[:, 0:128], lhsT=AAp[:, 128:256], rhs=AAp[:, 0:128], start=True, stop=True), reads=[(AAp, None)], writes=[(pA, None)], inc=False)
            yield T.op(T.pe, lambda: nc.tensor.matmul(pA[:, 128:256], lhsT=AAp[:, 0:128], rhs=AAp[:, 128:256], start=True, stop=True), reads=[(AAp, None)], writes=[(pA, None)])
            yield cp(2 * lvl, AAc[:], pA[:, 0:256], [(pA, None)], [(AAc, None)])
            rd = [(AAc, None), (UUp, None), (idb, None)]
            T.op(T.pe, lambda: nc.tensor.matmul(pA[:, 256:384], lhsT=AAc[:, 128:256], rhs=UUp[:, 0:128], start=True, stop=False), reads=rd, writes=[(pA, None)], inc=False)
            yield T.op(T.pe, lambda: nc.tensor.matmul(pA[:, 256:384], lhsT=idb[:], rhs=UUp[:, 0:128], start=False, stop=True), reads=rd, writes=[(pA, None)])
            yield cp(2 * lvl + 1, UUc[:, 0:128], pA[:, 256:384], [(pA, None)], [(UUc, None)])
            AAp, UUp = AAc, UUc
        pUT = pA.ap.bitcast(BF16)[:, 0:128]
        yield T.op(T.pe, lambda: nc.tensor.transpose(out=pUT, in_=UUp[:, 0:128], identity=idb[:]),
             reads=[(UUp, None), (idb, None)], writes=[(pA, None)])
        yield T.op(T.act, lambda: nc.scalar.copy(out=UUp[:, 128:256], in_=pUT), reads=[(pA, None)], writes=[(UUp, None)])
        for li, msn in enumerate(("ms16", "ms32", "ms64")):
            last = (li == 2)
            UUc = D.UU[u][li % 2]
            if UUc is UUp:
                UUc = D.UU[u][(li + 1) % 2]
            yield T.op(T.pe, lambda: nc.tensor.matmul(pA[:, 0:128], lhsT=NN[:, 128:256], rhs=UUp[:, 0:128], start=True, stop=True),
                 reads=[(NN, "T"), (UUp, None)], writes=[(pA, None)])
            yield T.op(T.dve, lambda: nc.vector.tensor_tensor(out=T1[:], in0=pA[:, 0:128], in1=D.c2[msn][:, 0:128], op=ALU.mult),
                 reads=[(pA, None), (D.c2[msn], None)], writes=[(T1, None)])
            if not last:
                T.op(T.pe, lambda: nc.tensor.matmul(pA[:, 256:384], lhsT=UUp[:, 128:256], rhs=T1[:], start=True, stop=True),
                     reads=[(UUp, None), (T1, None)], writes=[(pA, None)], inc=False)
                yield T.op(T.pe, lambda: nc.tensor.matmul(pA[:, 384:512], lhsT=T1[:], rhs=UUp[:, 128:256], start=True, stop=True),
                     reads=[(UUp, None), (T1, None)], writes=[(pA, None)])
                yield T.op(T.dve, lambda: nc.vector.tensor_tensor(out=UUc[:], in0=UUp[:], in1=pA[:, 256:512], op=ALU.subtract),
                     reads=[(pA, None), (UUp, None)], writes=[(UUc, None)])
                UUp = UUc
            else:
                Uf = D.U[d][c % D.R]
                yield T.op(T.pe, lambda: nc.tensor.matmul(pA[:, 256:384], lhsT=UUp[:, 128:256], rhs=T1[:], start=True, stop=True),
                     reads=[(UUp, None), (T1, None)], writes=[(pA, None)])
                yield T.op(T.dve, lambda: nc.vector.tensor_tensor(out=Uf[:], in0=UUp[:, 0:128], in1=pA[:, 256:384], op=ALU.subtract),
                     reads=[(pA, None), (UUp, None)], writes=[(Uf, None)])

    def scan_step(c, d):
        col = d * 8 + hd
        ch = slice(c * 128, (c + 1) * 128)
        p1 = ps[4 + 2 * d]
        p2 = ps[5 + 2 * d]
        S, Sb = D.S[d], D.Sb[d]
        U, iT, kd = D.U[d][c % D.R], D.iT[d][c % D.R], D.kd[d][c % D.R]
        kq = [(D.kT, ("b", c // 4)), (D.qT, ("b", c // 4))]
        T.op(T.pe, lambda: nc.tensor.matmul(p1[:, 0:128], lhsT=D.kT[:, ch], rhs=Sb[:], start=True, stop=True), reads=kq + [(Sb, None)], writes=[(p1, None)], inc=False)
        yield T.op(T.pe, lambda: nc.tensor.matmul(p1[:, 128:256], lhsT=D.qT[:, ch], rhs=Sb[:], start=True, stop=True), reads=kq + [(Sb, None)], writes=[(p1, None)])
        yield T.op(T.dve, lambda: nc.vector.scalar_tensor_tensor(out=D.Rp[d][:], in0=p1[:, 0:128], scalar=D.negegc[:, c, col:col + 1], in1=D.vtm[:, c, :], op0=ALU.mult, op1=ALU.add),
             reads=[(p1, None), (D.negegc, None), (D.vtm, ("c", c))], writes=[(D.Rp[d], None)])
        yield T.op(T.pe, lambda: nc.tensor.matmul(p1[:, 256:384], lhsT=U[:], rhs=D.Rp[d][:], start=True, stop=True), reads=[(U, None), (D.Rp[d], None)], writes=[(p1, None)])
        yield T.op(T.act, lambda: nc.scalar.activation(out=D.vn[d][:], in_=p1[:, 256:384], func=AF.Copy, scale=D.beta[:, c, col:col + 1]),
             reads=[(p1, None), (D.beta, None)], writes=[(D.vn[d], None)])
        T.op(T.pe, lambda: nc.tensor.matmul(p2[:, 0:128], lhsT=iT[:], rhs=D.vn[d][:], start=True, stop=True), reads=[(iT, None), (D.vn[d], None)], writes=[(p2, None)], inc=False)
        yield T.op(T.pe, lambda: nc.tensor.matmul(p2[:, 128:256], lhsT=kd[:], rhs=D.vn[d][:], start=True, stop=True), reads=[(kd, None), (D.vn[d], None)], writes=[(p2, None)])
        yield T.op(T.dve, lambda: nc.vector.scalar_tensor_tensor(out=S[:], in0=S[:], scalar=D.egtot[:, c, col:col + 1], in1=p2[:, 128:256], op0=ALU.mult, op1=ALU.add),
             reads=[(S, None), (D.egtot, None), (p2, None)], writes=[(S, None)])
        seg_edge = (nseg == 2) and ((d == 0 and c == 15) or (d == 1 and c == 16))
        if seg_edge:
            yield T.op(T.dve, lambda: nc.vector.tensor_scalar(out=S[:], in0=S[:], scalar1=linkcol[:, 0:1], scalar2=None, op0=ALU.mult),
                 reads=[(S, None), (linkcol, None)], writes=[(S, None)])
        yield T.op(T.act, lambda: nc.scalar.copy(out=Sb[:], in_=S[:]), reads=[(S, None)], writes=[(Sb, None)])
        yield T.op(T.dve, lambda: nc.vector.scalar_tensor_tensor(out=D.ot[d][:], in0=p1[:, 128:256], scalar=D.egc[:, c, col:col + 1], in1=D.oacc[:, c, :], op0=ALU.mult, op1=ALU.add),
             reads=[(p1, None), (D.egc, None), (D.oacc, ("c", c))], writes=[(D.ot[d], None)])
        yield T.op(T.dve, lambda: nc.vector.tensor_tensor(out=D.oacc[:, c, :], in0=p2[:, 0:128], in1=D.ot[d][:], op=ALU.add),
             reads=[(p2, None), (D.ot[d], None)], writes=[(D.oacc, ("c", c))])


    orders = [list(range(NC)), list(range(NC - 1, -1, -1))]
    pre_done, scan_done = set(), set()

    def scan_chain(d):
        for c in orders[d]:
            while (c, d) not in pre_done:
                yield "blocked"
            yield from scan_step(c, d)
            scan_done.add((c, d))

    free_banks = [0, 1, 2, 3]

    def pre_wrap(u, c, d):
        yield from pre_unit(u, c, d)
        pre_done.add((c, d))
        free_banks.append(u)

    pq = []
    for i in range(NC):
        pq.append((orders[0][i], 0, i))
        pq.append((orders[1][i], 1, i))
    gens = [scan_chain(0), scan_chain(1)]
    pqi = 0
    idle = 0
    rnd = 0
    last_start = -100
    while gens:
        rnd += 1
        if pqi < len(pq) and free_banks and rnd - last_start >= START_GAP:
            c, d, i = pq[pqi]
            if not (i >= D.R and (orders[d][i - D.R], d) not in scan_done):
                u = free_banks.pop(0)
                gens.append(pre_wrap(u, c, d))
                pqi += 1
                last_start = rnd
        progressed = False
        for g_ in list(gens):
            try:
                r = next(g_)
                if r != "blocked":
                    progressed = True
            except StopIteration:
                gens.remove(g_)
                progressed = True
        idle = 0 if progressed else idle + 1
        assert idle < 4, "DN scheduler stuck"
    return None


def out_gen_factory(T, nc, D, hT, Ts, hd, ps, out_cb, load_weights):
    NC = Ts // 128

    def gen():
        load_weights(("z",))
        yield None
        for c in range(NC):
            yield T.op(T.act, lambda: nc.scalar.activation(out=D.tmp[0][:], in_=D.oacc[:, c, :], func=AF.Square, accum_out=D.nss[:, c:c + 1]),
                 reads=[(D.oacc, ("c", c))], writes=[(D.tmp[0], None), (D.nss, ("c", c))])
        yield T.op(T.dve, lambda: nc.vector.tensor_scalar(out=D.nrs[:, 0:NC], in0=D.nss[:, 0:NC], scalar1=1.0 / 128, scalar2=1e-6, op0=ALU.mult, op1=ALU.add),
             reads=[(D.nss, None)], writes=[(D.nrs, None)])
        yield T.op(T.act, lambda: nc.scalar.activation(out=D.nrs[:, 0:NC], in_=D.nrs[:, 0:NC], func=AF.Sqrt), reads=[(D.nrs, None)], writes=[(D.nrs, None)])
        yield T.op(T.dve, lambda: nc.vector.reciprocal(out=D.nrs[:, 0:NC], in_=D.nrs[:, 0:NC]), reads=[(D.nrs, None)], writes=[(D.nrs, None)])
        for c in range(NC):
            pz = ps[4 + c % 2]
            for k in range(8):
                T.op(T.pe, lambda: nc.tensor.matmul(pz[:, 0:128], lhsT=hT[:, k, c * 128:(c + 1) * 128], rhs=D.w["z"][:, k, :], start=(k == 0), stop=(k == 7)),
                     reads=[(hT, ("t", c)), (D.w["z"], None)], writes=[(pz, None)], inc=(k == 7))
            yield None
            zs = D.zs[c % 2]
            yield T.op(T.act, lambda: nc.scalar.activation(out=zs[:], in_=pz[:, 0:128], func=AF.Silu), reads=[(pz, None)], writes=[(zs, None)])
            yield T.op(T.pool, lambda: nc.gpsimd.tensor_tensor(out=zs[:], in0=zs[:], in1=D.gob[:], op=ALU.mult), reads=[(zs, None), (D.gob, None)], writes=[(zs, None)])
            on = D.on[c % 2]
            yield T.op(T.dve, lambda: nc.vector.scalar_tensor_tensor(out=on[:], in0=D.oacc[:, c, :], scalar=D.nrs[:, c:c + 1], in1=zs[:], op0=ALU.mult, op1=ALU.mult),
                 reads=[(D.oacc, ("c", c)), (D.nrs, None), (zs, None)], writes=[(on, None)])
            b4 = c // 4
            pT = ps[6 + b4 % 2]
            pTv = pT.ap.bitcast(BF16)
            onT = D.onT[b4 % 2]
            yield T.op(T.pe, lambda: nc.tensor.transpose(out=pTv[:, (c % 4) * 128:(c % 4 + 1) * 128], in_=on[:], identity=D.identb[:]),
                 reads=[(on, None), (D.identb, None)], writes=[(pT, None)])
            if c % 4 == 3:
                yield T.op(T.act, lambda: nc.scalar.copy(out=onT[:], in_=pTv[:, 0:512]), reads=[(pT, None)], writes=[(onT, None)])
                out_cb(hd, b4, onT)

    return gen()


class P3Bufs:
    def __init__(self, T):
        sb = T.sbuf
        self.stg = [sb(f"p3stg{i}", [128, 8, 128], F32) for i in range(3)]
        self.wg = sb("p3wg", [128, 8, 2048], BF16)
        self.wa = sb("p3wa", [128, 4, 1024], BF16)
        self.wb = sb("p3wb", [128, 8, 1024], BF16)
        self.wo = sb("p3wo", [128, 8, 1024], BF16)
        self.attb = sb("p3attb", [128, 4, 512], BF16)
        self.dnb = sb("p3dnb", [128, 8, 512], BF16)
        self.ga = [sb(f"p3ga{i}", [128, 512], F32) for i in range(2)]
        self.gb = [sb(f"p3gb{i}", [128, 512], F32) for i in range(2)]
        self.m1 = [sb(f"p3m1{i}", [128, 512], F32) for i in range(1)]
        self.m2 = [sb(f"p3m2{i}", [128, 512], F32) for i in range(1)]
        self.mix = sb("p3mix", [128, 8, 512], BF16)
        self.xt = [sb(f"p3xt{i}", [128, 1024], F32) for i in range(1)]
        self.x1 = [sb(f"p3x1{i}", [128, 1024], F32) for i in range(2)]
        self.xn = [sb(f"p3xn{i}", [128, 1024], BF16) for i in range(2)]
        self.ssq = sb("p3ssq", [128, 16], F32)
        self.gB = sb("p3gB", [128, 8, 128], F32)
        self.gcol = sb("p3gcol", [128, 8], F32)


def load_w_big(T, nc, w_d, nk, ncols, stg, dst, col0=0, dcol0=0):
    i = 0
    W = 128
    for k0 in range(0, nk, 8):
        kn = min(8, nk - k0)
        for c in range(0, ncols, W):
            n = min(W, ncols - c)
            st = stg[i % len(stg)]; i += 1
            T.op(T.sp, lambda: nc.sync.dma_start(out=st[:, 0:kn, 0:n], in_=w_d[k0 * 128:(k0 + kn) * 128, col0 + c:col0 + c + n].rearrange("(k p) c -> p k c", p=128)),
                 writes=[(st, None)])
            oo, ii_ = dst[:, k0:k0 + kn, dcol0 + c:dcol0 + c + n], st[:, 0:kn, 0:n]
            if i % 3 == 0:
                T.op(T.pool, lambda: nc.gpsimd.tensor_copy(out=oo, in_=ii_), reads=[(st, None)], writes=[(dst, ("c", i))])
            elif i % 3 == 1:
                T.op(T.act, lambda: nc.scalar.copy(out=oo, in_=ii_), reads=[(st, None)], writes=[(dst, ("c", i))])
            else:
                T.op(T.dve, lambda: nc.vector.tensor_copy(out=oo, in_=ii_), reads=[(st, None)], writes=[(dst, ("c", i))])


def rms_to_T(T, nc, src, B, gB, ident, pt, hT, tile_idx):
    i = tile_idx
    xn = B.xn[i % 2]
    o = 4 * (i % 4)
    T.op(T.act, lambda: nc.scalar.activation(out=xn[:], in_=src[:], func=AF.Square, accum_out=B.ssq[:, o + 0:o + 1]),
         reads=[(src, None)], writes=[(xn, None), (B.ssq, o + 0)])
    T.op(T.dve, lambda: nc.vector.tensor_scalar(out=B.ssq[:, o + 1:o + 2], in0=B.ssq[:, o + 0:o + 1], scalar1=1.0 / 1024, scalar2=1e-6, op0=ALU.mult, op1=ALU.add),
         reads=[(B.ssq, o + 0)], writes=[(B.ssq, o + 1)])
    T.op(T.act, lambda: nc.scalar.activation(out=B.ssq[:, o + 2:o + 3], in_=B.ssq[:, o + 1:o + 2], func=AF.Sqrt), reads=[(B.ssq, o + 1)], writes=[(B.ssq, o + 2)])
    T.op(T.dve, lambda: nc.vector.reciprocal(out=B.ssq[:, o + 3:o + 4], in_=B.ssq[:, o + 2:o + 3]), reads=[(B.ssq, o + 2)], writes=[(B.ssq, o + 3)])
    T.op(T.dve, lambda: nc.vector.tensor_scalar(out=xn[:], in0=src[:], scalar1=B.ssq[:, o + 3:o + 4], scalar2=None, op0=ALU.mult),
         reads=[(src, None), (B.ssq, o + 3)], writes=[(xn, None)])
    ptv = pt.ap.bitcast(BF16)
    for k in range(8):
        T.op(T.pe, lambda: nc.tensor.transpose(out=ptv[:, k * 128:(k + 1) * 128], in_=xn[:, k * 128:(k + 1) * 128], identity=ident[:]),
             reads=[(xn, None), (ident, None)], writes=[(pt, None)], inc=(k == 7))
    T.op(T.dve, lambda: nc.vector.tensor_tensor(out=hT[:, :, 128 * i:128 * (i + 1)], in0=ptv.rearrange("p (k t) -> p k t", k=8), in1=gB[:], op=ALU.mult),
         reads=[(pt, None), (gB, None)], writes=[(hT, ("t", i))])


def load_gB(T, nc, g_d, gcol, gB):
    with nc.allow_non_contiguous_dma(reason="small"):
        T.op(T.sp, lambda: nc.sync.dma_start(out=gcol[:], in_=g_d.rearrange("(k p) -> p k", p=128)), writes=[(gcol, None)])
    T.op(T.dve, lambda: nc.vector.tensor_copy(out=gB[:], in_=gcol[:].unsqueeze(2).to_broadcast([128, 8, 128])), reads=[(gcol, None)], writes=[(gB, None)])


def phase3(T, nc, B, hT, Ts, tok0, x_d, y_d, yb, w_in, w_a, w_b, w_o, g_ffn, att_sc, attb_buf, dn_sc, dnb_buf, ident, ps, side=None):
    load_w_big(T, nc, w_in, 8, 2048, B.stg, B.wg, col0=8736)
    load_w_big(T, nc, w_a, 4, 1024, B.stg, B.wa)
    load_w_big(T, nc, w_b, 8, 1024, B.stg, B.wb)
    load_w_big(T, nc, w_o, 8, 1024, B.stg, B.wo)
    load_gB(T, nc, g_ffn, B.gcol, B.gB)
    nblk = Ts // 512
    for b in range(nblk):
        bs = slice(512 * b, 512 * (b + 1))
        T.op(T.sp, lambda: nc.sync.dma_start(out=B.attb[:], in_=att_sc[:, :, bs].rearrange("k p t -> p k t")),
             reads=[(attb_buf, (k, b)) for k in range(4)], writes=[(B.attb, None)])
        T.op(T.sp, lambda: nc.sync.dma_start(out=B.dnb[:], in_=dn_sc[:, :, bs].rearrange("k p t -> p k t")),
             reads=[(dnb_buf, (k, b)) for k in range(8)], writes=[(B.dnb, None)])
        hk = [(hT, ("t", 4 * b + j)) for j in range(4)]
        for oc in range(8):
            pya, pyb, pga, pgb = ps[0 + 4 * (oc % 2)], ps[1 + 4 * (oc % 2)], ps[2 + 4 * (oc % 2)], ps[3 + 4 * (oc % 2)]
            os_ = slice(oc * 128, (oc + 1) * 128)
            for k in range(4):
                T.op(T.pe, lambda: nc.tensor.matmul(pya[:], lhsT=B.wa[:, k, os_], rhs=B.attb[:, k, :], start=(k == 0), stop=(k == 3)),
                     reads=[(B.wa, None), (B.attb, None)], writes=[(pya, None)], inc=(k == 3))
            for k in range(8):
                T.op(T.pe, lambda: nc.tensor.matmul(pyb[:], lhsT=B.wb[:, k, os_], rhs=B.dnb[:, k, :], start=(k == 0), stop=(k == 7)),
                     reads=[(B.wb, None), (B.dnb, None)], writes=[(pyb, None)], inc=(k == 7))
            for k in range(8):
                T.op(T.pe, lambda: nc.tensor.matmul(pga[:], lhsT=B.wg[:, k, os_], rhs=hT[:, k, bs], start=(k == 0), stop=(k == 7)),
                     reads=[(B.wg, None)] + hk, writes=[(pga, None)], inc=(k == 7))
            for k in range(8):
                T.op(T.pe, lambda: nc.tensor.matmul(pgb[:], lhsT=B.wg[:, k, 1024 + oc * 128:1024 + (oc + 1) * 128], rhs=hT[:, k, bs], start=(k == 0), stop=(k == 7)),
                     reads=[(B.wg, None)] + hk, writes=[(pgb, None)], inc=(k == 7))
            ga, gb, m1, m2 = B.ga[oc % 2], B.gb[oc % 2], B.m1[0], B.m2[0]
            T.op(T.act, lambda: nc.scalar.activation(out=ga[:], in_=pga[:], func=AF.Sigmoid), reads=[(pga, None)], writes=[(ga, None)])
            T.op(T.act, lambda: nc.scalar.activation(out=gb[:], in_=pgb[:], func=AF.Sigmoid), reads=[(pgb, None)], writes=[(gb, None)])
            T.op(T.dve, lambda: nc.vector.tensor_tensor(out=m1[:], in0=pya[:], in1=ga[:], op=ALU.mult), reads=[(pya, None), (ga, None)], writes=[(m1, None)])
            T.op(T.dve, lambda: nc.vector.tensor_tensor(out=m2[:], in0=pyb[:], in1=gb[:], op=ALU.mult), reads=[(pyb, None), (gb, None)], writes=[(m2, None)])
            T.op(T.dve, lambda: nc.vector.tensor_tensor(out=B.mix[:, oc, :], in0=m1[:], in1=m2[:], op=ALU.add),
                 reads=[(m1, None), (m2, None)], writes=[(B.mix, ("o", oc))])
            if side is not None:
                try:
                    next(side)
                except StopIteration:
                    side = None
        for tt in range(4):
            ti = 4 * b + tt
            xt, x1 = B.xt[0], B.x1[tt % 2]
            r0 = tok0 + 128 * ti
            T.op(T.sp, lambda: nc.sync.dma_start(out=xt[:], in_=x_d[r0:r0 + 128, :]), writes=[(xt, None)])
            for half in range(2):
                po = ps[half + 4 * (tt % 2)]
                for k in range(8):
                    T.op(T.pe, lambda: nc.tensor.matmul(po[:], lhsT=B.mix[:, k, tt * 128:(tt + 1) * 128], rhs=B.wo[:, k, half * 512:(half + 1) * 512], start=(k == 0), stop=(k == 7)),
                         reads=[(B.mix, None), (B.wo, None)], writes=[(po, None)], inc=(k == 7))
                T.op(T.dve, lambda: nc.vector.tensor_tensor(out=x1[:, half * 512:(half + 1) * 512], in0=po[:], in1=xt[:, half * 512:(half + 1) * 512], op=ALU.add),
                     reads=[(po, None), (xt, None)], writes=[(x1, None)])
            T.op(T.sp, lambda: nc.sync.dma_start(out=y_d[r0:r0 + 128, :], in_=x1[:]), reads=[(x1, None)], writes=[(yb, ("r", r0 // 128))])
            rms_to_T(T, nc, x1, B, B.gB, ident, ps[2 + 4 * (tt % 2)], hT, ti)


    if side is not None:
        for _ in side:
            pass


class P4Bufs:
    def __init__(self, T):
        sb = T.sbuf
        self.wd = sb("p4wd", [128, 22, 1024], BF16)
        self.stg = [sb(f"p4stg{i}", [128, 8, 128], F32) for i in range(3)]
        self.wgc = [sb(f"p4wg{i}", [128, 8, 128], BF16) for i in range(4)]
        self.wvc = [sb(f"p4wv{i}", [128, 8, 128], BF16) for i in range(4)]
        self.graw = [sb(f"p4graw{i}", [128, 514], F32) for i in range(2)]
        self.gprev = sb("p4gprev", [128, 22, 2], F32)
        self.cw = sb("p4cw", [128, 22, 4], F32)
        self.gcv = [sb(f"p4gcv{i}", [128, 512], F32) for i in range(2)]
        self.act = sb("p4act", [128, 22, 512], BF16)
        self.x1 = [sb(f"p4x1{i}", [128, 1024], F32) for i in range(2)]
        self.yo = [sb(f"p4yo{i}", [128, 1024], F32) for i in range(2)]
        self.ssq = sb("p4ssq", [128, 8], F32)
        self.gfin = sb("p4gfin", [128, 1024], F32)


def phase4(T, nc, B, hT, Ts, tok0, y_d, yb, wup_bf, wupb_buf, w_d, fcw, fcb, g_fin, linkcol, ps):
    load_w_big(T, nc, w_d, 22, 1024, B.stg, B.wd)
    with nc.allow_non_contiguous_dma(reason="small"):
        for kk in range(3):
            T.op(T.sp, lambda: nc.sync.dma_start(out=B.cw[:, :, kk:kk + 1], in_=fcw[kk:kk + 1, :].rearrange("o (c p) -> p c o", p=128)), writes=[(B.cw, None)])
        T.op(T.sp, lambda: nc.sync.dma_start(out=B.cw[:, :, 3:4], in_=fcb.rearrange("(c p o) -> p c o", p=128, o=1)), writes=[(B.cw, None)])
    T.op(T.sp, lambda: nc.sync.dma_start(out=B.gfin[:], in_=g_fin.partition_broadcast(128)), writes=[(B.gfin, None)])
    nblk = Ts // 512
    wi = 0
    for b in range(nblk):
        bs = slice(512 * b, 512 * (b + 1))
        hk = [(hT, ("t", 4 * b + j)) for j in range(4)]
        t_start, t_end = 512 * b, 512 * (b + 1)
        pf = 0 if t_start == 0 else ("L" if t_start % 2048 == 0 else 1)
        nf = 0 if t_end == Ts else ("L" if t_end % 2048 == 0 else 1)
        for c in range(22):
            wg, wv = B.wgc[wi % 4], B.wvc[wi % 4]; wi += 1
            T.op(T.sp, lambda: nc.sync.dma_start(out=wg[:], in_=wup_bf[:, c * 128:(c + 1) * 128].rearrange("(k p) c -> p k c", p=128)),
                 reads=[(wupb_buf, None)], writes=[(wg, None)])
            T.op(T.sp, lambda: nc.sync.dma_start(out=wv[:], in_=wup_bf[:, 2816 + c * 128:2816 + (c + 1) * 128].rearrange("(k p) c -> p k c", p=128)),
                 reads=[(wupb_buf, None)], writes=[(wv, None)])
            pg, pv, pn = ps[0 + 3 * (c % 2)], ps[1 + 3 * (c % 2)], ps[2 + 3 * (c % 2)]
            gr = B.graw[c % 2]
            nw = 512 if nf != 0 else 511
            hkw = hk + ([(hT, ("t", 4 * b + 4))] if nf != 0 else [])
            for k in range(8):
                T.op(T.pe, lambda: nc.tensor.matmul(pg[:, 0:nw], lhsT=wg[:, k, :], rhs=hT[:, k, t_start + 1:t_start + 1 + nw], start=(k == 0), stop=(k == 7)),
                     reads=[(wg, None)] + hkw, writes=[(pg, None)], inc=(k == 7))
            for k in range(8):
                T.op(T.pe, lambda: nc.tensor.matmul(pv[:], lhsT=wv[:, k, :], rhs=hT[:, k, bs], start=(k == 0), stop=(k == 7)),
                     reads=[(wv, None)] + hk, writes=[(pv, None)], inc=(k == 7))
            if pf == 0:
                for k in range(8):
                    T.op(T.pe, lambda: nc.tensor.matmul(pn[:, 0:1], lhsT=wg[:, k, :], rhs=hT[:, k, 0:1], start=(k == 0), stop=(k == 7)),
                         reads=[(wg, None), (hT, ("t", 0))], writes=[(pn, None)], inc=(k == 7))
                T.op(T.pool, lambda: nc.gpsimd.memset(gr[:, 0:1], 0.0), writes=[(gr, "p")])
                T.op(T.act, lambda: nc.scalar.copy(out=gr[:, 1:2], in_=pn[:, 0:1]), reads=[(pn, None)], writes=[(gr, "q")])
            else:
                if pf == 1:
                    T.op(T.pool, lambda: nc.gpsimd.tensor_copy(out=gr[:, 0:1], in_=B.gprev[:, c, 0:1]), reads=[(B.gprev, c)], writes=[(gr, "p")])
                else:
                    T.op(T.pool, lambda: nc.gpsimd.tensor_tensor(out=gr[:, 0:1], in0=B.gprev[:, c, 0:1], in1=linkcol[:, 0:1], op=ALU.mult),
                         reads=[(B.gprev, c), (linkcol, None)], writes=[(gr, "p")])
                T.op(T.pool, lambda: nc.gpsimd.tensor_copy(out=gr[:, 1:2], in_=B.gprev[:, c, 1:2]), reads=[(B.gprev, c)], writes=[(gr, "q")])
            T.op(T.act, lambda: nc.scalar.copy(out=gr[:, 2:2 + nw], in_=pg[:, 0:nw]), reads=[(pg, None)], writes=[(gr, "m")])
            if nf == 0:
                T.op(T.pool, lambda: nc.gpsimd.memset(gr[:, 513:514], 0.0), writes=[(gr, "n")])
            else:
                T.op(T.pool, lambda: nc.gpsimd.tensor_copy(out=B.gprev[:, c, :], in_=gr[:, 512:514]), reads=[(gr, "m")], writes=[(B.gprev, c)])
                if nf == "L":
                    T.op(T.pool, lambda: nc.gpsimd.tensor_tensor(out=gr[:, 513:514], in0=gr[:, 513:514], in1=linkcol[:, 0:1], op=ALU.mult),
                         reads=[(gr, "m"), (linkcol, None)], writes=[(gr, "m")])
            gcv = B.gcv[c % 2]
            T.op(T.dve, lambda: nc.vector.tensor_scalar(out=gcv[:], in0=gr[:, 0:512], scalar1=B.cw[:, c, 0:1], scalar2=B.cw[:, c, 3:4], op0=ALU.mult, op1=ALU.add),
                 reads=[(gr, None), (B.cw, None)], writes=[(gcv, None)])
            T.op(T.dve, lambda: nc.vector.scalar_tensor_tensor(out=gcv[:], in0=gr[:, 1:513], scalar=B.cw[:, c, 1:2], in1=gcv[:], op0=ALU.mult, op1=ALU.add),
                 reads=[(gr, None), (B.cw, None), (gcv, None)], writes=[(gcv, None)])
            T.op(T.dve, lambda: nc.vector.scalar_tensor_tensor(out=gcv[:], in0=gr[:, 2:514], scalar=B.cw[:, c, 2:3], in1=gcv[:], op0=ALU.mult, op1=ALU.add),
                 reads=[(gr, None), (B.cw, None), (gcv, None)], writes=[(gcv, None)])
            T.op(T.act, lambda: nc.scalar.activation(out=gcv[:], in_=gcv[:], func=AF.Gelu), reads=[(gcv, None)], writes=[(gcv, None)])
            T.op(T.dve, lambda: nc.vector.tensor_tensor(out=B.act[:, c, :], in0=pv[:], in1=gcv[:], op=ALU.mult),
                 reads=[(pv, None), (gcv, None)], writes=[(B.act, ("c", c))])
        for tt in range(4):
            ti = 4 * b + tt
            r0 = tok0 + 128 * ti
            x1, yo = B.x1[tt % 2], B.yo[tt % 2]
            o = 4 * (tt % 2)
            x2 = x1
            T.op(T.sp, lambda: nc.sync.dma_start(out=x1[:], in_=y_d[r0:r0 + 128, :]), reads=[(yb, ("r", r0 // 128))], writes=[(x1, None)])
            for half in range(2):
                po = ps[6 + half]
                for c in range(22):
                    T.op(T.pe, lambda: nc.tensor.matmul(po[:], lhsT=B.act[:, c, tt * 128:(tt + 1) * 128], rhs=B.wd[:, c, half * 512:(half + 1) * 512], start=(c == 0), stop=(c == 21)),
                         reads=[(B.act, None), (B.wd, None)], writes=[(po, None)], inc=(c == 21))
                T.op(T.dve, lambda: nc.vector.tensor_tensor(out=x2[:, half * 512:(half + 1) * 512], in0=po[:], in1=x1[:, half * 512:(half + 1) * 512], op=ALU.add),
                     reads=[(po, None), (x1, None)], writes=[(x1, None)])
            T.op(T.act, lambda: nc.scalar.activation(out=yo[:], in_=x2[:], func=AF.Square, accum_out=B.ssq[:, o + 0:o + 1]),
                 reads=[(x2, None)], writes=[(yo, None), (B.ssq, o + 0)])
            T.op(T.dve, lambda: nc.vector.tensor_scalar(out=B.ssq[:, o + 1:o + 2], in0=B.ssq[:, o + 0:o + 1], scalar1=1.0 / 1024, scalar2=1e-6, op0=ALU.mult, op1=ALU.add),
                 reads=[(B.ssq, o + 0)], writes=[(B.ssq, o + 1)])
            T.op(T.act, lambda: nc.scalar.activation(out=B.ssq[:, o + 2:o + 3], in_=B.ssq[:, o + 1:o + 2], func=AF.Sqrt), reads=[(B.ssq, o + 1)], writes=[(B.ssq, o + 2)])
            T.op(T.dve, lambda: nc.vector.reciprocal(out=B.ssq[:, o + 3:o + 4], in_=B.ssq[:, o + 2:o + 3]), reads=[(B.ssq, o + 2)], writes=[(B.ssq, o + 3)])
            T.op(T.dve, lambda: nc.vector.scalar_tensor_tensor(out=yo[:], in0=x2[:], scalar=B.ssq[:, o + 3:o + 4], in1=B.gfin[:], op0=ALU.mult, op1=ALU.mult),
                 reads=[(x2, None), (B.ssq, o + 3), (B.gfin, None)], writes=[(yo, None)])
            T.op(T.sp, lambda: nc.sync.dma_start(out=y_d[r0:r0 + 128, :], in_=yo[:]), reads=[(yo, None)], writes=[(yb, ("r", r0 // 128))])


def cast_wup_gen(T, nc, w_up, wup_bf, wupb_buf, stg, cst):
    i = 0
    for c in range(0, 5632, 128):
        st, cs = stg[i % len(stg)], cst[i % len(cst)]; i += 1
        T.op(T.sp, lambda: nc.sync.dma_start(out=st[:], in_=w_up[:, c:c + 128].rearrange("(k p) c -> p k c", p=128)), writes=[(st, None)])
        if i % 3 == 0:
            T.op(T.pool, lambda: nc.gpsimd.tensor_copy(out=cs[:], in_=st[:]), reads=[(st, None)], writes=[(cs, None)])
        elif i % 3 == 1:
            T.op(T.act, lambda: nc.scalar.copy(out=cs[:], in_=st[:]), reads=[(st, None)], writes=[(cs, None)])
        else:
            T.op(T.dve, lambda: nc.vector.tensor_copy(out=cs[:], in_=st[:]), reads=[(st, None)], writes=[(cs, None)])
        T.op(T.sp, lambda: nc.sync.dma_start(out=wup_bf[:, c:c + 128].rearrange("(k p) c -> p k c", p=128), in_=cs[:]), reads=[(cs, None)], writes=[(wupb_buf, ("c", c))])
        yield None


TOK = 6144
STREAMS = ((0, 4096), (4096, 2048))
W_NAMES = ("norm_mix_g", "w_in", "conv_qkv_w", "a_log_f", "a_log_b", "dt_bias_f", "dt_bias_b", "out_norm_g", "w_branch_a", "w_branch_b",
           "w_out", "norm_ffn_g", "w_up", "ffn_conv_w", "ffn_conv_b", "w_down", "norm_final_g")
W_SHAPES = {"norm_mix_g": [1024], "w_in": [1024, 10784], "conv_qkv_w": [5, 3072], "a_log_f": [8], "a_log_b": [8], "dt_bias_f": [8], "dt_bias_b": [8],
            "out_norm_g": [128], "w_branch_a": [512, 1024], "w_branch_b": [1024, 1024], "w_out": [1024, 1024], "norm_ffn_g": [1024],
            "w_up": [1024, 5632], "ffn_conv_w": [3, 2816], "ffn_conv_b": [2816], "w_down": [2816, 1024], "norm_final_g": [1024]}


def host_consts():
    c = {}
    c.update(att_consts_host())
    c.update(dn_consts_host())
    c["idn"] = np.eye(128, dtype=np.float32)
    return c


def build_program(streams=STREAMS, tok=TOK, dbg=False):
    nc = bass.Bass("TRN2", target_bir_lowering=False)

    def din(name, shape, dt=F32):
        return nc.dram_tensor(name, list(shape), dt, kind="ExternalInput").ap()
    x = din("x", [tok, 1024])
    linkd = din("link", [128, 1])
    W = {n: din(n, W_SHAPES[n]) for n in W_NAMES}
    HC = host_consts()
    C = {n: din(n, list(v.shape)) for n, v in HC.items()}
    y = nc.dram_tensor("y", [tok, 1024], F32, kind="ExternalOutput").ap()
    Tmax = max(t for _, t in streams)
    kw = dict(kind="ExternalOutput") if dbg else {}
    att_sc = nc.dram_tensor("att_sc", [4, 128, Tmax], BF16, **kw).ap()
    dn_sc = nc.dram_tensor("dn_sc", [8, 128, Tmax], BF16, **kw).ap()
    wup_bf = nc.dram_tensor("wup_bf", [1024, 5632], BF16).ap()

    def dbuf(ap, name):
        b = Buf(ap.tensor, name); b.ap = ap
        return b
    yb, attb_buf, dnb_buf, wupb_buf = dbuf(y, "y"), dbuf(att_sc, "att_sc"), dbuf(dn_sc, "dn_sc"), dbuf(wup_bf, "wup_bf")
    with ExitStack() as es:
        T = Trk(nc, es)
        hT = T.sbuf("hT", [128, 8, Tmax], BF16)
        identf = T.sbuf("identf", [128, 128], F32)
        ident = T.sbuf("ident", [128, 128], BF16)
        linkcol = T.sbuf("linkcol", [128, 1], F32)
        ps = [T.psum(f"ps{i}", [128, 512], F32) for i in range(8)]
        T.op(T.sp, lambda: nc.sync.dma_start(out=identf[:], in_=C["idn"][:, :]), writes=[(identf, None)])
        T.op(T.sp, lambda: nc.sync.dma_start(out=linkcol[:], in_=linkd[:, :]), writes=[(linkcol, None)])
        T.op(T.dve, lambda: nc.vector.tensor_copy(out=ident[:], in_=identf[:]), reads=[(identf, None)], writes=[(ident, None)])
        for (tok0, Ts) in streams:
            with ExitStack() as e1:
                T.es = e1
                gcol = T.sbuf("gcol", [128, 8], F32); gB = T.sbuf("gB", [128, 8, 128], F32)
                xbufs = [T.sbuf(f"xb{i}", [128, 1024], F32) for i in range(3)]
                xnbufs = [T.sbuf(f"xn{i}", [128, 1024], BF16) for i in range(2)]
                junk = T.sbuf("junk", [128, 1024], BF16); ssq = T.sbuf("ssq", [128, 16], F32)
                load_gB(T, nc, W["norm_mix_g"], gcol, gB)
                pst = []
                for i in (6, 7):
                    b = Buf(ps[i].t, f"pst{i}"); b.ap = ps[i].ap.bitcast(BF16); b.st = ps[i].st
                    pst.append(b)
                phase1(T, nc, x, tok0, Ts, hT, gB, ident, pst, xbufs, xnbufs, ssq, junk)
                T.barrier()
            with ExitStack() as e2:
                T.es = e2
                A = AttBufs(T, Ts)
                att_setup(T, nc, A, C["amask"], C["onesbd"], linkcol)

                def cb_a(hp, b, ao):
                    T.op(T.sp, lambda: nc.sync.dma_start(out=att_sc[hp, :, 512 * b:512 * (b + 1)], in_=ao[:]), reads=[(ao, None)], writes=[(attb_buf, (hp, b))])
                attention_phase(T, nc, A, hT, Ts, W["w_in"], C["ropec"], C["ropes"], ps, cb_a, ident)
                T.barrier()
            with ExitStack() as e3:
                T.es = e3
                D = DnBufs(T, Ts)
                dn_setup(T, nc, D, C, C["idn"], W["dt_bias_f"], W["dt_bias_b"], W["a_log_f"], W["a_log_b"], W["out_norm_g"])
                dn_scalars(T, nc, D, hT, Ts, W["w_in"], ps)

                def cb_d(hd, b, onT):
                    T.op(T.sp, lambda: nc.sync.dma_start(out=dn_sc[hd, :, 512 * b:512 * (b + 1)], in_=onT[:]), reads=[(onT, None)], writes=[(dnb_buf, (hd, b))])
                def dnh(hd, part):
                    return dn_head(T, nc, D, hT, Ts, hd, W["w_in"], W["conv_qkv_w"], linkcol, ps, cb_d, part)
                for _ in dnh(0, "pro"):
                    pass
                for hd in range(8):
                    dnh(hd, "scan")
                    gl = [dnh(hd, "out")] + ([dnh(hd + 1, "pro")] if hd < 7 else [])
                    while gl:
                        for g_ in list(gl):
                            try:
                                next(g_)
                            except StopIteration:
                                gl.remove(g_)
                T.barrier()
            with ExitStack() as e4:
                T.es = e4
                B3 = P3Bufs(T)
                side = None
                if tok0 == streams[0][0]:
                    stg = [T.sbuf("pstg0", [128, 8, 128], F32)]
                    cst = [T.sbuf("pcst0", [128, 8, 128], BF16)]
                    side = cast_wup_gen(T, nc, W["w_up"], wup_bf, wupb_buf, stg, cst)
                phase3(T, nc, B3, hT, Ts, tok0, x, y, yb, W["w_in"], W["w_branch_a"], W["w_branch_b"], W["w_out"], W["norm_ffn_g"],
                       att_sc, attb_buf, dn_sc, dnb_buf, ident, ps, side)
                T.barrier()
            with ExitStack() as e5:
                T.es = e5
                B4 = P4Bufs(T)
                phase4(T, nc, B4, hT, Ts, tok0, y, yb, wup_bf, wupb_buf, W["w_down"], W["ffn_conv_w"], W["ffn_conv_b"], W["norm_final_g"], linkcol, ps)
                T.barrier()
        T.es = es
        T.finish(T.sp, [(yb, None)])
    return nc, HC, T


def core_streams(x_prompt, x_sample):
    xs, links = [], []
    for c in range(8):
        if c < 4:
            a = x_sample[c]
            b = x_prompt[c]
            links.append(1.0)
        else:
            j = c - 4
            a = np.concatenate([x_prompt[4 + 2 * j], x_prompt[5 + 2 * j]], 0)
            b = x_prompt[12 + j]
            links.append(0.0)
        xs.append(np.ascontiguousarray(np.concatenate([a, b], 0)))
    return xs, links


def kernel(**inputs):
    x_prompt = np.asarray(inputs["x_prompt"], dtype=np.float32)
    x_sample = np.asarray(inputs["x_sample"], dtype=np.float32)
    nc, HC, _ = build_program()
    xs, links = core_streams(x_prompt, x_sample)
    wts = {n: np.ascontiguousarray(np.asarray(inputs[n], dtype=np.float32)) for n in W_NAMES}
    in_maps = []
    for c in range(8):
        m = {"x": xs[c], "link": np.full((128, 1), links[c], np.float32)}
        m.update(wts)
        m.update(HC)
        in_maps.append(m)
    res = run_bass_kernel_spmd(nc, in_maps, core_ids=list(range(8)))
    y_prompt = np.empty((16, 2048, 1024), np.float32)
    y_sample = np.empty((4, 4096, 1024), np.float32)
    for c in range(8):
        yc = np.asarray(res.results[c]["y"], dtype=np.float32)
        if c < 4:
            y_sample[c] = yc[0:4096]
            y_prompt[c] = yc[4096:6144]
        else:
            j = c - 4
            y_prompt[4 + 2 * j] = yc[0:2048]
            y_prompt[5 + 2 * j] = yc[2048:4096]
            y_prompt[12 + j] = yc[4096:6144]
    return (y_prompt, y_sample)
```
